# Optimizing a Trainium2 kernel written in Bass

```python
import jax, jax.numpy as jnp
from jax import lax
import numpy as np

D_MODEL = 1024
BATCH = 8
SEQ = 8192
DEPTH = 1

HEAD_DIM = 64
CONV_WIDTH_CH = 512
N_HEADS = 8
ATTN_WIDTH = N_HEADS * HEAD_DIM
MIX_WIDTH = CONV_WIDTH_CH + ATTN_WIDTH
CONV_K = 31
DILATED_PATTERNS = ((128, 1), (512, 4), (2048, 16))
Q_BLOCK = 128
D_FF = 2816
FFN_CONV_K = 3
EPS = 1e-6

kernel_name = "hymba_conformer_dilated_alibi_convffn"


def rms_norm(x, g):
    xf = x.astype(jnp.float32)
    y = xf * lax.rsqrt(jnp.mean(xf * xf, axis=-1, keepdims=True) + EPS)
    return (y * g.astype(jnp.float32)).astype(x.dtype)


def layer_norm(x, g, b):
    xf = x.astype(jnp.float32)
    mu = jnp.mean(xf, axis=-1, keepdims=True)
    var = jnp.mean(jnp.square(xf - mu), axis=-1, keepdims=True)
    y = (xf - mu) * lax.rsqrt(var + EPS)
    return (y * g.astype(jnp.float32) + b.astype(jnp.float32)).astype(x.dtype)


def causal_depthwise_conv(x, w, b):
    K, C = w.shape
    y = lax.conv_general_dilated(
        x, w[:, None, :].astype(x.dtype), window_strides=(1,), padding=[(K - 1, 0)],
        dimension_numbers=("NWC", "WIO", "NWC"), feature_group_count=C)
    return y + b.astype(x.dtype)


def alibi_slopes(n_heads):
    return 2.0 ** (-8.0 * jnp.arange(1, n_heads + 1, dtype=jnp.float32) / n_heads)


def dilated_window_attention(q, k, v, slopes, window, dilation):
    B, S, H, E = q.shape
    d = dilation
    n_back = window // d
    L = S // d
    nb = -(-L // Q_BLOCK)
    Lp = nb * Q_BLOCK

    def to_streams(a):
        a = a.reshape(B, L, d, H, E)
        a = jnp.pad(a, ((0, 0), (0, Lp - L), (0, 0), (0, 0), (0, 0)))
        return a.reshape(B, nb, Q_BLOCK, d, H, E)

    def with_prev(a):
        prev = jnp.concatenate([jnp.zeros_like(a[:, :1]), a[:, :-1]], axis=1)
        return jnp.concatenate([prev, a], axis=2)

    qs = to_streams(q)
    kk = with_prev(to_streams(k))
    vv = with_prev(to_streams(v))

    s = jnp.einsum("bnqrhe,bnkrhe->bnrhqk", qs, kk, preferred_element_type=jnp.float32)
    qi = jnp.arange(Q_BLOCK)[:, None]
    ki = jnp.arange(2 * Q_BLOCK)[None, :]
    delta = qi + Q_BLOCK - ki
    band = (delta >= 0) & (delta <= n_back)
    key_pos = jnp.arange(nb)[:, None] * Q_BLOCK - Q_BLOCK + jnp.arange(2 * Q_BLOCK)[None, :]
    key_ok = key_pos >= 0
    mask = band[None, :, :] & key_ok[:, None, :]
    dist = (delta * d).astype(jnp.float32)
    bias = -slopes[:, None, None] * dist[None]
    s = s + bias[None, None, None]
    s = jnp.where(mask[None, :, None, None], s, -jnp.inf)
    lse = jax.nn.logsumexp(s, axis=-1, keepdims=True)
    p = jnp.exp(s - lse)
    o = jnp.einsum("bnrhqk,bnkrhe->bnqrhe", p.astype(vv.dtype), vv,
                   preferred_element_type=jnp.float32)
    o = o.reshape(B, Lp, d, H, E)[:, :L].reshape(B, S, H, E)
    lse = jnp.transpose(lse[..., 0], (0, 1, 4, 2, 3))
    lse = lse.reshape(B, Lp, d, H)[:, :L].reshape(B, S, H)
    return o, lse


def setup_inputs(seed: int = 0) -> dict:
    key = jax.random.key(seed)
    ks = jax.random.split(key, 16)
    f32 = jnp.float32
    n_in = 2 * CONV_WIDTH_CH + 3 * ATTN_WIDTH
    nrm = lambda k, shape, fan: jax.random.normal(k, shape, f32) * (fan ** -0.5)
    gain = lambda k, n: 1.0 + 0.02 * jax.random.normal(k, (n,), f32)
    small = lambda k, n: 0.02 * jax.random.normal(k, (n,), f32)
    return {
        "x": jax.random.normal(ks[0], (BATCH, SEQ, D_MODEL), f32),
        "norm1_g": gain(ks[1], D_MODEL),
        "w_in": nrm(ks[2], (D_MODEL, n_in), D_MODEL),
        "conv_w": nrm(ks[3], (CONV_K, CONV_WIDTH_CH), CONV_K),
        "conv_b": small(ks[4], CONV_WIDTH_CH),
        "cn_g": gain(ks[5], CONV_WIDTH_CH),
        "cn_b": small(ks[6], CONV_WIDTH_CH),
        "q_norm_g": gain(ks[7], HEAD_DIM),
        "k_norm_g": gain(ks[8], HEAD_DIM),
        "w_out": nrm(ks[9], (MIX_WIDTH, D_MODEL), MIX_WIDTH),
        "norm2_g": gain(ks[10], D_MODEL),
        "w_up": nrm(ks[11], (D_MODEL, 2 * D_FF), D_MODEL),
        "ffconv_w": nrm(ks[12], (FFN_CONV_K, 2 * D_FF), FFN_CONV_K),
        "ffconv_b": small(ks[13], 2 * D_FF),
        "w_down": nrm(ks[14], (D_FF, D_MODEL), D_FF),
    }


def reference(x, norm1_g, w_in, conv_w, conv_b, cn_g, cn_b, q_norm_g, k_norm_g, w_out,
              norm2_g, w_up, ffconv_w, ffconv_b, w_down):
    B, S, _ = x.shape
    slopes = alibi_slopes(N_HEADS)
    for _layer in range(DEPTH):
        h = rms_norm(x, norm1_g)
        proj = h @ w_in
        c = CONV_WIDTH_CH
        a_val, a_gate, q, k, v = jnp.split(
            proj, [c, 2 * c, 2 * c + ATTN_WIDTH, 2 * c + 2 * ATTN_WIDTH], axis=-1)

        u = a_val * jax.nn.sigmoid(a_gate)
        u = causal_depthwise_conv(u, conv_w, conv_b)
        u = jax.nn.silu(layer_norm(u, cn_g, cn_b))

        q = rms_norm(q.reshape(B, S, N_HEADS, HEAD_DIM), q_norm_g) * (HEAD_DIM ** -0.5)
        k = rms_norm(k.reshape(B, S, N_HEADS, HEAD_DIM), k_norm_g)
        v = v.reshape(B, S, N_HEADS, HEAD_DIM)
        outs, lses = [], []
        for window, dilation in DILATED_PATTERNS:
            o_i, lse_i = dilated_window_attention(q, k, v, slopes, window, dilation)
            outs.append(o_i)
            lses.append(lse_i)
        wts = jax.nn.softmax(jnp.stack(lses, axis=0), axis=0)
        o = jnp.sum(wts[..., None] * jnp.stack(outs, axis=0), axis=0)
        o = o.astype(x.dtype).reshape(B, S, ATTN_WIDTH)

        x = x + jnp.concatenate([u, o], axis=-1) @ w_out

        h2 = rms_norm(x, norm2_g)
        up = causal_depthwise_conv(h2 @ w_up, ffconv_w, ffconv_b)
        gate, val = jnp.split(up, 2, axis=-1)
        x = x + (jax.nn.silu(gate) * val) @ w_down
    return x
```

```python
import contextlib
import numpy as np
import concourse.bass as bass
import concourse.mybir as mybir
from concourse.bass_utils import run_bass_kernel_spmd

F32 = mybir.dt.float32
BF16 = mybir.dt.bfloat16
ALU = mybir.AluOpType
AF = mybir.ActivationFunctionType

COMPUTE = ("pe", "act", "dve", "pool")
EPS = 1e-6
NH = 8
PATTERNS = ((128, 1), (512, 4), (2048, 16))


class Sched:
    def __init__(self, nc, stack):
        self.nc = nc
        self.stack = stack
        self.prog = {e: [] for e in COMPUTE + ("sp",)}
        self.cnt = {e: 0 for e in COMPUTE}
        self.sem = {e: stack.enter_context(nc.semaphore("s_" + e)) for e in COMPUTE}
        self.dma_sem = {}
        self.dma_cnt = {}
        self.last_w = {}
        self.readers = {}
        self.waited = {e: {} for e in self.prog}
        self.stores = []

    def _sem_of(self, k):
        return self.sem[k] if isinstance(k, str) else self.dma_sem[k[1]]

    def _deps(self, engine, reads, writes):
        toks = []
        for b in reads:
            if b in self.last_w:
                toks.append(self.last_w[b])
        for b in writes:
            if b in self.last_w:
                toks.append(self.last_w[b])
            toks.extend(self.readers.get(b, ()))
        best = {}
        for (k, v) in toks:
            if k == "pe" and engine == "pe":
                continue
            if best.get(k, 0) < v:
                best[k] = v
        out = []
        w = self.waited[engine]
        for k, v in best.items():
            if w.get(k, 0) >= v:
                continue
            w[k] = v
            out.append((k, v))
        return out

    def _record(self, tok, reads, writes):
        for b in writes:
            self.last_w[b] = tok
            self.readers[b] = []
        for b in reads:
            self.readers.setdefault(b, []).append(tok)

    def op(self, engine, fn, reads=(), writes=()):
        waits = self._deps(engine, reads, writes)
        self.cnt[engine] += 1
        tok = (engine, self.cnt[engine])
        self.prog[engine].append((waits, fn, engine, 1))
        self._record(tok, reads, writes)
        return tok

    def dma(self, key, fn, reads=(), writes=(), n=1, store=False):
        if key not in self.dma_sem:
            self.dma_sem[key] = self.stack.enter_context(
                self.nc.semaphore("d_%d" % len(self.dma_sem)))
            self.dma_cnt[key] = 0
        waits = self._deps("sp", reads, writes)
        self.dma_cnt[key] += n
        tok = (("dma", key), 16 * self.dma_cnt[key])
        self.prog["sp"].append((waits, fn, ("dma", key), n))
        self._record(tok, reads, writes)
        if store:
            self.stores.append(tok)
        return tok

    def dram_barrier(self):
        best = {}
        for (k, v) in self.stores:
            best[k] = max(best.get(k, 0), v)
        w = self.waited["sp"]
        waits = []
        for k, v in best.items():
            if w.get(k, 0) < v:
                w[k] = v
                waits.append((k, v))
        self.prog["sp"].append((waits, None, None, 0))
        self.stores = []

    def emit(self):
        nc = self.nc
        prog = self.prog
        self.prog = {e: [] for e in prog}
        with nc.Block() as block:
            def run(name):
                def body(e):
                    for (waits, fn, semkey, n) in prog[name]:
                        for (k, v) in waits:
                            e.wait_ge(self._sem_of(k), v)
                        if fn is None:
                            continue
                        r = fn(e)
                        if isinstance(semkey, str):
                            r.then_inc(self.sem[semkey], 1)
                        else:
                            if not isinstance(r, (list, tuple)):
                                r = [r]
                            assert len(r) == n, (len(r), n)
                            for ins in r:
                                ins.then_inc(self.dma_sem[semkey[1]], 16)
                return body
            block.tensor(run("pe"))
            block.scalar(run("act"))
            block.vector(run("dve"))
            block.gpsimd(run("pool"))
            block.sync(run("sp"))


class Ctx:
    pass


_UNIQ = [0]


def _mk(nc, st):
    _UNIQ[0] += 1
    pre = "k%d_" % _UNIQ[0]

    def sb(name, shape, dt):
        return st.enter_context(nc.sbuf_tensor(pre + name, shape, dt))

    def ps(name, shape, dt):
        return st.enter_context(nc.psum_tensor(pre + name, shape, dt))
    return sb, ps


def load_vec(S, nc, sb, name, src_ap, ncol):
    t = sb(name, [128, ncol], F32)
    S.dma(name, lambda e: [e.dma_start(out=t[:], in_=src_ap.rearrange("(c p) -> p c", p=128),
                                       allow_slow_non_contiguous=True)], writes=[name])
    return t


def pass1(nc, S, T, dr, C):
    NT = T // 512
    with contextlib.ExitStack() as st:
        sb, ps = _mk(nc, st)
        winb = sb("winb", [128, 8, 2560], BF16)
        g1t = load_vec(S, nc, sb, "g1t", dr["norm1_g"], 8)
        wst = [sb("wst%d" % i, [128, 2560], F32) for i in range(2)]
        for c in range(8):
            sid = "wst%d" % (c % 2)
            stg = wst[c % 2]
            S.dma(sid, lambda e, stg=stg, c=c: [e.dma_start(out=stg[:], in_=dr["w_in"][c * 128:(c + 1) * 128, :])],
                  writes=[sid])
            if c % 2 == 0:
                S.op("dve", lambda e, stg=stg, c=c: e.tensor_scalar(out=winb[:, c, :], in0=stg[:], scalar1=g1t[:, c:c + 1],
                                                                    scalar2=None, op0=ALU.mult),
                     reads=[sid, "g1t"], writes=["winb%d" % c])
            else:
                S.op("act", lambda e, stg=stg, c=c: e.activation(out=winb[:, c, :], in_=stg[:], func=AF.Identity,
                                                                 scale=g1t[:, c:c + 1]),
                     reads=[sid, "g1t"], writes=["winb%d" % c])
        gq = sb("gq", [128, 2], F32)
        S.dma("gq", lambda e: [
            e.dma_start(out=gq[0:64, 0:1], in_=dr["q_norm_g"].rearrange("(p o) -> p o", o=1)),
            e.dma_start(out=gq[64:128, 0:1], in_=dr["q_norm_g"].rearrange("(p o) -> p o", o=1)),
            e.dma_start(out=gq[0:64, 1:2], in_=dr["k_norm_g"].rearrange("(p o) -> p o", o=1)),
            e.dma_start(out=gq[64:128, 1:2], in_=dr["k_norm_g"].rearrange("(p o) -> p o", o=1)),
        ], writes=["gq0"], n=4)
        gqs = sb("gqs", [128, 2], F32)
        S.op("dve", lambda e: e.tensor_scalar(out=gqs[:, 0:1], in0=gq[:, 0:1], scalar1=0.125, scalar2=None, op0=ALU.mult),
             reads=["gq0"], writes=["gqs_a"])
        S.op("dve", lambda e: e.tensor_copy(out=gqs[:, 1:2], in_=gq[:, 1:2]), reads=["gq0"], writes=["gqs_b"])
        zt = sb("zt", [128, 4, 32], BF16)
        S.op("pool", lambda e: e.memset(zt[:], 0.0), writes=["zt"])
        S.dma("zt", lambda e: [e.dma_start(out=dr["UT"][:, 0:32].rearrange("(c p) t -> p c t", p=128), in_=zt[:])],
              reads=["zt"], store=True)

        xr = [sb("x%d" % i, [128, 4, 1024], F32) for i in range(3)]
        junk = sb("junk", [128, 1024], BF16)
        ssr = [sb("ss%d" % i, [128, 4], F32) for i in range(2)]
        lnr = [sb("ln%d" % i, [128, 4], F32) for i in range(2)]
        rsr = [sb("rs%d" % i, [128, 4], F32) for i in range(2)]
        xsr = [sb("xs%d" % i, [128, 1024], BF16) for i in range(2)]
        xTr = [sb("xT%d" % i, [128, 8, 512], BF16) for i in range(2)]
        uTr = [sb("uT%d" % i, [128, 4, 512], BF16) for i in range(2)]
        qkr = [sb("qk%d" % i, [128, 8, 512], BF16) for i in range(2)]
        var = [sb("va%d" % i, [128, 4, 8, 128], BF16) for i in range(2)]
        for i in range(2):
            S.op("pool", lambda e, i=i: e.memset(var[i][:, :, :, 64:128], 1.0), writes=["va_ones%d" % i])
        er = [sb("e%d" % i, [128, 512], F32) for i in range(2)]
        dr_ = [sb("d%d" % i, [128, 512], F32) for i in range(2)]
        sqr = [sb("sq%d" % i, [128, 512], BF16) for i in range(2)]
        lqr = [sb("lq%d" % i, [128, 512], F32) for i in range(2)]
        rqr = [sb("rq%d" % i, [128, 512], F32) for i in range(2)]
        PT = [ps("pT%d" % i, [128, 1024], BF16) for i in range(2)]
        PM = [ps("pM%d" % i, [128, 512], F32) for i in range(4)]
        PS_ = [ps("pS%d" % i, [128, 512], F32) for i in range(2)]
        cnt = {"T": 0, "M": 0, "S": 0, "e": 0, "q": 0}

        def nxt(k, n):
            v = cnt[k] % n
            cnt[k] += 1
            return v

        WINB = ["winb%d" % c for c in range(8)]

        def load_x(i):
            xt = xr[i % 3]
            S.dma("x%d" % (i % 3), lambda e: [e.dma_start(
                out=xt[:], in_=dr["x"][i * 512:(i + 1) * 512, :].rearrange("(s p) d -> p s d", p=128))],
                writes=["x%d" % (i % 3)])

        def front(i):
            b = i % 2
            xt, ss, ln, rs, xT = xr[i % 3], ssr[b], lnr[b], rsr[b], xTr[b]
            xid = "x%d" % (i % 3)
            for s in range(4):
                S.op("act", lambda e, s=s: e.activation(out=junk[:], in_=xt[:, s, :], func=AF.Square,
                                                        accum_out=ss[:, s:s + 1]),
                     reads=[xid], writes=["junk", "ss%d_%d" % (b, s)])
            S.op("act", lambda e: e.activation(out=ln[:], in_=ss[:], func=AF.Ln, scale=1.0 / 1024, bias=C.eps[:, 0:1]),
                 reads=["ss%d_%d" % (b, s) for s in range(4)], writes=["ln%d" % b])
            S.op("act", lambda e: e.activation(out=rs[:], in_=ln[:], func=AF.Exp, scale=-0.5),
                 reads=["ln%d" % b], writes=["rs%d" % b])
            for s in range(4):
                xs = xsr[s % 2]
                xsid = "xs%d" % (s % 2)
                S.op("dve", lambda e, s=s, xs=xs: e.tensor_scalar(out=xs[:], in0=xt[:, s, :], scalar1=rs[:, s:s + 1],
                                                                 scalar2=None, op0=ALU.mult),
                     reads=[xid, "rs%d" % b], writes=[xsid])
                tb = nxt("T", 2)
                pt = PT[tb]

                def tr(e, xs=xs, pt=pt):
                    r = None
                    for c in range(8):
                        r = e.transpose(pt[:, c * 128:(c + 1) * 128], xs[:, c * 128:(c + 1) * 128], C.ident[:])
                    return r
                S.op("pe", tr, reads=[xsid, "ident"], writes=["pT%d" % tb])
                S.op("act", lambda e, s=s, pt=pt: e.activation(
                    out=xT[:, :, s * 128:(s + 1) * 128], in_=pt[:].rearrange("p (c t) -> p c t", c=8), func=AF.Copy),
                    reads=["pT%d" % tb], writes=["xT%d_%d" % (b, s)])
        def tile(i):
            if i + 2 < NT:
                load_x(i + 2)
            b = i % 2
            xT, uT, qk, va = xTr[b], uTr[b], qkr[b], var[b]
            xTids = ["xT%d_%d" % (b, s) for s in range(4)]

            def proj(fc, pm):
                def f(e):
                    r = None
                    for c in range(8):
                        r = e.matmul(pm[:], lhsT=winb[:, c, fc * 128:(fc + 1) * 128], rhs=xT[:, c, :],
                                     start=(c == 0), stop=(c == 7))
                    return r
                return f
            for c4 in range(4):
                mv = nxt("M", 4)
                mg = nxt("M", 4)
                S.op("pe", proj(c4, PM[mv]), reads=xTids + WINB, writes=["pM%d" % mv])
                S.op("pe", proj(4 + c4, PM[mg]), reads=xTids + WINB, writes=["pM%d" % mg])
                eb = nxt("e", 2)
                et, dt_ = er[eb], dr_[eb]
                S.op("act", lambda e, mg=mg, et=et: e.activation(out=et[:], in_=PM[mg][:], func=AF.Exp, scale=-1.0),
                     reads=["pM%d" % mg], writes=["e%d" % eb])
                S.op("dve", lambda e, et=et, dt_=dt_: e.tensor_scalar(out=dt_[:], in0=et[:], scalar1=1.0, scalar2=None,
                                                                     op0=ALU.add),
                     reads=["e%d" % eb], writes=["d%d" % eb])
                S.op("dve", lambda e, et=et, dt_=dt_: e.reciprocal(out=et[:], in_=dt_[:]),
                     reads=["d%d" % eb], writes=["e%d" % eb])
                S.op("dve", lambda e, mv=mv, et=et, c4=c4: e.tensor_tensor(out=uT[:, c4, :], in0=PM[mv][:], in1=et[:],
                                                                         op=ALU.mult),
                     reads=["pM%d" % mv, "e%d" % eb], writes=["uT%d_%d" % (b, c4)])
            S.dma("uT%d" % b, lambda e: [e.dma_start(
                out=dr["UT"][:, 32 + i * 512:32 + (i + 1) * 512].rearrange("(c p) t -> p c t", p=128), in_=uT[:])],
                reads=["uT%d_%d" % (b, c4) for c4 in range(4)], store=True)
            if i + 1 < NT:
                front(i + 1)
            pend = []

            def qk_front(c8):
                m = nxt("M", 4)
                S.op("pe", proj(8 + c8, PM[m]), reads=xTids + WINB, writes=["pM%d" % m])
                qb = nxt("q", 2)
                sq = sqr[qb]
                S.op("act", lambda e, m=m, sq=sq: e.activation(out=sq[:], in_=PM[m][:], func=AF.Square),
                     reads=["pM%d" % m], writes=["sq%d" % qb])
                pend.append((c8, m, qb))

            def qk_back():
                (c8, m, qb) = pend.pop(0)
                sq, lq, rq = sqr[qb], lqr[qb], rqr[qb]
                sbk = nxt("S", 2)
                S.op("pe", lambda e, sq=sq, sbk=sbk: e.matmul(PS_[sbk][:], lhsT=C.blk[:], rhs=sq[:], start=True, stop=True),
                     reads=["sq%d" % qb, "blk"], writes=["pS%d" % sbk])
                S.op("act", lambda e, lq=lq, sbk=sbk: e.activation(out=lq[:], in_=PS_[sbk][:], func=AF.Ln, scale=1.0 / 64,
                                                                  bias=C.eps[:, 0:1]),
                     reads=["pS%d" % sbk], writes=["lq%d" % qb])
                S.op("act", lambda e, lq=lq, rq=rq: e.activation(out=rq[:], in_=lq[:], func=AF.Exp, scale=-0.5),
                     reads=["lq%d" % qb], writes=["rq%d" % qb])
                gcol = 0 if c8 < 4 else 1
                S.op("dve", lambda e, m=m, rq=rq, c8=c8, gcol=gcol: e.scalar_tensor_tensor(
                    out=qk[:, c8, :], in0=PM[m][:], scalar=gqs[:, gcol:gcol + 1], in1=rq[:], op0=ALU.mult, op1=ALU.mult),
                    reads=["pM%d" % m, "rq%d" % qb, "gqs_a", "gqs_b"], writes=["qk%d_%d" % (b, c8)])

            for c8 in range(8):
                qk_front(c8)
                if c8 > 0:
                    qk_back()
            qk_back()
            S.dma("qk%d" % b, lambda e: [
                e.dma_start(out=dr["QT"][:, i * 512:(i + 1) * 512].rearrange("(c p) t -> p c t", p=128), in_=qk[:, 0:4, :]),
                e.dma_start(out=dr["KT"][:, i * 512:(i + 1) * 512].rearrange("(c p) t -> p c t", p=128), in_=qk[:, 4:8, :]),
            ], reads=["qk%d_%d" % (b, c8) for c8 in range(8)], n=2, store=True)
            for s in range(4):
                m = nxt("M", 4)

                def vproj(e, s=s, m=m):
                    r = None
                    for c in range(8):
                        r = e.matmul(PM[m][:], lhsT=xT[:, c, s * 128:(s + 1) * 128], rhs=winb[:, c, 2048:2560],
                                     start=(c == 0), stop=(c == 7))
                    return r
                S.op("pe", vproj, reads=xTids + WINB, writes=["pM%d" % m])
                S.op("act", lambda e, s=s, m=m: e.activation(out=va[:, s, :, 0:64],
                                                             in_=PM[m][:].rearrange("p (h e) -> p h e", h=8), func=AF.Copy),
                     reads=["pM%d" % m, "va_ones%d" % b], writes=["va%d_%d" % (b, s)])
            S.dma("va%d" % b, lambda e: [e.dma_start(
                out=dr["VA"][i * 512:(i + 1) * 512, :, :].rearrange("(s p) h e -> p s h e", p=128), in_=va[:])],
                reads=["va%d_%d" % (b, s) for s in range(4)], store=True)
        load_x(0)
        if NT > 1:
            load_x(1)
        front(0)
        for i in range(NT):
            tile(i)
        S.dram_barrier()
        S.emit()
    nc.all_engine_barrier()


def pass2a(nc, S, T, dr, C):
    NT = T // 512
    with contextlib.ExitStack() as st:
        sb, ps = _mk(nc, st)
        cw = sb("cw", [128, 4, 31], F32)
        S.dma("cw", lambda e: [e.dma_start(out=cw[:, c, :], in_=dr["conv_w"][:, c * 128:(c + 1) * 128].rearrange("k p -> p k"),
                                           allow_slow_non_contiguous=True) for c in range(4)], writes=["cw"], n=4)
        cb = load_vec(S, nc, sb, "cb", dr["conv_b"], 4)
        cg = load_vec(S, nc, sb, "cg", dr["cn_g"], 4)
        cnb = load_vec(S, nc, sb, "cnb", dr["cn_b"], 4)
        ncg = sb("ncg", [128, 4], F32)
        ncb = sb("ncb", [128, 4], F32)
        S.op("dve", lambda e: e.tensor_scalar(out=ncg[:], in0=cg[:], scalar1=-1.0, scalar2=None, op0=ALU.mult),
             reads=["cg"], writes=["ncg"])
        S.op("dve", lambda e: e.tensor_scalar(out=ncb[:], in0=cnb[:], scalar1=-1.0, scalar2=None, op0=ALU.mult),
             reads=["cnb"], writes=["ncb"])
        diag = sb("diag", [128, 124, 128], BF16)
        for c in range(4):
            for k in range(31):
                if k % 2 == 0:
                    S.op("dve", lambda e, c=c, k=k: e.tensor_scalar(out=diag[:, c * 31 + k, :], in0=C.ident[:],
                                                                    scalar1=cw[:, c, k:k + 1], scalar2=None, op0=ALU.mult),
                         reads=["cw", "ident"], writes=["diag_%d_%d" % (c, k)])
                else:
                    S.op("act", lambda e, c=c, k=k: e.activation(out=diag[:, c * 31 + k, :], in_=C.ident[:],
                                                                 func=AF.Identity, scale=cw[:, c, k:k + 1]),
                         reads=["cw", "ident"], writes=["diag_%d_%d" % (c, k)])
        diag_ids = ["diag_%d_%d" % (c, k) for c in range(4) for k in range(31)]
        ur = [sb("U%d" % i, [128, 4, 544], BF16) for i in range(2)]
        y32r = [sb("y32_%d" % i, [128, 4, 512], F32) for i in range(2)]
        ybr = [sb("yb_%d" % i, [128, 4, 512], BF16) for i in range(2)]
        ysr = [sb("ys_%d" % i, [128, 4, 512], BF16) for i in range(2)]
        aTr = [sb("aT_%d" % i, [128, 4, 512], BF16) for i in range(2)]
        mean = sb("mean", [128, 512], F32)
        msq = sb("msq", [128, 512], F32)
        varr = sb("var", [128, 512], F32)
        lv = sb("lv", [128, 512], F32)
        rstd = sb("rstd", [128, 512], F32)
        er = [sb("ce%d" % i, [128, 512], F32) for i in range(2)]
        dd = [sb("cd%d" % i, [128, 512], F32) for i in range(2)]
        PM = [ps("pM%d" % i, [128, 512], F32) for i in range(4)]
        PS1 = ps("pS1", [128, 512], F32)
        PS2 = ps("pS2", [128, 512], F32)
        cnt = {"M": 0, "e": 0}

        def nxt(k, n):
            v = cnt[k] % n
            cnt[k] += 1
            return v

        def load_u(i):
            U = ur[i % 2]
            S.dma("U%d" % (i % 2), lambda e: [e.dma_start(
                out=U[:, :, 0:542], in_=dr["UT"][:, 2 + i * 512:2 + i * 512 + 542].rearrange("(c p) t -> p c t", p=128))],
                writes=["U%d" % (i % 2)])

        def tile(i):
            if i + 1 < NT:
                load_u(i + 1)
            b = i % 2
            U, y32, yb, ys, aT = ur[b], y32r[b], ybr[b], ysr[b], aTr[b]
            for c in range(4):
                m = nxt("M", 4)

                def conv(e, c=c, m=m):
                    r = None
                    for k in range(31):
                        r = e.matmul(PM[m][:], lhsT=diag[:, c * 31 + k, :], rhs=U[:, c, k:k + 512],
                                     start=(k == 0), stop=(k == 30))
                    return r
                S.op("pe", conv, reads=["U%d" % b] + diag_ids[c * 31:(c + 1) * 31], writes=["pM%d" % m])
                S.op("act", lambda e, c=c, m=m: e.activation(out=y32[:, c, :], in_=PM[m][:], func=AF.Identity,
                                                             bias=cb[:, c:c + 1]),
                     reads=["pM%d" % m, "cb"], writes=["y32_%d_%d" % (b, c)])
                S.op("act", lambda e, c=c, m=m: e.activation(out=yb[:, c, :], in_=PM[m][:], func=AF.Identity,
                                                             bias=cb[:, c:c + 1]),
                     reads=["pM%d" % m, "cb"], writes=["yb_%d_%d" % (b, c)])
                S.op("act", lambda e, c=c, m=m: e.activation(out=ys[:, c, :], in_=PM[m][:], func=AF.Square,
                                                             bias=cb[:, c:c + 1]),
                     reads=["pM%d" % m, "cb"], writes=["ys_%d_%d" % (b, c)])

            def st1(e):
                r = None
                for c in range(4):
                    r = e.matmul(PS1[:], lhsT=C.ones[:], rhs=yb[:, c, :], start=(c == 0), stop=(c == 3))
                return r

            def st2(e):
                r = None
                for c in range(4):
                    r = e.matmul(PS2[:], lhsT=C.ones[:], rhs=ys[:, c, :], start=(c == 0), stop=(c == 3))
                return r
            S.op("pe", st1, reads=["yb_%d_%d" % (b, c) for c in range(4)] + ["ones"], writes=["pS1"])
            S.op("pe", st2, reads=["ys_%d_%d" % (b, c) for c in range(4)] + ["ones"], writes=["pS2"])
            S.op("dve", lambda e: e.tensor_scalar(out=mean[:], in0=PS1[:], scalar1=1.0 / 512, scalar2=None, op0=ALU.mult),
                 reads=["pS1"], writes=["mean"])
            S.op("dve", lambda e: e.tensor_tensor(out=msq[:], in0=mean[:], in1=mean[:], op=ALU.mult),
                 reads=["mean"], writes=["msq"])
            S.op("dve", lambda e: e.scalar_tensor_tensor(out=varr[:], in0=PS2[:], scalar=1.0 / 512, in1=msq[:],
                                                         op0=ALU.mult, op1=ALU.subtract),
                 reads=["pS2", "msq"], writes=["var"])
            S.op("act", lambda e: e.activation(out=lv[:], in_=varr[:], func=AF.Ln, bias=C.eps[:, 0:1]),
                 reads=["var", "eps"], writes=["lv"])
            S.op("act", lambda e: e.activation(out=rstd[:], in_=lv[:], func=AF.Exp, scale=-0.5),
                 reads=["lv"], writes=["rstd"])
            for c in range(4):
                yid = "y32_%d_%d" % (b, c)
                eb = nxt("e", 2)
                et, dt_ = er[eb], dd[eb]
                S.op("dve", lambda e, c=c: e.tensor_tensor(out=y32[:, c, :], in0=y32[:, c, :], in1=mean[:], op=ALU.subtract),
                     reads=[yid, "mean"], writes=[yid])
                S.op("dve", lambda e, c=c: e.tensor_tensor(out=y32[:, c, :], in0=y32[:, c, :], in1=rstd[:], op=ALU.mult),
                     reads=[yid, "rstd"], writes=[yid])
                S.op("act", lambda e, c=c, et=et: e.activation(out=et[:], in_=y32[:, c, :], func=AF.Exp,
                                                               scale=ncg[:, c:c + 1], bias=ncb[:, c:c + 1]),
                     reads=[yid, "ncg", "ncb"], writes=["ce%d" % eb])
                S.op("dve", lambda e, c=c: e.tensor_scalar(out=y32[:, c, :], in0=y32[:, c, :], scalar1=cg[:, c:c + 1],
                                                           scalar2=cnb[:, c:c + 1], op0=ALU.mult, op1=ALU.add),
                     reads=[yid, "cg", "cnb", "ce%d" % eb], writes=[yid])
                S.op("dve", lambda e, et=et, dt_=dt_: e.tensor_scalar(out=dt_[:], in0=et[:], scalar1=1.0, scalar2=None,
                                                                     op0=ALU.add),
                     reads=["ce%d" % eb], writes=["cd%d" % eb])
                S.op("dve", lambda e, et=et, dt_=dt_: e.reciprocal(out=et[:], in_=dt_[:]),
                     reads=["cd%d" % eb], writes=["ce%d" % eb])
                S.op("dve", lambda e, c=c, et=et: e.tensor_tensor(out=aT[:, c, :], in0=y32[:, c, :], in1=et[:], op=ALU.mult),
                     reads=[yid, "ce%d" % eb], writes=["aT_%d_%d" % (b, c)])
            S.dma("aT%d" % b, lambda e: [e.dma_start(
                out=dr["MIXT"][0:512, i * 512:(i + 1) * 512].rearrange("(c p) t -> p c t", p=128), in_=aT[:])],
                reads=["aT_%d_%d" % (b, c) for c in range(4)], store=True)
        load_u(0)
        for i in range(NT):
            tile(i)
        S.dram_barrier()
        S.emit()
    nc.all_engine_barrier()


def pass2c(nc, S, T, dr, C):
    NT = T // 512
    with contextlib.ExitStack() as st:
        sb, ps = _mk(nc, st)
        woutb = sb("woutb", [128, 8, 1024], BF16)
        wst = [sb("wost%d" % i, [128, 1024], F32) for i in range(2)]
        for c in range(8):
            sid = "wost%d" % (c % 2)
            stg = wst[c % 2]
            S.dma(sid, lambda e, stg=stg, c=c: [e.dma_start(out=stg[:], in_=dr["w_out"][c * 128:(c + 1) * 128, :])],
                  writes=[sid])
            if c % 2 == 0:
                S.op("dve", lambda e, stg=stg, c=c: e.tensor_copy(out=woutb[:, c, :], in_=stg[:]),
                     reads=[sid], writes=["woutb%d" % c])
            else:
                S.op("act", lambda e, stg=stg, c=c: e.activation(out=woutb[:, c, :], in_=stg[:], func=AF.Copy),
                     reads=[sid], writes=["woutb%d" % c])
        mr = [sb("mix%d" % i, [128, 8, 512], BF16) for i in range(2)]
        xr = [sb("x%d" % i, [128, 4, 1024], F32) for i in range(2)]
        x1r = [sb("x1_%d" % i, [128, 4, 1024], F32) for i in range(2)]
        x1Tr = [sb("x1T%d" % i, [128, 8, 512], BF16) for i in range(2)]
        junk = sb("junk", [128, 1024], BF16)
        ssr = [sb("ss%d" % i, [128, 4], F32) for i in range(2)]
        lnr = [sb("ln%d" % i, [128, 4], F32) for i in range(2)]
        rsr = [sb("rs%d" % i, [128, 4], F32) for i in range(2)]
        xsr = [sb("xs%d" % i, [128, 1024], BF16) for i in range(2)]
        PT = [ps("pT%d" % i, [128, 1024], BF16) for i in range(2)]
        PM = [ps("pM%d" % i, [128, 512], F32) for i in range(4)]
        cnt = {"M": 0, "T": 0}

        def nxt(k, n):
            v = cnt[k] % n
            cnt[k] += 1
            return v

        def load(i):
            b = i % 2
            mix, xt = mr[b], xr[b]
            S.dma("mix%d" % b, lambda e: [e.dma_start(
                out=mix[:], in_=dr["MIXT"][:, i * 512:(i + 1) * 512].rearrange("(c p) t -> p c t", p=128))],
                writes=["mix%d" % b])
            S.dma("x%d" % b, lambda e: [e.dma_start(
                out=xt[:], in_=dr["x"][i * 512:(i + 1) * 512, :].rearrange("(s p) d -> p s d", p=128))],
                writes=["x%d" % b])

        def tile(i):
            if i + 1 < NT:
                load(i + 1)
            b = i % 2
            mix, xt, x1, x1T, ss, ln, rs = mr[b], xr[b], x1r[b], x1Tr[b], ssr[b], lnr[b], rsr[b]
            for s in range(4):
                for hh in range(2):
                    m = nxt("M", 4)

                    def mm(e, s=s, hh=hh, m=m):
                        r = None
                        for c in range(8):
                            r = e.matmul(PM[m][:], lhsT=mix[:, c, s * 128:(s + 1) * 128],
                                         rhs=woutb[:, c, hh * 512:(hh + 1) * 512], start=(c == 0), stop=(c == 7))
                        return r
                    S.op("pe", mm, reads=["mix%d" % b] + ["woutb%d" % c for c in range(8)], writes=["pM%d" % m])
                    S.op("dve", lambda e, s=s, hh=hh, m=m: e.tensor_tensor(
                        out=x1[:, s, hh * 512:(hh + 1) * 512], in0=PM[m][:], in1=xt[:, s, hh * 512:(hh + 1) * 512], op=ALU.add),
                        reads=["pM%d" % m, "x%d" % b], writes=["x1_%d_%d_%d" % (b, s, hh)])
                S.op("act", lambda e, s=s: e.activation(out=junk[:], in_=x1[:, s, :], func=AF.Square,
                                                        accum_out=ss[:, s:s + 1]),
                     reads=["x1_%d_%d_0" % (b, s), "x1_%d_%d_1" % (b, s)], writes=["junk", "ss%d_%d" % (b, s)])
            x1ids = ["x1_%d_%d_%d" % (b, s, hh) for s in range(4) for hh in range(2)]
            S.dma("x1_%d" % b, lambda e: [e.dma_start(
                out=dr["X1"][i * 512:(i + 1) * 512, :].rearrange("(s p) d -> p s d", p=128), in_=x1[:])],
                reads=x1ids, store=True)
            S.op("act", lambda e: e.activation(out=ln[:], in_=ss[:], func=AF.Ln, scale=1.0 / 1024, bias=C.eps[:, 0:1]),
                 reads=["ss%d_%d" % (b, s) for s in range(4)] + ["eps"], writes=["ln%d" % b])
            S.op("act", lambda e: e.activation(out=rs[:], in_=ln[:], func=AF.Exp, scale=-0.5),
                 reads=["ln%d" % b], writes=["rs%d" % b])
            for s in range(4):
                xs = xsr[s % 2]
                xsid = "xs%d" % (s % 2)
                S.op("dve", lambda e, s=s, xs=xs: e.tensor_scalar(out=xs[:], in0=x1[:, s, :], scalar1=rs[:, s:s + 1],
                                                                 scalar2=None, op0=ALU.mult),
                     reads=["x1_%d_%d_0" % (b, s), "x1_%d_%d_1" % (b, s), "rs%d" % b], writes=[xsid])
                tb = nxt("T", 2)
                pt = PT[tb]

                def tr(e, xs=xs, pt=pt):
                    r = None
                    for c in range(8):
                        r = e.transpose(pt[:, c * 128:(c + 1) * 128], xs[:, c * 128:(c + 1) * 128], C.ident[:])
                    return r
                S.op("pe", tr, reads=[xsid, "ident"], writes=["pT%d" % tb])
                S.op("act", lambda e, s=s, pt=pt: e.activation(
                    out=x1T[:, :, s * 128:(s + 1) * 128], in_=pt[:].rearrange("p (c t) -> p c t", c=8), func=AF.Copy),
                    reads=["pT%d" % tb], writes=["x1T%d_%d" % (b, s)])
            S.dma("x1T%d" % b, lambda e: [e.dma_start(
                out=dr["X1T"][:, i * 512:(i + 1) * 512].rearrange("(c p) t -> p c t", p=128), in_=x1T[:])],
                reads=["x1T%d_%d" % (b, s) for s in range(4)], store=True)
        load(0)
        for i in range(NT):
            tile(i)
        S.dram_barrier()
        S.emit()
    nc.all_engine_barrier()


def pass3(nc, S, T, dr, C, hf):
    NT = T // 512
    NP = 11
    with contextlib.ExitStack() as st:
        sb, ps = _mk(nc, st)
        g2t = load_vec(S, nc, sb, "g2t%d" % hf, dr["norm2_g"], 8)
        ffb = load_vec(S, nc, sb, "ffb%d" % hf, dr["ffconv_b"], 44)
        ffw = sb("ffw", [128, 44, 3], F32)
        S.dma("ffw%d" % hf, lambda e: [e.dma_start(out=ffw[:, :, k], in_=dr["ffconv_w"][k, :].rearrange("(j p) -> p j", p=128),
                                                   allow_slow_non_contiguous=True) for k in range(3)], writes=["ffw"], n=3)
        wupb = sb("wupb", [128, 8, 2816], BF16)
        wdb = sb("wdb", [128, NP, 1024], BF16)
        ust = [sb("ust%d" % i, [128, 2816], F32) for i in range(2)]
        dst = [sb("dst%d" % i, [128, 1024], F32) for i in range(2)]
        g2id = "g2t%d" % hf
        for c in range(8):
            sid = "ust%d" % (c % 2)
            stg = ust[c % 2]
            S.dma(sid, lambda e, stg=stg, c=c: [
                e.dma_start(out=stg[:, 0:1408], in_=dr["w_up"][c * 128:(c + 1) * 128, hf * 1408:(hf + 1) * 1408]),
                e.dma_start(out=stg[:, 1408:2816], in_=dr["w_up"][c * 128:(c + 1) * 128, 2816 + hf * 1408:2816 + (hf + 1) * 1408]),
            ], writes=[sid], n=2)
            if c % 2 == 0:
                S.op("dve", lambda e, stg=stg, c=c: e.tensor_scalar(out=wupb[:, c, :], in0=stg[:], scalar1=g2t[:, c:c + 1],
                                                                    scalar2=None, op0=ALU.mult),
                     reads=[sid, g2id], writes=["wupb%d" % c])
            else:
                S.op("act", lambda e, stg=stg, c=c: e.activation(out=wupb[:, c, :], in_=stg[:], func=AF.Identity,
                                                                 scale=g2t[:, c:c + 1]),
                     reads=[sid, g2id], writes=["wupb%d" % c])
        for j in range(NP):
            sid = "dst%d" % (j % 2)
            stg = dst[j % 2]
            r0 = (hf * NP + j) * 128
            S.dma(sid, lambda e, stg=stg, r0=r0: [e.dma_start(out=stg[:], in_=dr["w_down"][r0:r0 + 128, :])], writes=[sid])
            if j % 2 == 0:
                S.op("act", lambda e, stg=stg, j=j: e.activation(out=wdb[:, j, :], in_=stg[:], func=AF.Identity, scale=0.5),
                     reads=[sid], writes=["wdb%d" % j])
            else:
                S.op("dve", lambda e, stg=stg, j=j: e.tensor_scalar(out=wdb[:, j, :], in0=stg[:], scalar1=0.5, scalar2=None,
                                                                    op0=ALU.mult),
                     reads=[sid], writes=["wdb%d" % j])
        hal = sb("hal", [128, 22, 2], BF16)
        S.op("pool", lambda e: e.memset(hal[:], 0.0), writes=["hal%d" % k for k in range(22)])
        xTr = [sb("xT%d" % i, [128, 8, 512], BF16) for i in range(2)]
        pr = [sb("prev%d" % i, [128, 4, 1024], F32) for i in range(2)]
        Hr = [sb("H%d" % i, [128, NP, 512], BF16) for i in range(2)]
        Ur = [sb("Ub%d" % i, [128, 514], BF16) for i in range(4)]
        A2r = [sb("A2_%d" % i, [128, 512], F32) for i in range(4)]
        tbr = [sb("tb%d" % i, [128, 512], F32) for i in range(2)]
        PM = [ps("pM%d" % i, [128, 512], F32) for i in range(4)]
        PD = [ps("pD%d" % i, [128, 512], F32) for i in range(2)]
        cnt = {"M": 0, "D": 0, "U": 0, "t": 0}
        prev_src = dr["X1"] if hf == 0 else dr["out"]

        def nxt(k, n):
            v = cnt[k] % n
            cnt[k] += 1
            return v

        def load_xT(i):
            b = i % 2
            xT = xTr[b]
            S.dma("xT%d" % b, lambda e: [e.dma_start(
                out=xT[:], in_=dr["X1T"][:, i * 512:(i + 1) * 512].rearrange("(c p) t -> p c t", p=128))],
                writes=["xT%d" % b])

        def load_prev(i):
            b = i % 2
            pv = pr[b]
            S.dma("prev%d" % b, lambda e: [e.dma_start(
                out=pv[:], in_=prev_src[i * 512:(i + 1) * 512, :].rearrange("(s p) d -> p s d", p=128))],
                writes=["prev%d_%d_%d" % (b, s, hh) for s in range(4) for hh in range(2)])

        def down_group(i, gi):
            b = i % 2
            pv, H = pr[b], Hr[b]
            s, hh = gi // 2, gi % 2
            Hids = ["H%d_%d" % (b, jj) for jj in range(NP)]
            d = nxt("D", 2)

            def down(e):
                r = None
                for jj in range(NP):
                    r = e.matmul(PD[d][:], lhsT=H[:, jj, s * 128:(s + 1) * 128],
                                 rhs=wdb[:, jj, hh * 512:(hh + 1) * 512], start=(jj == 0), stop=(jj == NP - 1))
                return r
            S.op("pe", down, reads=Hids + ["wdb%d" % j for j in range(NP)], writes=["pD%d" % d])
            pid = "prev%d_%d_%d" % (b, s, hh)
            S.op("dve", lambda e: e.tensor_tensor(
                out=pv[:, s, hh * 512:(hh + 1) * 512], in0=PD[d][:], in1=pv[:, s, hh * 512:(hh + 1) * 512], op=ALU.add),
                reads=["pD%d" % d, pid], writes=[pid])

        def store(i):
            b = i % 2
            pv = pr[b]
            S.dma("prev%d" % b, lambda e: [e.dma_start(
                out=dr["out"][i * 512:(i + 1) * 512, :].rearrange("(s p) d -> p s d", p=128), in_=pv[:])],
                reads=["prev%d_%d_%d" % (b, s, hh) for s in range(4) for hh in range(2)], store=True)

        def tile(i):
            if i + 1 < NT:
                load_xT(i + 1)
            if i == 0 and NT > 1:
                load_prev(1)
            b = i % 2
            xT, H = xTr[b], Hr[b]

            def branch(jj, isval):
                col0 = jj * 128 + (1408 if isval else 0)
                J = hf * NP + jj + (22 if isval else 0)
                hidx = jj + (11 if isval else 0)
                m = nxt("M", 4)
                u = nxt("U", 4)
                U, A2 = Ur[u], A2r[u]

                def up(e):
                    r = None
                    for c in range(8):
                        r = e.matmul(PM[m][:], lhsT=wupb[:, c, col0:col0 + 128], rhs=xT[:, c, :],
                                     start=(c == 0), stop=(c == 7))
                    return r
                S.op("pe", up, reads=["xT%d" % b] + ["wupb%d" % c for c in range(8)], writes=["pM%d" % m])
                S.op("pool", lambda e: e.tensor_copy(out=U[:, 0:2], in_=hal[:, hidx, :]),
                     reads=["hal%d" % hidx], writes=["Ub%d_h" % u])
                S.op("act", lambda e: e.activation(out=U[:, 2:514], in_=PM[m][:], func=AF.Copy),
                     reads=["pM%d" % m], writes=["Ub%d_b" % u])
                S.op("act", lambda e: e.activation(out=A2[:], in_=PM[m][:], func=AF.Identity,
                                                   scale=ffw[:, J, 2:3], bias=ffb[:, J:J + 1]),
                     reads=["pM%d" % m, "ffw", "ffb%d" % hf], writes=["A2_%d" % u])
                S.op("pool", lambda e: e.tensor_copy(out=hal[:, hidx, :], in_=U[:, 512:514]),
                     reads=["Ub%d_b" % u], writes=["hal%d" % hidx])
                S.op("dve", lambda e: e.scalar_tensor_tensor(out=A2[:], in0=U[:, 1:513], scalar=ffw[:, J, 1:2], in1=A2[:],
                                                             op0=ALU.mult, op1=ALU.add),
                     reads=["Ub%d_h" % u, "Ub%d_b" % u, "A2_%d" % u, "ffw"], writes=["A2_%d" % u])
                S.op("dve", lambda e: e.scalar_tensor_tensor(out=A2[:], in0=U[:, 0:512], scalar=ffw[:, J, 0:1], in1=A2[:],
                                                             op0=ALU.mult, op1=ALU.add),
                     reads=["Ub%d_h" % u, "Ub%d_b" % u, "A2_%d" % u, "ffw"], writes=["A2_%d" % u])
                return u

            for jj in range(NP):
                ug = branch(jj, False)
                uv = branch(jj, True)
                t = nxt("t", 2)
                tb = tbr[t]
                zg, zv = A2r[ug], A2r[uv]
                S.op("act", lambda e, tb=tb, zg=zg: e.activation(out=tb[:], in_=zg[:], func=AF.Tanh, scale=0.5),
                     reads=["A2_%d" % ug], writes=["tb%d" % t])
                S.op("dve", lambda e, tb=tb, zg=zg: e.scalar_tensor_tensor(out=tb[:], in0=tb[:], scalar=1.0, in1=zg[:],
                                                                          op0=ALU.add, op1=ALU.mult),
                     reads=["tb%d" % t, "A2_%d" % ug], writes=["tb%d" % t])
                S.op("dve", lambda e, tb=tb, zv=zv, jj=jj: e.tensor_tensor(out=H[:, jj, :], in0=tb[:], in1=zv[:], op=ALU.mult),
                     reads=["tb%d" % t, "A2_%d" % uv], writes=["H%d_%d" % (b, jj)])
                if i > 0 and jj < 8:
                    down_group(i - 1, jj)
                    if jj == 7:
                        store(i - 1)
                        if i + 1 < NT:
                            load_prev(i + 1)
        load_xT(0)
        load_prev(0)
        for i in range(NT):
            tile(i)
        for gi in range(8):
            down_group(NT - 1, gi)
        store(NT - 1)
        S.dram_barrier()
        S.emit()
    nc.all_engine_barrier()


def pass2b(nc, S, T, dr, C):
    with contextlib.ExitStack() as st:
        sb, ps = _mk(nc, st)
        mk = sb("mk", [128, 24, 512], BF16)
        mst = [sb("mst%d" % i, [128, 1024], F32) for i in range(2)]
        for g in range(12):
            sid = "mst%d" % (g % 2)
            stg = mst[g % 2]
            S.dma(sid, lambda e, stg=stg, g=g: [e.dma_start(out=stg[:], in_=dr["amask"][:, g * 1024:(g + 1) * 1024])],
                  writes=[sid])
            if g % 2 == 0:
                S.op("dve", lambda e, stg=stg, g=g: e.tensor_copy(out=mk[:, g * 2:(g + 1) * 2, :],
                                                                  in_=stg[:].rearrange("p (a b) -> p a b", a=2)),
                     reads=[sid], writes=["mk"])
            else:
                S.op("act", lambda e, stg=stg, g=g: e.activation(out=mk[:, g * 2:(g + 1) * 2, :],
                                                                 in_=stg[:].rearrange("p (a b) -> p a b", a=2), func=AF.Copy),
                     reads=[sid], writes=["mk"])
        qtr = [sb("QTt%d" % i, [128, T], BF16) for i in range(1)]
        ktr = [sb("KTt%d" % i, [128, T], BF16) for i in range(1)]
        ACC = [sb("ACC%d" % i, [128, T], F32) for i in range(2)]
        OTst = sb("OTst", [128, T], BF16)
        RC = 2048 if T >= 2048 else T
        Rt = [sb("Rt%d" % i, [64, RC], F32) for i in range(2)]
        NVA = 8
        VAr = [sb("VAt%d" % i, [128, 2, 128], BF16) for i in range(NVA)]
        Pr = [sb("P%d" % i, [128, 2, 512], BF16) for i in range(4)]
        PSs = [[ps("pSs%d_%d" % (hh, i), [128, 512], F32) for i in range(2)] for hh in range(2)]
        PO = [[ps("pO%d_%d" % (hh, i), [128, 512], F32) for i in range(2)] for hh in range(2)]
        cnt = {"S": 0, "E": 0, "V": 0, "G0": 0, "G1": 0, "R": 0}

        def nxt(k, n):
            v = cnt[k] % n
            cnt[k] += 1
            return v

        def load_qk(hp):
            b = 0
            S.dma("QTt%d" % b, lambda e: [e.dma_start(out=qtr[b][:], in_=dr["QT"][hp * 128:(hp + 1) * 128, :])],
                  writes=["QTt%d" % b])
            S.dma("KTt%d" % b, lambda e: [e.dma_start(out=ktr[b][:], in_=dr["KT"][hp * 128:(hp + 1) * 128, :])],
                  writes=["KTt%d" % b])

        def pair(hp):
            load_qk(hp)
            b = 0
            QTt, KTt = qtr[b], ktr[b]
            qid, kid = "QTt%d" % b, "KTt%d" % b
            items = []

            def accids(hh, lo, hi):
                return ["acc%d_%d" % (hh, bb) for bb in range(lo // 2048, (hi - 1) // 2048 + 1)]

            def stageA(item):
                (pi, d, nb, gs, r, grp_bank, fresh, j0) = item
                js = [j for j in (j0, j0 + 1) if j < nb]
                nqs = [256 if j + 1 < nb else 128 for j in js]
                ncols = 256 * (len(js) - 1) + nqs[-1]
                vts = []
                for j in js:
                    t_lo = 128 * j * d + r
                    v = nxt("V", NVA)
                    S.dma("VAt%d" % v, lambda e, v=v, t_lo=t_lo: [e.dma_start(
                        out=VAr[v][:], in_=dr["VA"][t_lo:t_lo + 127 * d + 1:d, 2 * hp:2 * hp + 2, :])],
                        writes=["VAt%d" % v])
                    vts.append(v)
                sbk = nxt("S", 2)
                eb = nxt("E", 4)
                P = Pr[eb]

                def smm(e):
                    rr = None
                    for bi, j in enumerate(js):
                        t_lo = 128 * j * d + r
                        for hh in range(2):
                            pb = hh * 64
                            rr = e.matmul(PSs[hh][sbk][:, bi * 256:bi * 256 + nqs[bi]],
                                          lhsT=KTt[pb:pb + 64, t_lo:t_lo + 127 * d + 1:d],
                                          rhs=QTt[pb:pb + 64, t_lo:t_lo + (nqs[bi] - 1) * d + 1:d],
                                          start=True, stop=False)
                        for hh in range(2):
                            m0 = (2 * hp + hh) * 3 + pi
                            rr = e.matmul(PSs[hh][sbk][:, bi * 256:bi * 256 + nqs[bi]],
                                          lhsT=C.ident[:], rhs=mk[:, m0, 0:nqs[bi]], start=False, stop=True)
                    return rr
                S.op("pe", smm, reads=[qid, kid, "mk", "ident"], writes=["pSs0_%d" % sbk, "pSs1_%d" % sbk])
                for hh in range(2):
                    S.op("act", lambda e, hh=hh: e.activation(
                        out=P[:, hh, 0:ncols], in_=PSs[hh][sbk][:, 0:ncols], func=AF.Exp),
                        reads=["pSs%d_%d" % (hh, sbk)], writes=["P%d_%d" % (eb, hh)])
                return (item, js, vts, eb)

            def stageB(ctx):
                (item, js, vts, eb) = ctx
                (pi, d, nb, gs, r, grp_bank, fresh, j0) = item
                P = Pr[eb]
                plan = []
                wr = []
                for hh in range(2):
                    for bi, j in enumerate(js):
                        for (qb, half) in ((j, 0), (j + 1, 1)):
                            if qb >= nb:
                                continue
                            g = qb // gs
                            key = (hh, g)
                            if key not in grp_bank:
                                grp_bank[key] = nxt("G%d" % hh, 2)
                                fresh[key] = True
                            bank = grp_bank[key]
                            plan.append((hh, bi, half, bank, (qb % gs) * 128, fresh[key]))
                            fresh[key] = False
                            bid = "pO%d_%d" % (hh, bank)
                            if bid not in wr:
                                wr.append(bid)

                def pv(e):
                    rr = None
                    for (hh, bi, half, bank, col, fr) in plan:
                        rr = e.matmul(PO[hh][bank][:, col:col + 128], lhsT=VAr[vts[bi]][:, hh, :],
                                      rhs=P[:, hh, bi * 256 + half * 128:bi * 256 + (half + 1) * 128],
                                      start=fr, stop=False, skip_group_check=True)
                    return rr
                S.op("pe", pv, reads=["P%d_0" % eb, "P%d_1" % eb] + ["VAt%d" % v for v in vts], writes=wr)
                for j in js:
                    if j % gs != gs - 1:
                        continue
                    g = j // gs
                    c_lo = g * gs * 128 * d + r
                    ncol = gs * 128
                    c_hi = c_lo + (ncol - 1) * d + 1
                    for hh in range(2):
                        bank = grp_bank[(hh, g)]
                        bid = "pO%d_%d" % (hh, bank)
                        acc = ACC[hh]
                        aids = accids(hh, c_lo, c_hi)
                        if pi == 0:
                            S.op("act", lambda e, hh=hh, bank=bank, acc=acc, c_lo=c_lo, c_hi=c_hi: e.activation(
                                out=acc[:, c_lo:c_hi:d], in_=PO[hh][bank][:, 0:ncol], func=AF.Copy),
                                reads=[bid], writes=aids)
                        else:
                            S.op("dve", lambda e, hh=hh, bank=bank, acc=acc, c_lo=c_lo, c_hi=c_hi: e.tensor_tensor(
                                out=acc[:, c_lo:c_hi:d], in0=PO[hh][bank][:, 0:ncol],
                                in1=acc[:, c_lo:c_hi:d], op=ALU.add),
                                reads=[bid] + aids, writes=aids)

            for pi, (w, d) in enumerate(PATTERNS):
                L = T // d
                nb = L // 128
                gs = min(4, nb)
                for r in range(d):
                    grp_bank = {}
                    fresh = {}

                    for j0 in range(0, nb, 2):
                        items.append((pi, d, nb, gs, r, grp_bank, fresh, j0))
            ctxs = {}
            LA = 3
            for k in range(len(items) + LA):
                if k < len(items):
                    ctxs[k] = stageA(items[k])
                if k - LA >= 0:
                    stageB(ctxs.pop(k - LA))
            for hh in range(2):
                acc = ACC[hh]
                for c0 in range(0, T, RC):
                    rb = nxt("R", 2)
                    Rtile = Rt[rb]
                    S.op("dve", lambda e, acc=acc, c0=c0, Rtile=Rtile: e.reciprocal(out=Rtile[0:64, :], in_=acc[64:128, c0:c0 + RC]),
                         reads=accids(hh, c0, c0 + RC), writes=["Rt%d" % rb])
                    S.op("dve", lambda e, acc=acc, c0=c0, Rtile=Rtile, hh=hh: e.tensor_tensor(
                        out=OTst[hh * 64:(hh + 1) * 64, c0:c0 + RC], in0=acc[0:64, c0:c0 + RC], in1=Rtile[0:64, :], op=ALU.mult),
                        reads=accids(hh, c0, c0 + RC) + ["Rt%d" % rb], writes=["OTst_%d_%d" % (hh, c0)])
            S.dma("OTst", lambda e: [e.dma_start(out=dr["MIXT"][512 + hp * 128:512 + (hp + 1) * 128, :], in_=OTst[:])],
                  reads=["OTst_%d_%d" % (hh, c0) for hh in range(2) for c0 in range(0, T, RC)], store=True)
        for hp in range(4):
            pair(hp)
        S.dram_barrier()
        S.emit()
    nc.all_engine_barrier()


def consts(nc, S, st, dr):
    sb, ps = _mk(nc, st)
    C = Ctx()
    C.eps = sb("eps", [128, 1], F32)
    C.ident = sb("ident", [128, 128], BF16)
    C.blk = sb("blk", [128, 128], BF16)
    C.ones = sb("ones", [128, 128], BF16)
    S.op("pool", lambda e: e.memset(C.eps[:], EPS), writes=["eps"])
    S.op("pool", lambda e: e.memset(C.ident[:], 0.0), writes=["ident"])
    S.op("pool", lambda e: e.affine_select(out=C.ident[:], in_=C.ident[:], pattern=[[-1, 128]],
                                           compare_op=ALU.not_equal, fill=1.0, base=0, channel_multiplier=1),
         reads=["ident"], writes=["ident"])
    S.op("pool", lambda e: e.memset(C.blk[:], 0.0), writes=["blk"])
    S.op("pool", lambda e: e.memset(C.blk[0:64, 0:64], 1.0), reads=["blk"], writes=["blk"])
    S.op("pool", lambda e: e.memset(C.blk[64:128, 64:128], 1.0), reads=["blk"], writes=["blk"])
    S.op("pool", lambda e: e.memset(C.ones[:], 1.0), writes=["ones"])
    return C


def build(T, stages=("p1",), debug=False):
    nc = bass.Bass("TRN2", target_bir_lowering=False)
    dr = {}

    def inp(name, shape):
        dr[name] = nc.dram_tensor(name, shape, F32, kind="ExternalInput").ap()
    inp("x", [T, 1024])
    inp("norm1_g", [1024])
    inp("w_in", [1024, 2560])
    inp("conv_w", [31, 512])
    inp("conv_b", [512])
    inp("cn_g", [512])
    inp("cn_b", [512])
    inp("q_norm_g", [64])
    inp("k_norm_g", [64])
    inp("w_out", [1024, 1024])
    inp("norm2_g", [1024])
    inp("w_up", [1024, 5632])
    inp("ffconv_w", [3, 5632])
    inp("ffconv_b", [5632])
    inp("w_down", [2816, 1024])
    inp("amask", [128, 24 * 512])
    dr["out"] = nc.dram_tensor("out", [T, 1024], F32, kind="ExternalOutput").ap()
    kind = "ExternalOutput" if debug else "Internal"

    def scr(name, shape, dt):
        dr[name] = nc.dram_tensor(name, shape, dt, kind=kind).ap()
    scr("UT", [512, 32 + T], BF16)
    scr("QT", [512, T], BF16)
    scr("KT", [512, T], BF16)
    scr("VA", [T, 8, 128], BF16)
    scr("MIXT", [1024, T], BF16)
    scr("X1", [T, 1024], F32)
    scr("X1T", [1024, T], BF16)
    with contextlib.ExitStack() as st:
        S = Sched(nc, st)
        C = consts(nc, S, st, dr)
        if "p1" in stages:
            pass1(nc, S, T, dr, C)
        if "p2a" in stages:
            pass2a(nc, S, T, dr, C)
        if "p2b" in stages:
            pass2b(nc, S, T, dr, C)
        if "p2c" in stages:
            pass2c(nc, S, T, dr, C)
        if "p3" in stages:
            pass3(nc, S, T, dr, C, 0)
            pass3(nc, S, T, dr, C, 1)
    return nc


ALL_STAGES = ("p1", "p2a", "p2b", "p2c", "p3")


def kernel(**inputs):
    T = 8192
    nc = build(T, stages=ALL_STAGES)
    in_maps = [host_inputs(inputs, b, T) for b in range(8)]
    res = run_bass_kernel_spmd(nc, in_maps, core_ids=list(range(8)))
    out = np.stack([np.asarray(r["out"]).reshape(T, 1024) for r in res.results], 0)
    return out.astype(np.float32)


def attn_mask_table():
    k = np.arange(128)[:, None]
    q = np.arange(256)[None, :]
    delta = q - k
    valid = (delta >= 0) & (delta <= 128)
    out = np.zeros((128, 24, 2, 256), np.float32)
    for h in range(8):
        slope = 2.0 ** (-(h + 1))
        for p, (w, d) in enumerate(PATTERNS):
            m = np.where(valid, -slope * d * np.maximum(delta, 0).astype(np.float64), -30000.0)
            out[:, h * 3 + p, 0] = m
            out[:, h * 3 + p, 1] = m
    return out.reshape(128, 24 * 512)


def host_inputs(inputs, b, T):
    m = {}
    for k, v in inputs.items():
        v = np.asarray(v)
        if k == "x":
            m[k] = np.ascontiguousarray(v[b, :T])
        else:
            m[k] = np.ascontiguousarray(v)
    m["amask"] = attn_mask_table()
    return m
```

```python
import contextlib
import numpy as np
import concourse.bass as bass
import concourse.mybir as mybir
from concourse.bass_utils import run_bass_kernel_spmd

F32 = mybir.dt.float32
BF16 = mybir.dt.bfloat16
ALU = mybir.AluOpType
AF = mybir.ActivationFunctionType

COMPUTE = ("pe", "act", "dve", "pool")
EPS = 1e-6
NH = 8
PATTERNS = ((128, 1), (512, 4), (2048, 16))


class Sched:
    def __init__(self, nc, stack):
        self.nc = nc
        self.stack = stack
        self.prog = {e: [] for e in COMPUTE + ("sp",)}
        self.cnt = {e: 0 for e in COMPUTE}
        self.sem = {e: stack.enter_context(nc.semaphore("s_" + e)) for e in COMPUTE}
        self.dma_sem = {}
        self.dma_cnt = {}
        self.last_w = {}
        self.readers = {}
        self.waited = {e: {} for e in self.prog}
        self.stores = []

    def _sem_of(self, k):
        return self.sem[k] if isinstance(k, str) else self.dma_sem[k[1]]

    def _deps(self, engine, reads, writes):
        toks = []
        for b in reads:
            if b in self.last_w:
                toks.append(self.last_w[b])
        for b in writes:
            if b in self.last_w:
                toks.append(self.last_w[b])
            toks.extend(self.readers.get(b, ()))
        best = {}
        for (k, v) in toks:
            if k == "pe" and engine == "pe":
                continue
            if best.get(k, 0) < v:
                best[k] = v
        out = []
        w = self.waited[engine]
        for k, v in best.items():
            if w.get(k, 0) >= v:
                continue
            w[k] = v
            out.append((k, v))
        return out

    def _record(self, tok, reads, writes):
        for b in writes:
            self.last_w[b] = tok
            self.readers[b] = []
        for b in reads:
            self.readers.setdefault(b, []).append(tok)

    def op(self, engine, fn, reads=(), writes=()):
        waits = self._deps(engine, reads, writes)
        self.cnt[engine] += 1
        tok = (engine, self.cnt[engine])
        self.prog[engine].append((waits, fn, engine, 1))
        self._record(tok, reads, writes)
        return tok

    def dma(self, key, fn, reads=(), writes=(), n=1, store=False):
        if key not in self.dma_sem:
            self.dma_sem[key] = self.stack.enter_context(
                self.nc.semaphore("d_%d" % len(self.dma_sem)))
            self.dma_cnt[key] = 0
        waits = self._deps("sp", reads, writes)
        self.dma_cnt[key] += n
        tok = (("dma", key), 16 * self.dma_cnt[key])
        self.prog["sp"].append((waits, fn, ("dma", key), n))
        self._record(tok, reads, writes)
        if store:
            self.stores.append(tok)
        return tok

    def dram_barrier(self):
        best = {}
        for (k, v) in self.stores:
            best[k] = max(best.get(k, 0), v)
        w = self.waited["sp"]
        waits = []
        for k, v in best.items():
            if w.get(k, 0) < v:
                w[k] = v
                waits.append((k, v))
        self.prog["sp"].append((waits, None, None, 0))
        self.stores = []

    def emit(self):
        nc = self.nc
        prog = self.prog
        self.prog = {e: [] for e in prog}
        with nc.Block() as block:
            def run(name):
                def body(e):
                    for (waits, fn, semkey, n) in prog[name]:
                        for (k, v) in waits:
                            e.wait_ge(self._sem_of(k), v)
                        if fn is None:
                            continue
                        r = fn(e)
                        if isinstance(semkey, str):
                            r.then_inc(self.sem[semkey], 1)
                        else:
                            if not isinstance(r, (list, tuple)):
                                r = [r]
                            assert len(r) == n, (len(r), n)
                            for ins in r:
                                ins.then_inc(self.dma_sem[semkey[1]], 16)
                return body
            block.tensor(run("pe"))
            block.scalar(run("act"))
            block.vector(run("dve"))
            block.gpsimd(run("pool"))
            block.sync(run("sp"))


class Ctx:
    pass


_UNIQ = [0]


def _mk(nc, st):
    _UNIQ[0] += 1
    pre = "k%d_" % _UNIQ[0]

    def sb(name, shape, dt):
        return st.enter_context(nc.sbuf_tensor(pre + name, shape, dt))

    def ps(name, shape, dt):
        return st.enter_context(nc.psum_tensor(pre + name, shape, dt))
    return sb, ps


def load_vec(S, nc, sb, name, src_ap, ncol):
    t = sb(name, [128, ncol], F32)
    S.dma(name, lambda e: [e.dma_start(out=t[:], in_=src_ap.rearrange("(c p) -> p c", p=128),
                                       allow_slow_non_contiguous=True)], writes=[name])
    return t


def pass1(nc, S, T, dr, C):
    NT = T // 512
    with contextlib.ExitStack() as st:
        sb, ps = _mk(nc, st)
        winb = sb("winb", [128, 8, 2560], BF16)
        g1t = load_vec(S, nc, sb, "g1t", dr["norm1_g"], 8)
        wst = [sb("wst%d" % i, [128, 2560], F32) for i in range(2)]
        for c in range(8):
            sid = "wst%d" % (c % 2)
            stg = wst[c % 2]
            S.dma(sid, lambda e, stg=stg, c=c: [e.dma_start(out=stg[:], in_=dr["w_in"][c * 128:(c + 1) * 128, :])],
                  writes=[sid])
            if c % 2 == 0:
                S.op("dve", lambda e, stg=stg, c=c: e.tensor_scalar(out=winb[:, c, :], in0=stg[:], scalar1=g1t[:, c:c + 1],
                                                                    scalar2=None, op0=ALU.mult),
                     reads=[sid, "g1t"], writes=["winb%d" % c])
            else:
                S.op("act", lambda e, stg=stg, c=c: e.activation(out=winb[:, c, :], in_=stg[:], func=AF.Identity,
                                                                 scale=g1t[:, c:c + 1]),
                     reads=[sid, "g1t"], writes=["winb%d" % c])
        gq = sb("gq", [128, 2], F32)
        S.dma("gq", lambda e: [
            e.dma_start(out=gq[0:64, 0:1], in_=dr["q_norm_g"].rearrange("(p o) -> p o", o=1)),
            e.dma_start(out=gq[64:128, 0:1], in_=dr["q_norm_g"].rearrange("(p o) -> p o", o=1)),
            e.dma_start(out=gq[0:64, 1:2], in_=dr["k_norm_g"].rearrange("(p o) -> p o", o=1)),
            e.dma_start(out=gq[64:128, 1:2], in_=dr["k_norm_g"].rearrange("(p o) -> p o", o=1)),
        ], writes=["gq0"], n=4)
        gqs = sb("gqs", [128, 2], F32)
        S.op("dve", lambda e: e.tensor_scalar(out=gqs[:, 0:1], in0=gq[:, 0:1], scalar1=0.125, scalar2=None, op0=ALU.mult),
             reads=["gq0"], writes=["gqs_a"])
        S.op("dve", lambda e: e.tensor_copy(out=gqs[:, 1:2], in_=gq[:, 1:2]), reads=["gq0"], writes=["gqs_b"])
        zt = sb("zt", [128, 4, 32], BF16)
        S.op("pool", lambda e: e.memset(zt[:], 0.0), writes=["zt"])
        S.dma("zt", lambda e: [e.dma_start(out=dr["UT"][:, 0:32].rearrange("(c p) t -> p c t", p=128), in_=zt[:])],
              reads=["zt"], store=True)

        xr = [sb("x%d" % i, [128, 4, 1024], F32) for i in range(3)]
        junk = sb("junk", [128, 1024], BF16)
        ssr = [sb("ss%d" % i, [128, 4], F32) for i in range(2)]
        lnr = [sb("ln%d" % i, [128, 4], F32) for i in range(2)]
        rsr = [sb("rs%d" % i, [128, 4], F32) for i in range(2)]
        xsr = [sb("xs%d" % i, [128, 1024], BF16) for i in range(2)]
        xTr = [sb("xT%d" % i, [128, 8, 512], BF16) for i in range(2)]
        uTr = [sb("uT%d" % i, [128, 4, 512], BF16) for i in range(2)]
        qkr = [sb("qk%d" % i, [128, 8, 512], BF16) for i in range(2)]
        var = [sb("va%d" % i, [128, 4, 8, 128], BF16) for i in range(2)]
        for i in range(2):
            S.op("pool", lambda e, i=i: e.memset(var[i][:, :, :, 64:128], 1.0), writes=["va_ones%d" % i])
        er = [sb("e%d" % i, [128, 512], F32) for i in range(2)]
        dr_ = [sb("d%d" % i, [128, 512], F32) for i in range(2)]
        sqr = [sb("sq%d" % i, [128, 512], BF16) for i in range(2)]
        lqr = [sb("lq%d" % i, [128, 512], F32) for i in range(2)]
        rqr = [sb("rq%d" % i, [128, 512], F32) for i in range(2)]
        PT = [ps("pT%d" % i, [128, 1024], BF16) for i in range(2)]
        PM = [ps("pM%d" % i, [128, 512], F32) for i in range(4)]
        PS_ = [ps("pS%d" % i, [128, 512], F32) for i in range(2)]
        cnt = {"T": 0, "M": 0, "S": 0, "e": 0, "q": 0}

        def nxt(k, n):
            v = cnt[k] % n
            cnt[k] += 1
            return v

        WINB = ["winb%d" % c for c in range(8)]

        def load_x(i):
            xt = xr[i % 3]
            S.dma("x%d" % (i % 3), lambda e: [e.dma_start(
                out=xt[:], in_=dr["x"][i * 512:(i + 1) * 512, :].rearrange("(s p) d -> p s d", p=128))],
                writes=["x%d" % (i % 3)])

        def front(i):
            b = i % 2
            xt, ss, ln, rs, xT = xr[i % 3], ssr[b], lnr[b], rsr[b], xTr[b]
            xid = "x%d" % (i % 3)
            for s in range(4):
                S.op("act", lambda e, s=s: e.activation(out=junk[:], in_=xt[:, s, :], func=AF.Square,
                                                        accum_out=ss[:, s:s + 1]),
                     reads=[xid], writes=["junk", "ss%d_%d" % (b, s)])
            S.op("act", lambda e: e.activation(out=ln[:], in_=ss[:], func=AF.Ln, scale=1.0 / 1024, bias=C.eps[:, 0:1]),
                 reads=["ss%d_%d" % (b, s) for s in range(4)], writes=["ln%d" % b])
            S.op("act", lambda e: e.activation(out=rs[:], in_=ln[:], func=AF.Exp, scale=-0.5),
                 reads=["ln%d" % b], writes=["rs%d" % b])
            for s in range(4):
                xs = xsr[s % 2]
                xsid = "xs%d" % (s % 2)
                S.op("dve", lambda e, s=s, xs=xs: e.tensor_scalar(out=xs[:], in0=xt[:, s, :], scalar1=rs[:, s:s + 1],
                                                                 scalar2=None, op0=ALU.mult),
                     reads=[xid, "rs%d" % b], writes=[xsid])
                tb = nxt("T", 2)
                pt = PT[tb]

                def tr(e, xs=xs, pt=pt):
                    r = None
                    for c in range(8):
                        r = e.transpose(pt[:, c * 128:(c + 1) * 128], xs[:, c * 128:(c + 1) * 128], C.ident[:])
                    return r
                S.op("pe", tr, reads=[xsid, "ident"], writes=["pT%d" % tb])
                S.op("act", lambda e, s=s, pt=pt: e.activation(
                    out=xT[:, :, s * 128:(s + 1) * 128], in_=pt[:].rearrange("p (c t) -> p c t", c=8), func=AF.Copy),
                    reads=["pT%d" % tb], writes=["xT%d_%d" % (b, s)])
        def tile(i):
            if i + 2 < NT:
                load_x(i + 2)
            b = i % 2
            xT, uT, qk, va = xTr[b], uTr[b], qkr[b], var[b]
            xTids = ["xT%d_%d" % (b, s) for s in range(4)]

            def proj(fc, pm):
                def f(e):
                    r = None
                    for c in range(8):
                        r = e.matmul(pm[:], lhsT=winb[:, c, fc * 128:(fc + 1) * 128], rhs=xT[:, c, :],
                                     start=(c == 0), stop=(c == 7))
                    return r
                return f
            for c4 in range(4):
                mv = nxt("M", 4)
                mg = nxt("M", 4)
                S.op("pe", proj(c4, PM[mv]), reads=xTids + WINB, writes=["pM%d" % mv])
                S.op("pe", proj(4 + c4, PM[mg]), reads=xTids + WINB, writes=["pM%d" % mg])
                eb = nxt("e", 2)
                et, dt_ = er[eb], dr_[eb]
                S.op("act", lambda e, mg=mg, et=et: e.activation(out=et[:], in_=PM[mg][:], func=AF.Exp, scale=-1.0),
                     reads=["pM%d" % mg], writes=["e%d" % eb])
                S.op("act", lambda e, et=et, dt_=dt_: e.activation(out=dt_[:], in_=et[:], func=AF.Ln, bias=C.one[:, 0:1]),
                     reads=["e%d" % eb, "one"], writes=["d%d" % eb])
                S.op("act", lambda e, et=et, dt_=dt_: e.activation(out=et[:], in_=dt_[:], func=AF.Exp, scale=-1.0),
                     reads=["d%d" % eb], writes=["e%d" % eb])
                S.op("dve", lambda e, mv=mv, et=et, c4=c4: e.tensor_tensor(out=uT[:, c4, :], in0=PM[mv][:], in1=et[:],
                                                                         op=ALU.mult),
                     reads=["pM%d" % mv, "e%d" % eb], writes=["uT%d_%d" % (b, c4)])
            S.dma("uT%d" % b, lambda e: [e.dma_start(
                out=dr["UT"][:, 32 + i * 512:32 + (i + 1) * 512].rearrange("(c p) t -> p c t", p=128), in_=uT[:])],
                reads=["uT%d_%d" % (b, c4) for c4 in range(4)], store=True)
            if i + 1 < NT:
                front(i + 1)
            pend = []

            def qk_front(c8):
                m = nxt("M", 4)
                S.op("pe", proj(8 + c8, PM[m]), reads=xTids + WINB, writes=["pM%d" % m])
                qb = nxt("q", 2)
                sq = sqr[qb]
                S.op("act", lambda e, m=m, sq=sq: e.activation(out=sq[:], in_=PM[m][:], func=AF.Square),
                     reads=["pM%d" % m], writes=["sq%d" % qb])
                pend.append((c8, m, qb))

            def qk_back():
                (c8, m, qb) = pend.pop(0)
                sq, lq, rq = sqr[qb], lqr[qb], rqr[qb]
                sbk = nxt("S", 2)
                S.op("pe", lambda e, sq=sq, sbk=sbk: e.matmul(PS_[sbk][:], lhsT=C.blk[:], rhs=sq[:], start=True, stop=True),
                     reads=["sq%d" % qb, "blk"], writes=["pS%d" % sbk])
                S.op("act", lambda e, lq=lq, sbk=sbk: e.activation(out=lq[:], in_=PS_[sbk][:], func=AF.Ln, scale=1.0 / 64,
                                                                  bias=C.eps[:, 0:1]),
                     reads=["pS%d" % sbk], writes=["lq%d" % qb])
                S.op("act", lambda e, lq=lq, rq=rq: e.activation(out=rq[:], in_=lq[:], func=AF.Exp, scale=-0.5),
                     reads=["lq%d" % qb], writes=["rq%d" % qb])
                gcol = 0 if c8 < 4 else 1
                S.op("dve", lambda e, m=m, rq=rq, c8=c8, gcol=gcol: e.scalar_tensor_tensor(
                    out=qk[:, c8, :], in0=PM[m][:], scalar=gqs[:, gcol:gcol + 1], in1=rq[:], op0=ALU.mult, op1=ALU.mult),
                    reads=["pM%d" % m, "rq%d" % qb, "gqs_a", "gqs_b"], writes=["qk%d_%d" % (b, c8)])

            for c8 in range(8):
                qk_front(c8)
                if c8 > 0:
                    qk_back()
            qk_back()
            S.dma("qk%d" % b, lambda e: [
                e.dma_start(out=dr["QT"][:, i * 512:(i + 1) * 512].rearrange("(c p) t -> p c t", p=128), in_=qk[:, 0:4, :]),
                e.dma_start(out=dr["KT"][:, i * 512:(i + 1) * 512].rearrange("(c p) t -> p c t", p=128), in_=qk[:, 4:8, :]),
            ], reads=["qk%d_%d" % (b, c8) for c8 in range(8)], n=2, store=True)
            for s in range(4):
                m = nxt("M", 4)

                def vproj(e, s=s, m=m):
                    r = None
                    for c in range(8):
                        r = e.matmul(PM[m][:], lhsT=xT[:, c, s * 128:(s + 1) * 128], rhs=winb[:, c, 2048:2560],
                                     start=(c == 0), stop=(c == 7))
                    return r
                S.op("pe", vproj, reads=xTids + WINB, writes=["pM%d" % m])
                S.op("act", lambda e, s=s, m=m: e.activation(out=va[:, s, :, 0:64],
                                                             in_=PM[m][:].rearrange("p (h e) -> p h e", h=8), func=AF.Copy),
                     reads=["pM%d" % m, "va_ones%d" % b], writes=["va%d_%d" % (b, s)])
            S.dma("va%d" % b, lambda e: [e.dma_start(
                out=dr["VA"][i * 512:(i + 1) * 512, :, :].rearrange("(s p) h e -> p s h e", p=128), in_=va[:])],
                reads=["va%d_%d" % (b, s) for s in range(4)], store=True)
        load_x(0)
        if NT > 1:
            load_x(1)
        front(0)
        for i in range(NT):
            tile(i)
        S.dram_barrier()
        S.emit()
    nc.all_engine_barrier()


def pass2a(nc, S, T, dr, C):
    NT = T // 512
    with contextlib.ExitStack() as st:
        sb, ps = _mk(nc, st)
        cw = sb("cw", [128, 4, 31], F32)
        S.dma("cw", lambda e: [e.dma_start(out=cw[:, c, :], in_=dr["conv_w"][:, c * 128:(c + 1) * 128].rearrange("k p -> p k"),
                                           allow_slow_non_contiguous=True) for c in range(4)], writes=["cw"], n=4)
        cb = load_vec(S, nc, sb, "cb", dr["conv_b"], 4)
        cg = load_vec(S, nc, sb, "cg", dr["cn_g"], 4)
        cnb = load_vec(S, nc, sb, "cnb", dr["cn_b"], 4)
        ncg = sb("ncg", [128, 4], F32)
        ncb = sb("ncb", [128, 4], F32)
        S.op("dve", lambda e: e.tensor_scalar(out=ncg[:], in0=cg[:], scalar1=-1.0, scalar2=None, op0=ALU.mult),
             reads=["cg"], writes=["ncg"])
        S.op("dve", lambda e: e.tensor_scalar(out=ncb[:], in0=cnb[:], scalar1=-1.0, scalar2=None, op0=ALU.mult),
             reads=["cnb"], writes=["ncb"])
        diag = sb("diag", [128, 124, 128], BF16)
        for c in range(4):
            for k in range(31):
                if k % 2 == 0:
                    S.op("dve", lambda e, c=c, k=k: e.tensor_scalar(out=diag[:, c * 31 + k, :], in0=C.ident[:],
                                                                    scalar1=cw[:, c, k:k + 1], scalar2=None, op0=ALU.mult),
                         reads=["cw", "ident"], writes=["diag_%d_%d" % (c, k)])
                else:
                    S.op("act", lambda e, c=c, k=k: e.activation(out=diag[:, c * 31 + k, :], in_=C.ident[:],
                                                                 func=AF.Identity, scale=cw[:, c, k:k + 1]),
                         reads=["cw", "ident"], writes=["diag_%d_%d" % (c, k)])
        diag_ids = ["diag_%d_%d" % (c, k) for c in range(4) for k in range(31)]
        ur = [sb("U%d" % i, [128, 4, 544], BF16) for i in range(2)]
        y32r = [sb("y32_%d" % i, [128, 4, 512], F32) for i in range(2)]
        ybr = [sb("yb_%d" % i, [128, 4, 512], BF16) for i in range(2)]
        ysr = [sb("ys_%d" % i, [128, 4, 512], BF16) for i in range(2)]
        aTr = [sb("aT_%d" % i, [128, 4, 512], BF16) for i in range(2)]
        mean = sb("mean", [128, 512], F32)
        msq = sb("msq", [128, 512], F32)
        varr = sb("var", [128, 512], F32)
        lv = sb("lv", [128, 512], F32)
        rstd = sb("rstd", [128, 512], F32)
        er = [sb("ce%d" % i, [128, 512], F32) for i in range(2)]
        dd = [sb("cd%d" % i, [128, 512], F32) for i in range(2)]
        PM = [ps("pM%d" % i, [128, 512], F32) for i in range(4)]
        PS1 = ps("pS1", [128, 512], F32)
        PS2 = ps("pS2", [128, 512], F32)
        cnt = {"M": 0, "e": 0}

        def nxt(k, n):
            v = cnt[k] % n
            cnt[k] += 1
            return v

        def load_u(i):
            U = ur[i % 2]
            S.dma("U%d" % (i % 2), lambda e: [e.dma_start(
                out=U[:, :, 0:542], in_=dr["UT"][:, 2 + i * 512:2 + i * 512 + 542].rearrange("(c p) t -> p c t", p=128))],
                writes=["U%d" % (i % 2)])

        def tile(i):
            if i + 1 < NT:
                load_u(i + 1)
            b = i % 2
            U, y32, yb, ys, aT = ur[b], y32r[b], ybr[b], ysr[b], aTr[b]
            for c in range(4):
                m = nxt("M", 4)

                def conv(e, c=c, m=m):
                    r = None
                    for k in range(31):
                        r = e.matmul(PM[m][:], lhsT=diag[:, c * 31 + k, :], rhs=U[:, c, k:k + 512],
                                     start=(k == 0), stop=(k == 30))
                    return r
                S.op("pe", conv, reads=["U%d" % b] + diag_ids[c * 31:(c + 1) * 31], writes=["pM%d" % m])
                S.op("act", lambda e, c=c, m=m: e.activation(out=y32[:, c, :], in_=PM[m][:], func=AF.Identity,
                                                             bias=cb[:, c:c + 1]),
                     reads=["pM%d" % m, "cb"], writes=["y32_%d_%d" % (b, c)])
                S.op("act", lambda e, c=c, m=m: e.activation(out=yb[:, c, :], in_=PM[m][:], func=AF.Identity,
                                                             bias=cb[:, c:c + 1]),
                     reads=["pM%d" % m, "cb"], writes=["yb_%d_%d" % (b, c)])
                S.op("act", lambda e, c=c, m=m: e.activation(out=ys[:, c, :], in_=PM[m][:], func=AF.Square,
                                                             bias=cb[:, c:c + 1]),
                     reads=["pM%d" % m, "cb"], writes=["ys_%d_%d" % (b, c)])

            def st1(e):
                r = None
                for c in range(4):
                    r = e.matmul(PS1[:], lhsT=C.ones[:], rhs=yb[:, c, :], start=(c == 0), stop=(c == 3))
                return r

            def st2(e):
                r = None
                for c in range(4):
                    r = e.matmul(PS2[:], lhsT=C.ones[:], rhs=ys[:, c, :], start=(c == 0), stop=(c == 3))
                return r
            S.op("pe", st1, reads=["yb_%d_%d" % (b, c) for c in range(4)] + ["ones"], writes=["pS1"])
            S.op("pe", st2, reads=["ys_%d_%d" % (b, c) for c in range(4)] + ["ones"], writes=["pS2"])
            S.op("dve", lambda e: e.tensor_scalar(out=mean[:], in0=PS1[:], scalar1=1.0 / 512, scalar2=None, op0=ALU.mult),
                 reads=["pS1"], writes=["mean"])
            S.op("dve", lambda e: e.tensor_tensor(out=msq[:], in0=mean[:], in1=mean[:], op=ALU.mult),
                 reads=["mean"], writes=["msq"])
            S.op("dve", lambda e: e.scalar_tensor_tensor(out=varr[:], in0=PS2[:], scalar=1.0 / 512, in1=msq[:],
                                                         op0=ALU.mult, op1=ALU.subtract),
                 reads=["pS2", "msq"], writes=["var"])
            S.op("act", lambda e: e.activation(out=lv[:], in_=varr[:], func=AF.Ln, bias=C.eps[:, 0:1]),
                 reads=["var", "eps"], writes=["lv"])
            S.op("act", lambda e: e.activation(out=rstd[:], in_=lv[:], func=AF.Exp, scale=-0.5),
                 reads=["lv"], writes=["rstd"])
            for c in range(4):
                yid = "y32_%d_%d" % (b, c)
                eb = nxt("e", 2)
                et, dt_ = er[eb], dd[eb]
                S.op("dve", lambda e, c=c: e.tensor_tensor(out=y32[:, c, :], in0=y32[:, c, :], in1=mean[:], op=ALU.subtract),
                     reads=[yid, "mean"], writes=[yid])
                S.op("dve", lambda e, c=c: e.tensor_tensor(out=y32[:, c, :], in0=y32[:, c, :], in1=rstd[:], op=ALU.mult),
                     reads=[yid, "rstd"], writes=[yid])
                S.op("act", lambda e, c=c, et=et: e.activation(out=et[:], in_=y32[:, c, :], func=AF.Exp,
                                                               scale=ncg[:, c:c + 1], bias=ncb[:, c:c + 1]),
                     reads=[yid, "ncg", "ncb"], writes=["ce%d" % eb])
                S.op("dve", lambda e, c=c: e.tensor_scalar(out=y32[:, c, :], in0=y32[:, c, :], scalar1=cg[:, c:c + 1],
                                                           scalar2=cnb[:, c:c + 1], op0=ALU.mult, op1=ALU.add),
                     reads=[yid, "cg", "cnb", "ce%d" % eb], writes=[yid])
                S.op("act", lambda e, et=et, dt_=dt_: e.activation(out=dt_[:], in_=et[:], func=AF.Ln, bias=C.one[:, 0:1]),
                     reads=["ce%d" % eb, "one"], writes=["cd%d" % eb])
                S.op("act", lambda e, et=et, dt_=dt_: e.activation(out=et[:], in_=dt_[:], func=AF.Exp, scale=-1.0),
                     reads=["cd%d" % eb], writes=["ce%d" % eb])
                S.op("dve", lambda e, c=c, et=et: e.tensor_tensor(out=aT[:, c, :], in0=y32[:, c, :], in1=et[:], op=ALU.mult),
                     reads=[yid, "ce%d" % eb], writes=["aT_%d_%d" % (b, c)])
            S.dma("aT%d" % b, lambda e: [e.dma_start(
                out=dr["MIXT"][0:512, i * 512:(i + 1) * 512].rearrange("(c p) t -> p c t", p=128), in_=aT[:])],
                reads=["aT_%d_%d" % (b, c) for c in range(4)], store=True)
        load_u(0)
        for i in range(NT):
            tile(i)
        S.dram_barrier()
        S.emit()
    nc.all_engine_barrier()


def pass2c(nc, S, T, dr, C):
    NT = T // 512
    with contextlib.ExitStack() as st:
        sb, ps = _mk(nc, st)
        woutb = sb("woutb", [128, 8, 1024], BF16)
        wst = [sb("wost%d" % i, [128, 1024], F32) for i in range(2)]
        for c in range(8):
            sid = "wost%d" % (c % 2)
            stg = wst[c % 2]
            S.dma(sid, lambda e, stg=stg, c=c: [e.dma_start(out=stg[:], in_=dr["w_out"][c * 128:(c + 1) * 128, :])],
                  writes=[sid])
            if c % 2 == 0:
                S.op("dve", lambda e, stg=stg, c=c: e.tensor_copy(out=woutb[:, c, :], in_=stg[:]),
                     reads=[sid], writes=["woutb%d" % c])
            else:
                S.op("act", lambda e, stg=stg, c=c: e.activation(out=woutb[:, c, :], in_=stg[:], func=AF.Copy),
                     reads=[sid], writes=["woutb%d" % c])
        mr = [sb("mix%d" % i, [128, 8, 512], BF16) for i in range(2)]
        xr = [sb("x%d" % i, [128, 4, 1024], F32) for i in range(2)]
        x1r = [sb("x1_%d" % i, [128, 4, 1024], F32) for i in range(2)]
        x1Tr = [sb("x1T%d" % i, [128, 8, 512], BF16) for i in range(2)]
        junk = sb("junk", [128, 1024], BF16)
        ssr = [sb("ss%d" % i, [128, 4], F32) for i in range(2)]
        lnr = [sb("ln%d" % i, [128, 4], F32) for i in range(2)]
        rsr = [sb("rs%d" % i, [128, 4], F32) for i in range(2)]
        xsr = [sb("xs%d" % i, [128, 1024], BF16) for i in range(2)]
        PT = [ps("pT%d" % i, [128, 1024], BF16) for i in range(2)]
        PM = [ps("pM%d" % i, [128, 512], F32) for i in range(4)]
        cnt = {"M": 0, "T": 0}

        def nxt(k, n):
            v = cnt[k] % n
            cnt[k] += 1
            return v

        def load(i):
            b = i % 2
            mix, xt = mr[b], xr[b]
            S.dma("mix%d" % b, lambda e: [e.dma_start(
                out=mix[:], in_=dr["MIXT"][:, i * 512:(i + 1) * 512].rearrange("(c p) t -> p c t", p=128))],
                writes=["mix%d" % b])
            S.dma("x%d" % b, lambda e: [e.dma_start(
                out=xt[:], in_=dr["x"][i * 512:(i + 1) * 512, :].rearrange("(s p) d -> p s d", p=128))],
                writes=["x%d" % b])

        def tile(i):
            if i + 1 < NT:
                load(i + 1)
            b = i % 2
            mix, xt, x1, x1T, ss, ln, rs = mr[b], xr[b], x1r[b], x1Tr[b], ssr[b], lnr[b], rsr[b]
            for s in range(4):
                for hh in range(2):
                    m = nxt("M", 4)

                    def mm(e, s=s, hh=hh, m=m):
                        r = None
                        for c in range(8):
                            r = e.matmul(PM[m][:], lhsT=mix[:, c, s * 128:(s + 1) * 128],
                                         rhs=woutb[:, c, hh * 512:(hh + 1) * 512], start=(c == 0), stop=(c == 7))
                        return r
                    S.op("pe", mm, reads=["mix%d" % b] + ["woutb%d" % c for c in range(8)], writes=["pM%d" % m])
                    S.op("dve", lambda e, s=s, hh=hh, m=m: e.tensor_tensor(
                        out=x1[:, s, hh * 512:(hh + 1) * 512], in0=PM[m][:], in1=xt[:, s, hh * 512:(hh + 1) * 512], op=ALU.add),
                        reads=["pM%d" % m, "x%d" % b], writes=["x1_%d_%d_%d" % (b, s, hh)])
                S.op("act", lambda e, s=s: e.activation(out=junk[:], in_=x1[:, s, :], func=AF.Square,
                                                        accum_out=ss[:, s:s + 1]),
                     reads=["x1_%d_%d_0" % (b, s), "x1_%d_%d_1" % (b, s)], writes=["junk", "ss%d_%d" % (b, s)])
            x1ids = ["x1_%d_%d_%d" % (b, s, hh) for s in range(4) for hh in range(2)]
            S.dma("x1_%d" % b, lambda e: [e.dma_start(
                out=dr["X1"][i * 512:(i + 1) * 512, :].rearrange("(s p) d -> p s d", p=128), in_=x1[:])],
                reads=x1ids, store=True)
            S.op("act", lambda e: e.activation(out=ln[:], in_=ss[:], func=AF.Ln, scale=1.0 / 1024, bias=C.eps[:, 0:1]),
                 reads=["ss%d_%d" % (b, s) for s in range(4)] + ["eps"], writes=["ln%d" % b])
            S.op("act", lambda e: e.activation(out=rs[:], in_=ln[:], func=AF.Exp, scale=-0.5),
                 reads=["ln%d" % b], writes=["rs%d" % b])
            for s in range(4):
                xs = xsr[s % 2]
                xsid = "xs%d" % (s % 2)
                S.op("dve", lambda e, s=s, xs=xs: e.tensor_scalar(out=xs[:], in0=x1[:, s, :], scalar1=rs[:, s:s + 1],
                                                                 scalar2=None, op0=ALU.mult),
                     reads=["x1_%d_%d_0" % (b, s), "x1_%d_%d_1" % (b, s), "rs%d" % b], writes=[xsid])
                tb = nxt("T", 2)
                pt = PT[tb]

                def tr(e, xs=xs, pt=pt):
                    r = None
                    for c in range(8):
                        r = e.transpose(pt[:, c * 128:(c + 1) * 128], xs[:, c * 128:(c + 1) * 128], C.ident[:])
                    return r
                S.op("pe", tr, reads=[xsid, "ident"], writes=["pT%d" % tb])
                S.op("act", lambda e, s=s, pt=pt: e.activation(
                    out=x1T[:, :, s * 128:(s + 1) * 128], in_=pt[:].rearrange("p (c t) -> p c t", c=8), func=AF.Copy),
                    reads=["pT%d" % tb], writes=["x1T%d_%d" % (b, s)])
            S.dma("x1T%d" % b, lambda e: [e.dma_start(
                out=dr["X1T"][:, i * 512:(i + 1) * 512].rearrange("(c p) t -> p c t", p=128), in_=x1T[:])],
                reads=["x1T%d_%d" % (b, s) for s in range(4)], store=True)
        load(0)
        for i in range(NT):
            tile(i)
        S.dram_barrier()
        S.emit()
    nc.all_engine_barrier()


def pass3(nc, S, T, dr, C, hf):
    NT = T // 512
    NP = 11
    with contextlib.ExitStack() as st:
        sb, ps = _mk(nc, st)
        g2t = load_vec(S, nc, sb, "g2t%d" % hf, dr["norm2_g"], 8)
        ffb = load_vec(S, nc, sb, "ffb%d" % hf, dr["ffconv_b"], 44)
        ffw = sb("ffw", [128, 44, 3], F32)
        S.dma("ffw%d" % hf, lambda e: [e.dma_start(out=ffw[:, :, k], in_=dr["ffconv_w"][k, :].rearrange("(j p) -> p j", p=128),
                                                   allow_slow_non_contiguous=True) for k in range(3)], writes=["ffw"], n=3)
        wupb = sb("wupb", [128, 8, 2816], BF16)
        wdb = sb("wdb", [128, NP, 1024], BF16)
        ust = [sb("ust%d" % i, [128, 2816], F32) for i in range(2)]
        dst = [sb("dst%d" % i, [128, 1024], F32) for i in range(2)]
        g2id = "g2t%d" % hf
        for c in range(8):
            sid = "ust%d" % (c % 2)
            stg = ust[c % 2]
            S.dma(sid, lambda e, stg=stg, c=c: [
                e.dma_start(out=stg[:, 0:1408], in_=dr["w_up"][c * 128:(c + 1) * 128, hf * 1408:(hf + 1) * 1408]),
                e.dma_start(out=stg[:, 1408:2816], in_=dr["w_up"][c * 128:(c + 1) * 128, 2816 + hf * 1408:2816 + (hf + 1) * 1408]),
            ], writes=[sid], n=2)
            if c % 2 == 0:
                S.op("dve", lambda e, stg=stg, c=c: e.tensor_scalar(out=wupb[:, c, :], in0=stg[:], scalar1=g2t[:, c:c + 1],
                                                                    scalar2=None, op0=ALU.mult),
                     reads=[sid, g2id], writes=["wupb%d" % c])
            else:
                S.op("act", lambda e, stg=stg, c=c: e.activation(out=wupb[:, c, :], in_=stg[:], func=AF.Identity,
                                                                 scale=g2t[:, c:c + 1]),
                     reads=[sid, g2id], writes=["wupb%d" % c])
        for j in range(NP):
            sid = "dst%d" % (j % 2)
            stg = dst[j % 2]
            r0 = (hf * NP + j) * 128
            S.dma(sid, lambda e, stg=stg, r0=r0: [e.dma_start(out=stg[:], in_=dr["w_down"][r0:r0 + 128, :])], writes=[sid])
            if j % 2 == 0:
                S.op("act", lambda e, stg=stg, j=j: e.activation(out=wdb[:, j, :], in_=stg[:], func=AF.Identity, scale=0.5),
                     reads=[sid], writes=["wdb%d" % j])
            else:
                S.op("dve", lambda e, stg=stg, j=j: e.tensor_scalar(out=wdb[:, j, :], in0=stg[:], scalar1=0.5, scalar2=None,
                                                                    op0=ALU.mult),
                     reads=[sid], writes=["wdb%d" % j])
        hal = sb("hal", [128, 22, 2], BF16)
        S.op("pool", lambda e: e.memset(hal[:], 0.0), writes=["hal%d" % k for k in range(22)])
        xTr = [sb("xT%d" % i, [128, 8, 512], BF16) for i in range(2)]
        pr = [sb("prev%d" % i, [128, 4, 1024], F32) for i in range(2)]
        Hr = [sb("H%d" % i, [128, NP, 512], BF16) for i in range(2)]
        Ur = [sb("Ub%d" % i, [128, 514], BF16) for i in range(4)]
        A2r = [sb("A2_%d" % i, [128, 512], F32) for i in range(4)]
        tbr = [sb("tb%d" % i, [128, 512], F32) for i in range(2)]
        PM = [ps("pM%d" % i, [128, 512], F32) for i in range(4)]
        PD = [ps("pD%d" % i, [128, 512], F32) for i in range(2)]
        cnt = {"M": 0, "D": 0, "U": 0, "t": 0}
        prev_src = dr["X1"] if hf == 0 else dr["out"]

        def nxt(k, n):
            v = cnt[k] % n
            cnt[k] += 1
            return v

        def load_xT(i):
            b = i % 2
            xT = xTr[b]
            S.dma("xT%d" % b, lambda e: [e.dma_start(
                out=xT[:], in_=dr["X1T"][:, i * 512:(i + 1) * 512].rearrange("(c p) t -> p c t", p=128))],
                writes=["xT%d" % b])

        def load_prev(i):
            b = i % 2
            pv = pr[b]
            S.dma("prev%d" % b, lambda e: [e.dma_start(
                out=pv[:], in_=prev_src[i * 512:(i + 1) * 512, :].rearrange("(s p) d -> p s d", p=128))],
                writes=["prev%d_%d_%d" % (b, s, hh) for s in range(4) for hh in range(2)])

        def down_group(i, gi):
            b = i % 2
            pv, H = pr[b], Hr[b]
            s, hh = gi // 2, gi % 2
            Hids = ["H%d_%d" % (b, jj) for jj in range(NP)]
            d = nxt("D", 2)

            def down(e):
                r = None
                for jj in range(NP):
                    r = e.matmul(PD[d][:], lhsT=H[:, jj, s * 128:(s + 1) * 128],
                                 rhs=wdb[:, jj, hh * 512:(hh + 1) * 512], start=(jj == 0), stop=(jj == NP - 1))
                return r
            S.op("pe", down, reads=Hids + ["wdb%d" % j for j in range(NP)], writes=["pD%d" % d])
            pid = "prev%d_%d_%d" % (b, s, hh)
            S.op("dve", lambda e: e.tensor_tensor(
                out=pv[:, s, hh * 512:(hh + 1) * 512], in0=PD[d][:], in1=pv[:, s, hh * 512:(hh + 1) * 512], op=ALU.add),
                reads=["pD%d" % d, pid], writes=[pid])

        def store(i):
            b = i % 2
            pv = pr[b]
            S.dma("prev%d" % b, lambda e: [e.dma_start(
                out=dr["out"][i * 512:(i + 1) * 512, :].rearrange("(s p) d -> p s d", p=128), in_=pv[:])],
                reads=["prev%d_%d_%d" % (b, s, hh) for s in range(4) for hh in range(2)], store=True)

        def tile(i):
            if i + 1 < NT:
                load_xT(i + 1)
            if i == 0 and NT > 1:
                load_prev(1)
            b = i % 2
            xT, H = xTr[b], Hr[b]

            def branch(jj, isval):
                col0 = jj * 128 + (1408 if isval else 0)
                J = hf * NP + jj + (22 if isval else 0)
                hidx = jj + (11 if isval else 0)
                m = nxt("M", 4)
                u = nxt("U", 4)
                U, A2 = Ur[u], A2r[u]

                def up(e):
                    r = None
                    for c in range(8):
                        r = e.matmul(PM[m][:], lhsT=wupb[:, c, col0:col0 + 128], rhs=xT[:, c, :],
                                     start=(c == 0), stop=(c == 7))
                    return r
                S.op("pe", up, reads=["xT%d" % b] + ["wupb%d" % c for c in range(8)], writes=["pM%d" % m])
                S.op("pool", lambda e: e.tensor_copy(out=U[:, 0:2], in_=hal[:, hidx, :]),
                     reads=["hal%d" % hidx], writes=["Ub%d_h" % u])
                S.op("act", lambda e: e.activation(out=U[:, 2:514], in_=PM[m][:], func=AF.Copy),
                     reads=["pM%d" % m], writes=["Ub%d_b" % u])
                S.op("act", lambda e: e.activation(out=A2[:], in_=PM[m][:], func=AF.Identity,
                                                   scale=ffw[:, J, 2:3], bias=ffb[:, J:J + 1]),
                     reads=["pM%d" % m, "ffw", "ffb%d" % hf], writes=["A2_%d" % u])
                S.op("pool", lambda e: e.tensor_copy(out=hal[:, hidx, :], in_=U[:, 512:514]),
                     reads=["Ub%d_b" % u], writes=["hal%d" % hidx])
                S.op("dve", lambda e: e.scalar_tensor_tensor(out=A2[:], in0=U[:, 1:513], scalar=ffw[:, J, 1:2], in1=A2[:],
                                                             op0=ALU.mult, op1=ALU.add),
                     reads=["Ub%d_h" % u, "Ub%d_b" % u, "A2_%d" % u, "ffw"], writes=["A2_%d" % u])
                S.op("dve", lambda e: e.scalar_tensor_tensor(out=A2[:], in0=U[:, 0:512], scalar=ffw[:, J, 0:1], in1=A2[:],
                                                             op0=ALU.mult, op1=ALU.add),
                     reads=["Ub%d_h" % u, "Ub%d_b" % u, "A2_%d" % u, "ffw"], writes=["A2_%d" % u])
                return u

            for jj in range(NP):
                ug = branch(jj, False)
                uv = branch(jj, True)
                t = nxt("t", 2)
                tb = tbr[t]
                zg, zv = A2r[ug], A2r[uv]
                S.op("act", lambda e, tb=tb, zg=zg: e.activation(out=tb[:], in_=zg[:], func=AF.Tanh, scale=0.5),
                     reads=["A2_%d" % ug], writes=["tb%d" % t])
                S.op("dve", lambda e, tb=tb, zg=zg: e.scalar_tensor_tensor(out=tb[:], in0=tb[:], scalar=1.0, in1=zg[:],
                                                                          op0=ALU.add, op1=ALU.mult),
                     reads=["tb%d" % t, "A2_%d" % ug], writes=["tb%d" % t])
                S.op("dve", lambda e, tb=tb, zv=zv, jj=jj: e.tensor_tensor(out=H[:, jj, :], in0=tb[:], in1=zv[:], op=ALU.mult),
                     reads=["tb%d" % t, "A2_%d" % uv], writes=["H%d_%d" % (b, jj)])
                if i > 0 and jj < 8:
                    down_group(i - 1, jj)
                    if jj == 7:
                        store(i - 1)
                        if i + 1 < NT:
                            load_prev(i + 1)
        load_xT(0)
        load_prev(0)
        for i in range(NT):
            tile(i)
        for gi in range(8):
            down_group(NT - 1, gi)
        store(NT - 1)
        S.dram_barrier()
        S.emit()
    nc.all_engine_barrier()


def pass2b(nc, S, T, dr, C):
    with contextlib.ExitStack() as st:
        sb, ps = _mk(nc, st)
        mk = sb("mk", [128, 24, 512], BF16)
        mst = [sb("mst%d" % i, [128, 1024], F32) for i in range(2)]
        for g in range(12):
            sid = "mst%d" % (g % 2)
            stg = mst[g % 2]
            S.dma(sid, lambda e, stg=stg, g=g: [e.dma_start(out=stg[:], in_=dr["amask"][:, g * 1024:(g + 1) * 1024])],
                  writes=[sid])
            if g % 2 == 0:
                S.op("dve", lambda e, stg=stg, g=g: e.tensor_copy(out=mk[:, g * 2:(g + 1) * 2, :],
                                                                  in_=stg[:].rearrange("p (a b) -> p a b", a=2)),
                     reads=[sid], writes=["mk"])
            else:
                S.op("act", lambda e, stg=stg, g=g: e.activation(out=mk[:, g * 2:(g + 1) * 2, :],
                                                                 in_=stg[:].rearrange("p (a b) -> p a b", a=2), func=AF.Copy),
                     reads=[sid], writes=["mk"])
        qtr = [sb("QTt%d" % i, [128, T], BF16) for i in range(1)]
        ktr = [sb("KTt%d" % i, [128, T], BF16) for i in range(1)]
        ACC = [sb("ACC%d" % i, [128, T], F32) for i in range(2)]
        OTst = sb("OTst", [128, T], BF16)
        RC = 2048 if T >= 2048 else T
        Rt = [sb("Rt%d" % i, [128, RC], F32) for i in range(2)]
        NVA = 8
        VAr = [sb("VAt%d" % i, [128, 2, 128], BF16) for i in range(NVA)]
        Pr = [sb("P%d" % i, [128, 2, 512], BF16) for i in range(4)]
        PSs = [[ps("pSs%d_%d" % (hh, i), [128, 512], F32) for i in range(2)] for hh in range(2)]
        PO = [[ps("pO%d_%d" % (hh, i), [128, 512], F32) for i in range(2)] for hh in range(2)]
        cnt = {"S": 0, "E": 0, "V": 0, "G0": 0, "G1": 0, "R": 0}

        def nxt(k, n):
            v = cnt[k] % n
            cnt[k] += 1
            return v

        def load_qk(hp):
            b = 0
            S.dma("QTt%d" % b, lambda e: [e.dma_start(out=qtr[b][:], in_=dr["QT"][hp * 128:(hp + 1) * 128, :])],
                  writes=["QTt%d" % b])
            S.dma("KTt%d" % b, lambda e: [e.dma_start(out=ktr[b][:], in_=dr["KT"][hp * 128:(hp + 1) * 128, :])],
                  writes=["KTt%d" % b])

        def pair(hp):
            load_qk(hp)
            b = 0
            QTt, KTt = qtr[b], ktr[b]
            qid, kid = "QTt%d" % b, "KTt%d" % b
            items = []

            def accids(hh, lo, hi):
                return ["acc%d_%d" % (hh, bb) for bb in range(lo // 2048, (hi - 1) // 2048 + 1)]

            def stageA(item):
                (pi, d, nb, gs, r, grp_bank, fresh, j0) = item
                js = [j for j in (j0, j0 + 1) if j < nb]
                nqs = [256 if j + 1 < nb else 128 for j in js]
                ncols = 256 * (len(js) - 1) + nqs[-1]
                vts = []
                for j in js:
                    t_lo = 128 * j * d + r
                    v = nxt("V", NVA)
                    S.dma("VAt%d" % v, lambda e, v=v, t_lo=t_lo: [e.dma_start(
                        out=VAr[v][:], in_=dr["VA"][t_lo:t_lo + 127 * d + 1:d, 2 * hp:2 * hp + 2, :])],
                        writes=["VAt%d" % v])
                    vts.append(v)
                sbk = nxt("S", 2)
                eb = nxt("E", 4)
                P = Pr[eb]

                def smm(e):
                    rr = None
                    for bi, j in enumerate(js):
                        t_lo = 128 * j * d + r
                        for hh in range(2):
                            pb = hh * 64
                            rr = e.matmul(PSs[hh][sbk][:, bi * 256:bi * 256 + nqs[bi]],
                                          lhsT=KTt[pb:pb + 64, t_lo:t_lo + 127 * d + 1:d],
                                          rhs=QTt[pb:pb + 64, t_lo:t_lo + (nqs[bi] - 1) * d + 1:d],
                                          start=True, stop=False)
                        for hh in range(2):
                            m0 = (2 * hp + hh) * 3 + pi
                            rr = e.matmul(PSs[hh][sbk][:, bi * 256:bi * 256 + nqs[bi]],
                                          lhsT=C.ident[:], rhs=mk[:, m0, 0:nqs[bi]], start=False, stop=True)
                    return rr
                S.op("pe", smm, reads=[qid, kid, "mk", "ident"], writes=["pSs0_%d" % sbk, "pSs1_%d" % sbk])
                for hh in range(2):
                    S.op("act", lambda e, hh=hh: e.activation(
                        out=P[:, hh, 0:ncols], in_=PSs[hh][sbk][:, 0:ncols], func=AF.Exp),
                        reads=["pSs%d_%d" % (hh, sbk)], writes=["P%d_%d" % (eb, hh)])
                return (item, js, vts, eb)

            def stageB(ctx):
                (item, js, vts, eb) = ctx
                (pi, d, nb, gs, r, grp_bank, fresh, j0) = item
                P = Pr[eb]
                plan = []
                wr = []
                for hh in range(2):
                    for bi, j in enumerate(js):
                        parts = [(qb, half) for (qb, half) in ((j, 0), (j + 1, 1)) if qb < nb]
                        if len(parts) == 2 and parts[0][0] // gs == parts[1][0] // gs:
                            parts = [(j, 0, 2)]
                        else:
                            parts = [(qb, half, 1) for (qb, half) in parts]
                        for (qb, half, w2) in parts:
                            g = qb // gs
                            key = (hh, g)
                            if key not in grp_bank:
                                grp_bank[key] = nxt("G%d" % hh, 2)
                                fresh[key] = True
                            bank = grp_bank[key]
                            plan.append((hh, bi, half, bank, (qb % gs) * 128, fresh[key], w2))
                            fresh[key] = False
                            bid = "pO%d_%d" % (hh, bank)
                            if bid not in wr:
                                wr.append(bid)

                def pv(e):
                    rr = None
                    for (hh, bi, half, bank, col, fr, w2) in plan:
                        rr = e.matmul(PO[hh][bank][:, col:col + 128 * w2], lhsT=VAr[vts[bi]][:, hh, :],
                                      rhs=P[:, hh, bi * 256 + half * 128:bi * 256 + half * 128 + 128 * w2],
                                      start=fr, stop=False, skip_group_check=True)
                    return rr
                S.op("pe", pv, reads=["P%d_0" % eb, "P%d_1" % eb] + ["VAt%d" % v for v in vts], writes=wr)
                for j in js:
                    if j % gs != gs - 1:
                        continue
                    g = j // gs
                    c_lo = g * gs * 128 * d + r
                    ncol = gs * 128
                    c_hi = c_lo + (ncol - 1) * d + 1
                    for hh in range(2):
                        bank = grp_bank[(hh, g)]
                        bid = "pO%d_%d" % (hh, bank)
                        acc = ACC[hh]
                        aids = accids(hh, c_lo, c_hi)
                        if pi == 0:
                            S.op("act", lambda e, hh=hh, bank=bank, acc=acc, c_lo=c_lo, c_hi=c_hi: e.activation(
                                out=acc[:, c_lo:c_hi:d], in_=PO[hh][bank][:, 0:ncol], func=AF.Copy),
                                reads=[bid], writes=aids)
                        else:
                            S.op("dve", lambda e, hh=hh, bank=bank, acc=acc, c_lo=c_lo, c_hi=c_hi: e.tensor_tensor(
                                out=acc[:, c_lo:c_hi:d], in0=PO[hh][bank][:, 0:ncol],
                                in1=acc[:, c_lo:c_hi:d], op=ALU.add),
                                reads=[bid] + aids, writes=aids)

            for pi, (w, d) in enumerate(PATTERNS):
                L = T // d
                nb = L // 128
                gs = min(4, nb)
                for r in range(d):
                    grp_bank = {}
                    fresh = {}

                    for j0 in range(0, nb, 2):
                        items.append((pi, d, nb, gs, r, grp_bank, fresh, j0))
            ctxs = {}
            LA = 3
            for k in range(len(items) + LA):
                if k < len(items):
                    ctxs[k] = stageA(items[k])
                if k - LA >= 0:
                    stageB(ctxs.pop(k - LA))
            for hh in range(2):
                acc = ACC[hh]
                for c0 in range(0, T, RC):
                    rb = nxt("R", 2)
                    Rtile = Rt[rb]
                    S.op("act", lambda e, acc=acc, c0=c0, Rtile=Rtile: e.activation(
                        out=Rtile[64:128, :], in_=acc[64:128, c0:c0 + RC], func=AF.Ln),
                        reads=accids(hh, c0, c0 + RC), writes=["Rl%d" % rb])
                    S.op("act", lambda e, Rtile=Rtile: e.activation(
                        out=Rtile[0:64, :], in_=Rtile[64:128, :], func=AF.Exp, scale=-1.0),
                        reads=["Rl%d" % rb], writes=["Rt%d" % rb])
                    S.op("dve", lambda e, acc=acc, c0=c0, Rtile=Rtile, hh=hh: e.tensor_tensor(
                        out=OTst[hh * 64:(hh + 1) * 64, c0:c0 + RC], in0=acc[0:64, c0:c0 + RC], in1=Rtile[0:64, :], op=ALU.mult),
                        reads=accids(hh, c0, c0 + RC) + ["Rt%d" % rb], writes=["OTst_%d_%d" % (hh, c0)])
            S.dma("OTst", lambda e: [e.dma_start(out=dr["MIXT"][512 + hp * 128:512 + (hp + 1) * 128, :], in_=OTst[:])],
                  reads=["OTst_%d_%d" % (hh, c0) for hh in range(2) for c0 in range(0, T, RC)], store=True)
        for hp in range(4):
            pair(hp)
        S.dram_barrier()
        S.emit()
    nc.all_engine_barrier()


def consts(nc, S, st, dr):
    sb, ps = _mk(nc, st)
    C = Ctx()
    C.eps = sb("eps", [128, 1], F32)
    C.one = sb("one", [128, 1], F32)
    C.ident = sb("ident", [128, 128], BF16)
    C.blk = sb("blk", [128, 128], BF16)
    C.ones = sb("ones", [128, 128], BF16)
    S.op("pool", lambda e: e.memset(C.eps[:], EPS), writes=["eps"])
    S.op("pool", lambda e: e.memset(C.one[:], 1.0), writes=["one"])
    S.op("pool", lambda e: e.memset(C.ident[:], 0.0), writes=["ident"])
    S.op("pool", lambda e: e.affine_select(out=C.ident[:], in_=C.ident[:], pattern=[[-1, 128]],
                                           compare_op=ALU.not_equal, fill=1.0, base=0, channel_multiplier=1),
         reads=["ident"], writes=["ident"])
    S.op("pool", lambda e: e.memset(C.blk[:], 0.0), writes=["blk"])
    S.op("pool", lambda e: e.memset(C.blk[0:64, 0:64], 1.0), reads=["blk"], writes=["blk"])
    S.op("pool", lambda e: e.memset(C.blk[64:128, 64:128], 1.0), reads=["blk"], writes=["blk"])
    S.op("pool", lambda e: e.memset(C.ones[:], 1.0), writes=["ones"])
    return C


def build(T, stages=("p1",), debug=False):
    nc = bass.Bass("TRN2", target_bir_lowering=False)
    dr = {}

    def inp(name, shape):
        dr[name] = nc.dram_tensor(name, shape, F32, kind="ExternalInput").ap()
    inp("x", [T, 1024])
    inp("norm1_g", [1024])
    inp("w_in", [1024, 2560])
    inp("conv_w", [31, 512])
    inp("conv_b", [512])
    inp("cn_g", [512])
    inp("cn_b", [512])
    inp("q_norm_g", [64])
    inp("k_norm_g", [64])
    inp("w_out", [1024, 1024])
    inp("norm2_g", [1024])
    inp("w_up", [1024, 5632])
    inp("ffconv_w", [3, 5632])
    inp("ffconv_b", [5632])
    inp("w_down", [2816, 1024])
    inp("amask", [128, 24 * 512])
    dr["out"] = nc.dram_tensor("out", [T, 1024], F32, kind="ExternalOutput").ap()
    kind = "ExternalOutput" if debug else "Internal"

    def scr(name, shape, dt):
        dr[name] = nc.dram_tensor(name, shape, dt, kind=kind).ap()
    scr("UT", [512, 32 + T], BF16)
    scr("QT", [512, T], BF16)
    scr("KT", [512, T], BF16)
    scr("VA", [T, 8, 128], BF16)
    scr("MIXT", [1024, T], BF16)
    scr("X1", [T, 1024], F32)
    scr("X1T", [1024, T], BF16)
    with contextlib.ExitStack() as st:
        S = Sched(nc, st)
        C = consts(nc, S, st, dr)
        if "p1" in stages:
            pass1(nc, S, T, dr, C)
        if "p2a" in stages:
            pass2a(nc, S, T, dr, C)
        if "p2b" in stages:
            pass2b(nc, S, T, dr, C)
        if "p2c" in stages:
            pass2c(nc, S, T, dr, C)
        if "p3" in stages:
            pass3(nc, S, T, dr, C, 0)
            pass3(nc, S, T, dr, C, 1)
    return nc


ALL_STAGES = ("p1", "p2a", "p2b", "p2c", "p3")


def kernel(**inputs):
    T = 8192
    nc = build(T, stages=ALL_STAGES)
    in_maps = [host_inputs(inputs, b, T) for b in range(8)]
    res = run_bass_kernel_spmd(nc, in_maps, core_ids=list(range(8)))
    out = np.stack([np.asarray(r["out"]).reshape(T, 1024) for r in res.results], 0)
    return out.astype(np.float32)


def attn_mask_table():
    k = np.arange(128)[:, None]
    q = np.arange(256)[None, :]
    delta = q - k
    valid = (delta >= 0) & (delta <= 128)
    out = np.zeros((128, 24, 2, 256), np.float32)
    for h in range(8):
        slope = 2.0 ** (-(h + 1))
        for p, (w, d) in enumerate(PATTERNS):
            m = np.where(valid, -slope * d * np.maximum(delta, 0).astype(np.float64), -30000.0)
            out[:, h * 3 + p, 0] = m
            out[:, h * 3 + p, 1] = m
    return out.reshape(128, 24 * 512)


def host_inputs(inputs, b, T):
    m = {}
    for k, v in inputs.items():
        v = np.asarray(v)
        if k == "x":
            m[k] = np.ascontiguousarray(v[b, :T])
        else:
            m[k] = np.ascontiguousarray(v)
    m["amask"] = attn_mask_table()
    return m
```

```python
import contextlib
import numpy as np
import concourse.bass as bass
import concourse.mybir as mybir
from concourse.bass_utils import run_bass_kernel_spmd

F32 = mybir.dt.float32
BF16 = mybir.dt.bfloat16
ALU = mybir.AluOpType
AF = mybir.ActivationFunctionType

COMPUTE = ("pe", "act", "dve", "pool")
EPS = 1e-6
NH = 8
PATTERNS = ((128, 1), (512, 4), (2048, 16))


class Sched:
    def __init__(self, nc, stack):
        self.nc = nc
        self.stack = stack
        self.prog = {e: [] for e in COMPUTE + ("sp",)}
        self.cnt = {e: 0 for e in COMPUTE}
        self.sem = {e: stack.enter_context(nc.semaphore("s_" + e)) for e in COMPUTE}
        self.dma_sem = {}
        self.dma_cnt = {}
        self.last_w = {}
        self.readers = {}
        self.waited = {e: {} for e in self.prog}
        self.stores = []

    def _sem_of(self, k):
        return self.sem[k] if isinstance(k, str) else self.dma_sem[k[1]]

    def _deps(self, engine, reads, writes):
        toks = []
        for b in reads:
            if b in self.last_w:
                toks.append(self.last_w[b])
        for b in writes:
            if b in self.last_w:
                toks.append(self.last_w[b])
            toks.extend(self.readers.get(b, ()))
        best = {}
        for (k, v) in toks:
            if k == "pe" and engine == "pe":
                continue
            if best.get(k, 0) < v:
                best[k] = v
        out = []
        w = self.waited[engine]
        for k, v in best.items():
            if w.get(k, 0) >= v:
                continue
            w[k] = v
            out.append((k, v))
        return out

    def _record(self, tok, reads, writes):
        for b in writes:
            self.last_w[b] = tok
            self.readers[b] = []
        for b in reads:
            self.readers.setdefault(b, []).append(tok)

    def op(self, engine, fn, reads=(), writes=()):
        waits = self._deps(engine, reads, writes)
        self.cnt[engine] += 1
        tok = (engine, self.cnt[engine])
        self.prog[engine].append((waits, fn, engine, 1))
        self._record(tok, reads, writes)
        return tok

    def dma(self, key, fn, reads=(), writes=(), n=1, store=False):
        if key not in self.dma_sem:
            self.dma_sem[key] = self.stack.enter_context(
                self.nc.semaphore("d_%d" % len(self.dma_sem)))
            self.dma_cnt[key] = 0
        waits = self._deps("sp", reads, writes)
        self.dma_cnt[key] += n
        tok = (("dma", key), 16 * self.dma_cnt[key])
        self.prog["sp"].append((waits, fn, ("dma", key), n))
        self._record(tok, reads, writes)
        if store:
            self.stores.append(tok)
        return tok

    def dram_barrier(self):
        best = {}
        for (k, v) in self.stores:
            best[k] = max(best.get(k, 0), v)
        w = self.waited["sp"]
        waits = []
        for k, v in best.items():
            if w.get(k, 0) < v:
                w[k] = v
                waits.append((k, v))
        self.prog["sp"].append((waits, None, None, 0))
        self.stores = []

    def emit(self):
        nc = self.nc
        prog = self.prog
        self.prog = {e: [] for e in prog}
        with nc.Block() as block:
            def run(name):
                def body(e):
                    for (waits, fn, semkey, n) in prog[name]:
                        for (k, v) in waits:
                            e.wait_ge(self._sem_of(k), v)
                        if fn is None:
                            continue
                        r = fn(e)
                        if isinstance(semkey, str):
                            r.then_inc(self.sem[semkey], 1)
                        else:
                            if not isinstance(r, (list, tuple)):
                                r = [r]
                            assert len(r) == n, (len(r), n)
                            for ins in r:
                                ins.then_inc(self.dma_sem[semkey[1]], 16)
                return body
            block.tensor(run("pe"))
            block.scalar(run("act"))
            block.vector(run("dve"))
            block.gpsimd(run("pool"))
            block.sync(run("sp"))


class Ctx:
    pass


_UNIQ = [0]


def _mk(nc, st):
    _UNIQ[0] += 1
    pre = "k%d_" % _UNIQ[0]

    def sb(name, shape, dt):
        return st.enter_context(nc.sbuf_tensor(pre + name, shape, dt))

    def ps(name, shape, dt):
        return st.enter_context(nc.psum_tensor(pre + name, shape, dt))
    return sb, ps


def load_vec(S, nc, sb, name, src_ap, ncol, C, pbank, pbid):
    t = sb(name, [128, ncol], F32)
    rows = sb(name + "_r", [ncol, 128], F32)
    S.dma(name + "_r", lambda e: [e.dma_start(out=rows[:], in_=src_ap.rearrange("(c p) -> c p", p=128))],
          writes=[name + "_r"])
    S.op("pe", lambda e: e.transpose(pbank[:, 0:ncol], rows[:], C.ident32[0:ncol, 0:ncol]),
         reads=[name + "_r", "ident32"], writes=[pbid])
    S.op("dve", lambda e: e.tensor_copy(out=t[:], in_=pbank[:, 0:ncol]), reads=[pbid], writes=[name])
    return t


def pass1(nc, S, T, dr, C):
    NT = T // 512
    with contextlib.ExitStack() as st:
        sb, ps = _mk(nc, st)
        winb = sb("winb", [128, 8, 2560], BF16)
        PM = [ps("pM%d" % i, [128, 512], F32) for i in range(4)]
        g1t = load_vec(S, nc, sb, "g1t", dr["norm1_g"], 8, C, PM[0], "pM0")
        wst = [sb("wst%d" % i, [128, 2560], F32) for i in range(2)]
        for c in range(8):
            sid = "wst%d" % (c % 2)
            stg = wst[c % 2]
            S.dma(sid, lambda e, stg=stg, c=c: [e.dma_start(out=stg[:], in_=dr["w_in"][c * 128:(c + 1) * 128, :])],
                  writes=[sid])
            if c % 2 == 0:
                S.op("dve", lambda e, stg=stg, c=c: e.tensor_scalar(out=winb[:, c, :], in0=stg[:], scalar1=g1t[:, c:c + 1],
                                                                    scalar2=None, op0=ALU.mult),
                     reads=[sid, "g1t"], writes=["winb%d" % c])
            else:
                S.op("act", lambda e, stg=stg, c=c: e.activation(out=winb[:, c, :], in_=stg[:], func=AF.Identity,
                                                                 scale=g1t[:, c:c + 1]),
                     reads=[sid, "g1t"], writes=["winb%d" % c])
        gq = sb("gq", [128, 2], F32)
        S.dma("gq", lambda e: [
            e.dma_start(out=gq[0:64, 0:1], in_=dr["q_norm_g"].rearrange("(p o) -> p o", o=1)),
            e.dma_start(out=gq[64:128, 0:1], in_=dr["q_norm_g"].rearrange("(p o) -> p o", o=1)),
            e.dma_start(out=gq[0:64, 1:2], in_=dr["k_norm_g"].rearrange("(p o) -> p o", o=1)),
            e.dma_start(out=gq[64:128, 1:2], in_=dr["k_norm_g"].rearrange("(p o) -> p o", o=1)),
        ], writes=["gq0"], n=4)
        gqs = sb("gqs", [128, 2], F32)
        S.op("dve", lambda e: e.tensor_scalar(out=gqs[:, 0:1], in0=gq[:, 0:1], scalar1=0.125, scalar2=None, op0=ALU.mult),
             reads=["gq0"], writes=["gqs_a"])
        S.op("dve", lambda e: e.tensor_copy(out=gqs[:, 1:2], in_=gq[:, 1:2]), reads=["gq0"], writes=["gqs_b"])
        zt = sb("zt", [128, 4, 32], BF16)
        S.op("pool", lambda e: e.memset(zt[:], 0.0), writes=["zt"])
        S.dma("zt", lambda e: [e.dma_start(out=dr["UT"][:, 0:32].rearrange("(c p) t -> p c t", p=128), in_=zt[:])],
              reads=["zt"], store=True)

        xr = [sb("x%d" % i, [128, 4, 1024], F32) for i in range(3)]
        junk = sb("junk", [128, 1024], BF16)
        ssr = [sb("ss%d" % i, [128, 4], F32) for i in range(2)]
        lnr = [sb("ln%d" % i, [128, 4], F32) for i in range(2)]
        rsr = [sb("rs%d" % i, [128, 4], F32) for i in range(2)]
        xsr = [sb("xs%d" % i, [128, 1024], BF16) for i in range(2)]
        xTr = [sb("xT%d" % i, [128, 8, 512], BF16) for i in range(2)]
        uTr = [sb("uT%d" % i, [128, 4, 512], BF16) for i in range(2)]
        qkr = [sb("qk%d" % i, [128, 8, 512], BF16) for i in range(2)]
        var = [sb("va%d" % i, [128, 4, 8, 128], BF16) for i in range(2)]
        for i in range(2):
            S.op("pool", lambda e, i=i: e.memset(var[i][:, :, :, 64:128], 1.0), writes=["va_ones%d" % i])
        er = [sb("e%d" % i, [128, 512], F32) for i in range(2)]
        dr_ = [sb("d%d" % i, [128, 512], F32) for i in range(2)]
        sqr = [sb("sq%d" % i, [128, 512], BF16) for i in range(2)]
        lqr = [sb("lq%d" % i, [128, 512], F32) for i in range(2)]
        rqr = [sb("rq%d" % i, [128, 512], F32) for i in range(2)]
        PT = [ps("pT%d" % i, [128, 1024], BF16) for i in range(2)]
        PS_ = [ps("pS%d" % i, [128, 512], F32) for i in range(2)]
        cnt = {"T": 0, "M": 0, "S": 0, "e": 0, "q": 0}

        def nxt(k, n):
            v = cnt[k] % n
            cnt[k] += 1
            return v

        WINB = ["winb%d" % c for c in range(8)]

        def load_x(i):
            xt = xr[i % 3]
            S.dma("x%d" % (i % 3), lambda e: [e.dma_start(
                out=xt[:], in_=dr["x"][i * 512:(i + 1) * 512, :].rearrange("(s p) d -> p s d", p=128))],
                writes=["x%d" % (i % 3)])

        def front(i):
            b = i % 2
            xt, ss, ln, rs, xT = xr[i % 3], ssr[b], lnr[b], rsr[b], xTr[b]
            xid = "x%d" % (i % 3)
            for s in range(4):
                S.op("act", lambda e, s=s: e.activation(out=junk[:], in_=xt[:, s, :], func=AF.Square,
                                                        accum_out=ss[:, s:s + 1]),
                     reads=[xid], writes=["junk", "ss%d_%d" % (b, s)])
            S.op("act", lambda e: e.activation(out=ln[:], in_=ss[:], func=AF.Ln, scale=1.0 / 1024, bias=C.eps[:, 0:1]),
                 reads=["ss%d_%d" % (b, s) for s in range(4)], writes=["ln%d" % b])
            S.op("act", lambda e: e.activation(out=rs[:], in_=ln[:], func=AF.Exp, scale=-0.5),
                 reads=["ln%d" % b], writes=["rs%d" % b])
            for s in range(4):
                xs = xsr[s % 2]
                xsid = "xs%d" % (s % 2)
                S.op("dve", lambda e, s=s, xs=xs: e.tensor_scalar(out=xs[:], in0=xt[:, s, :], scalar1=rs[:, s:s + 1],
                                                                 scalar2=None, op0=ALU.mult),
                     reads=[xid, "rs%d" % b], writes=[xsid])
                tb = nxt("T", 2)
                pt = PT[tb]

                def tr(e, xs=xs, pt=pt):
                    r = None
                    for c in range(8):
                        r = e.transpose(pt[:, c * 128:(c + 1) * 128], xs[:, c * 128:(c + 1) * 128], C.ident[:])
                    return r
                S.op("pe", tr, reads=[xsid, "ident"], writes=["pT%d" % tb])
                S.op("act", lambda e, s=s, pt=pt: e.activation(
                    out=xT[:, :, s * 128:(s + 1) * 128], in_=pt[:].rearrange("p (c t) -> p c t", c=8), func=AF.Copy),
                    reads=["pT%d" % tb], writes=["xT%d_%d" % (b, s)])
        def tile(i):
            if i + 2 < NT:
                load_x(i + 2)
            b = i % 2
            xT, uT, qk, va = xTr[b], uTr[b], qkr[b], var[b]
            xTids = ["xT%d_%d" % (b, s) for s in range(4)]

            def proj(fc, pm):
                def f(e):
                    r = None
                    for c in range(8):
                        r = e.matmul(pm[:], lhsT=winb[:, c, fc * 128:(fc + 1) * 128], rhs=xT[:, c, :],
                                     start=(c == 0), stop=(c == 7))
                    return r
                return f
            for c4 in range(4):
                mv = nxt("M", 4)
                mg = nxt("M", 4)
                S.op("pe", proj(c4, PM[mv]), reads=xTids + WINB, writes=["pM%d" % mv])
                S.op("pe", proj(4 + c4, PM[mg]), reads=xTids + WINB, writes=["pM%d" % mg])
                eb = nxt("e", 2)
                et, dt_ = er[eb], dr_[eb]
                S.op("act", lambda e, mg=mg, et=et: e.activation(out=et[:], in_=PM[mg][:], func=AF.Exp, scale=-1.0),
                     reads=["pM%d" % mg], writes=["e%d" % eb])
                S.op("act", lambda e, et=et, dt_=dt_: e.activation(out=dt_[:], in_=et[:], func=AF.Ln, bias=C.one[:, 0:1]),
                     reads=["e%d" % eb, "one"], writes=["d%d" % eb])
                S.op("act", lambda e, et=et, dt_=dt_: e.activation(out=et[:], in_=dt_[:], func=AF.Exp, scale=-1.0),
                     reads=["d%d" % eb], writes=["e%d" % eb])
                S.op("dve", lambda e, mv=mv, et=et, c4=c4: e.tensor_tensor(out=uT[:, c4, :], in0=PM[mv][:], in1=et[:],
                                                                         op=ALU.mult),
                     reads=["pM%d" % mv, "e%d" % eb], writes=["uT%d_%d" % (b, c4)])
            S.dma("uT%d" % b, lambda e: [e.dma_start(
                out=dr["UT"][:, 32 + i * 512:32 + (i + 1) * 512].rearrange("(c p) t -> p c t", p=128), in_=uT[:])],
                reads=["uT%d_%d" % (b, c4) for c4 in range(4)], store=True)
            if i + 1 < NT:
                front(i + 1)
            pend = []

            def qk_front(c8):
                m = nxt("M", 4)
                S.op("pe", proj(8 + c8, PM[m]), reads=xTids + WINB, writes=["pM%d" % m])
                qb = nxt("q", 2)
                sq = sqr[qb]
                S.op("act", lambda e, m=m, sq=sq: e.activation(out=sq[:], in_=PM[m][:], func=AF.Square),
                     reads=["pM%d" % m], writes=["sq%d" % qb])
                pend.append((c8, m, qb))

            def qk_back():
                (c8, m, qb) = pend.pop(0)
                sq, lq, rq = sqr[qb], lqr[qb], rqr[qb]
                sbk = nxt("S", 2)
                S.op("pe", lambda e, sq=sq, sbk=sbk: e.matmul(PS_[sbk][:], lhsT=C.blk[:], rhs=sq[:], start=True, stop=True),
                     reads=["sq%d" % qb, "blk"], writes=["pS%d" % sbk])
                S.op("act", lambda e, lq=lq, sbk=sbk: e.activation(out=lq[:], in_=PS_[sbk][:], func=AF.Ln, scale=1.0 / 64,
                                                                  bias=C.eps[:, 0:1]),
                     reads=["pS%d" % sbk], writes=["lq%d" % qb])
                S.op("act", lambda e, lq=lq, rq=rq: e.activation(out=rq[:], in_=lq[:], func=AF.Exp, scale=-0.5),
                     reads=["lq%d" % qb], writes=["rq%d" % qb])
                gcol = 0 if c8 < 4 else 1
                S.op("dve", lambda e, m=m, rq=rq, c8=c8, gcol=gcol: e.scalar_tensor_tensor(
                    out=qk[:, c8, :], in0=PM[m][:], scalar=gqs[:, gcol:gcol + 1], in1=rq[:], op0=ALU.mult, op1=ALU.mult),
                    reads=["pM%d" % m, "rq%d" % qb, "gqs_a", "gqs_b"], writes=["qk%d_%d" % (b, c8)])

            for c8 in range(8):
                qk_front(c8)
                if c8 > 0:
                    qk_back()
            qk_back()
            S.dma("qk%d" % b, lambda e: [
                e.dma_start(out=dr["QT"][:, i * 512:(i + 1) * 512].rearrange("(c p) t -> p c t", p=128), in_=qk[:, 0:4, :]),
                e.dma_start(out=dr["KT"][:, i * 512:(i + 1) * 512].rearrange("(c p) t -> p c t", p=128), in_=qk[:, 4:8, :]),
            ], reads=["qk%d_%d" % (b, c8) for c8 in range(8)], n=2, store=True)
            for s in range(4):
                m = nxt("M", 4)

                def vproj(e, s=s, m=m):
                    r = None
                    for c in range(8):
                        r = e.matmul(PM[m][:], lhsT=xT[:, c, s * 128:(s + 1) * 128], rhs=winb[:, c, 2048:2560],
                                     start=(c == 0), stop=(c == 7))
                    return r
                S.op("pe", vproj, reads=xTids + WINB, writes=["pM%d" % m])
                S.op("act", lambda e, s=s, m=m: e.activation(out=va[:, s, :, 0:64],
                                                             in_=PM[m][:].rearrange("p (h e) -> p h e", h=8), func=AF.Copy),
                     reads=["pM%d" % m, "va_ones%d" % b], writes=["va%d_%d" % (b, s)])
            S.dma("va%d" % b, lambda e: [e.dma_start(
                out=dr["VA"][i * 512:(i + 1) * 512, :, :].rearrange("(s p) h e -> p s h e", p=128), in_=va[:])],
                reads=["va%d_%d" % (b, s) for s in range(4)], store=True)
        load_x(0)
        if NT > 1:
            load_x(1)
        front(0)
        for i in range(NT):
            tile(i)
        S.dram_barrier()
        S.emit()
    nc.all_engine_barrier()


def pass2a(nc, S, T, dr, C):
    NT = T // 512
    with contextlib.ExitStack() as st:
        sb, ps = _mk(nc, st)
        PM = [ps("pM%d" % i, [128, 512], F32) for i in range(4)]
        cw = sb("cw", [128, 4, 31], F32)
        cwr = sb("cwr", [31, 512], F32)
        S.dma("cwr", lambda e: [e.dma_start(out=cwr[:], in_=dr["conv_w"][:, :])], writes=["cwr"])

        def cwT(e):
            r = None
            for c in range(4):
                r = e.transpose(PM[1][:, c * 31:(c + 1) * 31], cwr[:, c * 128:(c + 1) * 128], C.ident32[0:31, 0:31])
            return r
        S.op("pe", cwT, reads=["cwr", "ident32"], writes=["pM1"])
        S.op("dve", lambda e: e.tensor_copy(out=cw[:], in_=PM[1][:, 0:124].rearrange("p (c k) -> p c k", c=4)),
             reads=["pM1"], writes=["cw"])
        cb = load_vec(S, nc, sb, "cb", dr["conv_b"], 4, C, PM[0], "pM0")
        cg = load_vec(S, nc, sb, "cg", dr["cn_g"], 4, C, PM[2], "pM2")
        cnb = load_vec(S, nc, sb, "cnb", dr["cn_b"], 4, C, PM[3], "pM3")
        ncg = sb("ncg", [128, 4], F32)
        ncb = sb("ncb", [128, 4], F32)
        S.op("dve", lambda e: e.tensor_scalar(out=ncg[:], in0=cg[:], scalar1=-1.0, scalar2=None, op0=ALU.mult),
             reads=["cg"], writes=["ncg"])
        S.op("dve", lambda e: e.tensor_scalar(out=ncb[:], in0=cnb[:], scalar1=-1.0, scalar2=None, op0=ALU.mult),
             reads=["cnb"], writes=["ncb"])
        diag = sb("diag", [128, 124, 128], BF16)
        for c in range(4):
            for k in range(31):
                if k % 2 == 0:
                    S.op("dve", lambda e, c=c, k=k: e.tensor_scalar(out=diag[:, c * 31 + k, :], in0=C.ident[:],
                                                                    scalar1=cw[:, c, k:k + 1], scalar2=None, op0=ALU.mult),
                         reads=["cw", "ident"], writes=["diag_%d_%d" % (c, k)])
                else:
                    S.op("act", lambda e, c=c, k=k: e.activation(out=diag[:, c * 31 + k, :], in_=C.ident[:],
                                                                 func=AF.Identity, scale=cw[:, c, k:k + 1]),
                         reads=["cw", "ident"], writes=["diag_%d_%d" % (c, k)])
        diag_ids = ["diag_%d_%d" % (c, k) for c in range(4) for k in range(31)]
        ur = [sb("U%d" % i, [128, 4, 544], BF16) for i in range(2)]
        y32r = [sb("y32_%d" % i, [128, 4, 512], F32) for i in range(2)]
        ybr = [sb("yb_%d" % i, [128, 4, 512], BF16) for i in range(2)]
        ysr = [sb("ys_%d" % i, [128, 4, 512], BF16) for i in range(2)]
        aTr = [sb("aT_%d" % i, [128, 4, 512], BF16) for i in range(2)]
        mean = sb("mean", [128, 512], F32)
        msq = sb("msq", [128, 512], F32)
        varr = sb("var", [128, 512], F32)
        lv = sb("lv", [128, 512], F32)
        rstd = sb("rstd", [128, 512], F32)
        er = [sb("ce%d" % i, [128, 512], F32) for i in range(2)]
        dd = [sb("cd%d" % i, [128, 512], F32) for i in range(2)]
        PS1 = ps("pS1", [128, 512], F32)
        PS2 = ps("pS2", [128, 512], F32)
        cnt = {"M": 0, "e": 0}

        def nxt(k, n):
            v = cnt[k] % n
            cnt[k] += 1
            return v

        def load_u(i):
            U = ur[i % 2]
            S.dma("U%d" % (i % 2), lambda e: [e.dma_start(
                out=U[:, :, 0:542], in_=dr["UT"][:, 2 + i * 512:2 + i * 512 + 542].rearrange("(c p) t -> p c t", p=128))],
                writes=["U%d" % (i % 2)])

        def tile(i):
            if i + 1 < NT:
                load_u(i + 1)
            b = i % 2
            U, y32, yb, ys, aT = ur[b], y32r[b], ybr[b], ysr[b], aTr[b]
            for c in range(4):
                m = nxt("M", 4)

                def conv(e, c=c, m=m):
                    r = None
                    for k in range(31):
                        r = e.matmul(PM[m][:], lhsT=diag[:, c * 31 + k, :], rhs=U[:, c, k:k + 512],
                                     start=(k == 0), stop=(k == 30))
                    return r
                S.op("pe", conv, reads=["U%d" % b] + diag_ids[c * 31:(c + 1) * 31], writes=["pM%d" % m])
                S.op("act", lambda e, c=c, m=m: e.activation(out=y32[:, c, :], in_=PM[m][:], func=AF.Identity,
                                                             bias=cb[:, c:c + 1]),
                     reads=["pM%d" % m, "cb"], writes=["y32_%d_%d" % (b, c)])
                S.op("act", lambda e, c=c, m=m: e.activation(out=yb[:, c, :], in_=PM[m][:], func=AF.Identity,
                                                             bias=cb[:, c:c + 1]),
                     reads=["pM%d" % m, "cb"], writes=["yb_%d_%d" % (b, c)])
                S.op("act", lambda e, c=c, m=m: e.activation(out=ys[:, c, :], in_=PM[m][:], func=AF.Square,
                                                             bias=cb[:, c:c + 1]),
                     reads=["pM%d" % m, "cb"], writes=["ys_%d_%d" % (b, c)])

            def st1(e):
                r = None
                for c in range(4):
                    r = e.matmul(PS1[:], lhsT=C.ones[:], rhs=yb[:, c, :], start=(c == 0), stop=(c == 3))
                return r

            def st2(e):
                r = None
                for c in range(4):
                    r = e.matmul(PS2[:], lhsT=C.ones[:], rhs=ys[:, c, :], start=(c == 0), stop=(c == 3))
                return r
            S.op("pe", st1, reads=["yb_%d_%d" % (b, c) for c in range(4)] + ["ones"], writes=["pS1"])
            S.op("pe", st2, reads=["ys_%d_%d" % (b, c) for c in range(4)] + ["ones"], writes=["pS2"])
            S.op("dve", lambda e: e.tensor_scalar(out=mean[:], in0=PS1[:], scalar1=1.0 / 512, scalar2=None, op0=ALU.mult),
                 reads=["pS1"], writes=["mean"])
            S.op("dve", lambda e: e.tensor_tensor(out=msq[:], in0=mean[:], in1=mean[:], op=ALU.mult),
                 reads=["mean"], writes=["msq"])
            S.op("dve", lambda e: e.scalar_tensor_tensor(out=varr[:], in0=PS2[:], scalar=1.0 / 512, in1=msq[:],
                                                         op0=ALU.mult, op1=ALU.subtract),
                 reads=["pS2", "msq"], writes=["var"])
            S.op("act", lambda e: e.activation(out=lv[:], in_=varr[:], func=AF.Ln, bias=C.eps[:, 0:1]),
                 reads=["var", "eps"], writes=["lv"])
            S.op("act", lambda e: e.activation(out=rstd[:], in_=lv[:], func=AF.Exp, scale=-0.5),
                 reads=["lv"], writes=["rstd"])
            for c in range(4):
                yid = "y32_%d_%d" % (b, c)
                eb = nxt("e", 2)
                et, dt_ = er[eb], dd[eb]
                S.op("dve", lambda e, c=c: e.tensor_tensor(out=y32[:, c, :], in0=y32[:, c, :], in1=mean[:], op=ALU.subtract),
                     reads=[yid, "mean"], writes=[yid])
                S.op("dve", lambda e, c=c: e.tensor_tensor(out=y32[:, c, :], in0=y32[:, c, :], in1=rstd[:], op=ALU.mult),
                     reads=[yid, "rstd"], writes=[yid])
                S.op("act", lambda e, c=c, et=et: e.activation(out=et[:], in_=y32[:, c, :], func=AF.Exp,
                                                               scale=ncg[:, c:c + 1], bias=ncb[:, c:c + 1]),
                     reads=[yid, "ncg", "ncb"], writes=["ce%d" % eb])
                S.op("dve", lambda e, c=c: e.tensor_scalar(out=y32[:, c, :], in0=y32[:, c, :], scalar1=cg[:, c:c + 1],
                                                           scalar2=cnb[:, c:c + 1], op0=ALU.mult, op1=ALU.add),
                     reads=[yid, "cg", "cnb", "ce%d" % eb], writes=[yid])
                S.op("act", lambda e, et=et, dt_=dt_: e.activation(out=dt_[:], in_=et[:], func=AF.Ln, bias=C.one[:, 0:1]),
                     reads=["ce%d" % eb, "one"], writes=["cd%d" % eb])
                S.op("act", lambda e, et=et, dt_=dt_: e.activation(out=et[:], in_=dt_[:], func=AF.Exp, scale=-1.0),
                     reads=["cd%d" % eb], writes=["ce%d" % eb])
                S.op("dve", lambda e, c=c, et=et: e.tensor_tensor(out=aT[:, c, :], in0=y32[:, c, :], in1=et[:], op=ALU.mult),
                     reads=[yid, "ce%d" % eb], writes=["aT_%d_%d" % (b, c)])
            S.dma("aT%d" % b, lambda e: [e.dma_start(
                out=dr["MIXT"][0:512, i * 512:(i + 1) * 512].rearrange("(c p) t -> p c t", p=128), in_=aT[:])],
                reads=["aT_%d_%d" % (b, c) for c in range(4)], store=True)
        load_u(0)
        for i in range(NT):
            tile(i)
        S.dram_barrier()
        S.emit()
    nc.all_engine_barrier()


def pass2c(nc, S, T, dr, C):
    NT = T // 512
    with contextlib.ExitStack() as st:
        sb, ps = _mk(nc, st)
        woutb = sb("woutb", [128, 8, 1024], BF16)
        wst = [sb("wost%d" % i, [128, 1024], F32) for i in range(2)]
        for c in range(8):
            sid = "wost%d" % (c % 2)
            stg = wst[c % 2]
            S.dma(sid, lambda e, stg=stg, c=c: [e.dma_start(out=stg[:], in_=dr["w_out"][c * 128:(c + 1) * 128, :])],
                  writes=[sid])
            if c % 2 == 0:
                S.op("dve", lambda e, stg=stg, c=c: e.tensor_copy(out=woutb[:, c, :], in_=stg[:]),
                     reads=[sid], writes=["woutb%d" % c])
            else:
                S.op("act", lambda e, stg=stg, c=c: e.activation(out=woutb[:, c, :], in_=stg[:], func=AF.Copy),
                     reads=[sid], writes=["woutb%d" % c])
        mr = [sb("mix%d" % i, [128, 8, 1024], BF16) for i in range(2)]
        xr = [sb("x%d" % i, [128, 4, 1024], F32) for i in range(2)]
        x1r = [sb("x1_%d" % i, [128, 4, 1024], F32) for i in range(2)]
        x1Tr = [sb("x1T%d" % i, [128, 8, 1024], BF16) for i in range(2)]
        junk = sb("junk", [128, 1024], BF16)
        ssr = [sb("ss%d" % i, [128, 4], F32) for i in range(2)]
        lnr = [sb("ln%d" % i, [128, 4], F32) for i in range(2)]
        rsr = [sb("rs%d" % i, [128, 4], F32) for i in range(2)]
        xsr = [sb("xs%d" % i, [128, 1024], BF16) for i in range(2)]
        PT = [ps("pT%d" % i, [128, 1024], BF16) for i in range(2)]
        PM = [ps("pM%d" % i, [128, 512], F32) for i in range(4)]
        cnt = {"M": 0, "T": 0}

        def nxt(k, n):
            v = cnt[k] % n
            cnt[k] += 1
            return v

        def load(i):
            b = i % 2
            xt = xr[b]
            if i % 2 == 0:
                sI = i // 2
                mixs = mr[sI % 2]
                S.dma("mix%d" % (sI % 2), lambda e: [e.dma_start(
                    out=mixs[:], in_=dr["MIXT"][:, sI * 1024:(sI + 1) * 1024].rearrange("(c p) t -> p c t", p=128))],
                    writes=["mix%d" % (sI % 2)])
            S.dma("x%d" % b, lambda e: [e.dma_start(
                out=xt[:], in_=dr["x"][i * 512:(i + 1) * 512, :].rearrange("(s p) d -> p s d", p=128))],
                writes=["x%d" % b])

        def tile(i):
            if i + 1 < NT:
                load(i + 1)
            b = i % 2
            sI, hf2 = i // 2, i % 2
            mb = sI % 2
            mix, xt, x1, x1T, ss, ln, rs = mr[mb], xr[b], x1r[b], x1Tr[mb], ssr[b], lnr[b], rsr[b]
            o5 = hf2 * 512
            for s in range(4):
                for hh in range(2):
                    m = nxt("M", 4)

                    def mm(e, s=s, hh=hh, m=m):
                        r = None
                        for c in range(8):
                            r = e.matmul(PM[m][:], lhsT=mix[:, c, o5 + s * 128:o5 + (s + 1) * 128],
                                         rhs=woutb[:, c, hh * 512:(hh + 1) * 512], start=(c == 0), stop=(c == 7))
                        return r
                    S.op("pe", mm, reads=["mix%d" % mb] + ["woutb%d" % c for c in range(8)], writes=["pM%d" % m])
                    S.op("dve", lambda e, s=s, hh=hh, m=m: e.tensor_tensor(
                        out=x1[:, s, hh * 512:(hh + 1) * 512], in0=PM[m][:], in1=xt[:, s, hh * 512:(hh + 1) * 512], op=ALU.add),
                        reads=["pM%d" % m, "x%d" % b], writes=["x1_%d_%d_%d" % (b, s, hh)])
                S.op("act", lambda e, s=s: e.activation(out=junk[:], in_=x1[:, s, :], func=AF.Square,
                                                        accum_out=ss[:, s:s + 1]),
                     reads=["x1_%d_%d_0" % (b, s), "x1_%d_%d_1" % (b, s)], writes=["junk", "ss%d_%d" % (b, s)])
            x1ids = ["x1_%d_%d_%d" % (b, s, hh) for s in range(4) for hh in range(2)]
            S.dma("x1_%d" % b, lambda e: [e.dma_start(
                out=dr["X1"][i * 512:(i + 1) * 512, :].rearrange("(s p) d -> p s d", p=128), in_=x1[:])],
                reads=x1ids, store=True)
            S.op("act", lambda e: e.activation(out=ln[:], in_=ss[:], func=AF.Ln, scale=1.0 / 1024, bias=C.eps[:, 0:1]),
                 reads=["ss%d_%d" % (b, s) for s in range(4)] + ["eps"], writes=["ln%d" % b])
            S.op("act", lambda e: e.activation(out=rs[:], in_=ln[:], func=AF.Exp, scale=-0.5),
                 reads=["ln%d" % b], writes=["rs%d" % b])
            for s in range(4):
                xs = xsr[s % 2]
                xsid = "xs%d" % (s % 2)
                S.op("dve", lambda e, s=s, xs=xs: e.tensor_scalar(out=xs[:], in0=x1[:, s, :], scalar1=rs[:, s:s + 1],
                                                                 scalar2=None, op0=ALU.mult),
                     reads=["x1_%d_%d_0" % (b, s), "x1_%d_%d_1" % (b, s), "rs%d" % b], writes=[xsid])
                tb = nxt("T", 2)
                pt = PT[tb]

                def tr(e, xs=xs, pt=pt):
                    r = None
                    for c in range(8):
                        r = e.transpose(pt[:, c * 128:(c + 1) * 128], xs[:, c * 128:(c + 1) * 128], C.ident[:])
                    return r
                S.op("pe", tr, reads=[xsid, "ident"], writes=["pT%d" % tb])
                S.op("act", lambda e, s=s, pt=pt: e.activation(
                    out=x1T[:, :, o5 + s * 128:o5 + (s + 1) * 128], in_=pt[:].rearrange("p (c t) -> p c t", c=8), func=AF.Copy),
                    reads=["pT%d" % tb], writes=["x1T%d_%d_%d" % (mb, hf2, s)])
            if hf2 == 1:
                S.dma("x1T%d" % mb, lambda e: [e.dma_start(
                    out=dr["X1T"][:, sI * 1024:(sI + 1) * 1024].rearrange("(c p) t -> p c t", p=128), in_=x1T[:])],
                    reads=["x1T%d_%d_%d" % (mb, h2, s) for h2 in range(2) for s in range(4)], store=True)
        load(0)
        for i in range(NT):
            tile(i)
        S.dram_barrier()
        S.emit()
    nc.all_engine_barrier()


def pass3(nc, S, T, dr, C, hf):
    NT = T // 512
    NP = 11
    with contextlib.ExitStack() as st:
        sb, ps = _mk(nc, st)
        PM = [ps("pM%d" % i, [128, 512], F32) for i in range(4)]
        g2t = load_vec(S, nc, sb, "g2t%d" % hf, dr["norm2_g"], 8, C, PM[0], "pM0")
        ffb = load_vec(S, nc, sb, "ffb%d" % hf, dr["ffconv_b"], 44, C, PM[1], "pM1")
        ffw = sb("ffw", [128, 44, 3], F32)
        ffr = sb("ffr", [44, 3, 128], F32)
        S.dma("ffr%d" % hf, lambda e: [e.dma_start(out=ffr[:, k, :], in_=dr["ffconv_w"][k, :].rearrange("(j p) -> j p", p=128))
                                       for k in range(3)], writes=["ffr"], n=3)

        def ffT(e):
            r = None
            for k in range(3):
                r = e.transpose(PM[2][:, k * 44:(k + 1) * 44], ffr[:, k, :], C.ident32[0:44, 0:44])
            return r
        S.op("pe", ffT, reads=["ffr", "ident32"], writes=["pM2"])
        S.op("dve", lambda e: e.tensor_copy(out=ffw[:].rearrange("p j k -> p k j"),
                                            in_=PM[2][:, 0:132].rearrange("p (k j) -> p k j", k=3)),
             reads=["pM2"], writes=["ffw"])
        wupb = sb("wupb", [128, 8, 2816], BF16)
        wdb = sb("wdb", [128, NP, 1024], BF16)
        ust = [sb("ust%d" % i, [128, 2816], F32) for i in range(2)]
        dst = [sb("dst%d" % i, [128, 1024], F32) for i in range(2)]
        g2id = "g2t%d" % hf
        for c in range(8):
            sid = "ust%d" % (c % 2)
            stg = ust[c % 2]
            S.dma(sid, lambda e, stg=stg, c=c: [
                e.dma_start(out=stg[:, 0:1408], in_=dr["w_up"][c * 128:(c + 1) * 128, hf * 1408:(hf + 1) * 1408]),
                e.dma_start(out=stg[:, 1408:2816], in_=dr["w_up"][c * 128:(c + 1) * 128, 2816 + hf * 1408:2816 + (hf + 1) * 1408]),
            ], writes=[sid], n=2)
            if c % 2 == 0:
                S.op("dve", lambda e, stg=stg, c=c: e.tensor_scalar(out=wupb[:, c, :], in0=stg[:], scalar1=g2t[:, c:c + 1],
                                                                    scalar2=None, op0=ALU.mult),
                     reads=[sid, g2id], writes=["wupb%d" % c])
            else:
                S.op("act", lambda e, stg=stg, c=c: e.activation(out=wupb[:, c, :], in_=stg[:], func=AF.Identity,
                                                                 scale=g2t[:, c:c + 1]),
                     reads=[sid, g2id], writes=["wupb%d" % c])
        for j in range(NP):
            sid = "dst%d" % (j % 2)
            stg = dst[j % 2]
            r0 = (hf * NP + j) * 128
            S.dma(sid, lambda e, stg=stg, r0=r0: [e.dma_start(out=stg[:], in_=dr["w_down"][r0:r0 + 128, :])], writes=[sid])
            if j % 2 == 0:
                S.op("act", lambda e, stg=stg, j=j: e.activation(out=wdb[:, j, :], in_=stg[:], func=AF.Identity, scale=0.5),
                     reads=[sid], writes=["wdb%d" % j])
            else:
                S.op("dve", lambda e, stg=stg, j=j: e.tensor_scalar(out=wdb[:, j, :], in0=stg[:], scalar1=0.5, scalar2=None,
                                                                    op0=ALU.mult),
                     reads=[sid], writes=["wdb%d" % j])
        hal = sb("hal", [128, 22, 2], BF16)
        S.op("pool", lambda e: e.memset(hal[:], 0.0), writes=["hal%d" % k for k in range(22)])
        xTr = [sb("xT%d" % i, [128, 8, 512], BF16) for i in range(2)]
        pr = [sb("prev%d" % i, [128, 4, 1024], F32) for i in range(2)]
        Hr = [sb("H%d" % i, [128, NP, 512], BF16) for i in range(2)]
        Ur = [sb("Ub%d" % i, [128, 514], BF16) for i in range(4)]
        A2r = [sb("A2_%d" % i, [128, 512], F32) for i in range(4)]
        tbr = [sb("tb%d" % i, [128, 512], F32) for i in range(2)]
        PD = [ps("pD%d" % i, [128, 512], F32) for i in range(2)]
        cnt = {"M": 0, "D": 0, "U": 0, "t": 0}
        prev_src = dr["X1"] if hf == 0 else dr["out"]

        def nxt(k, n):
            v = cnt[k] % n
            cnt[k] += 1
            return v

        def load_xT(i):
            b = i % 2
            xT = xTr[b]
            S.dma("xT%d" % b, lambda e: [e.dma_start(
                out=xT[:], in_=dr["X1T"][:, i * 512:(i + 1) * 512].rearrange("(c p) t -> p c t", p=128))],
                writes=["xT%d" % b])

        def load_prev(i):
            b = i % 2
            pv = pr[b]
            S.dma("prev%d" % b, lambda e: [e.dma_start(
                out=pv[:], in_=prev_src[i * 512:(i + 1) * 512, :].rearrange("(s p) d -> p s d", p=128))],
                writes=["prev%d_%d_%d" % (b, s, hh) for s in range(4) for hh in range(2)])

        def down_group(i, gi):
            b = i % 2
            pv, H = pr[b], Hr[b]
            s, hh = gi // 2, gi % 2
            Hids = ["H%d_%d" % (b, jj) for jj in range(NP)]
            d = nxt("D", 2)

            def down(e):
                r = None
                for jj in range(NP):
                    r = e.matmul(PD[d][:], lhsT=H[:, jj, s * 128:(s + 1) * 128],
                                 rhs=wdb[:, jj, hh * 512:(hh + 1) * 512], start=(jj == 0), stop=(jj == NP - 1))
                return r
            S.op("pe", down, reads=Hids + ["wdb%d" % j for j in range(NP)], writes=["pD%d" % d])
            pid = "prev%d_%d_%d" % (b, s, hh)
            S.op("dve", lambda e: e.tensor_tensor(
                out=pv[:, s, hh * 512:(hh + 1) * 512], in0=PD[d][:], in1=pv[:, s, hh * 512:(hh + 1) * 512], op=ALU.add),
                reads=["pD%d" % d, pid], writes=[pid])

        def store(i):
            b = i % 2
            pv = pr[b]
            S.dma("prev%d" % b, lambda e: [e.dma_start(
                out=dr["out"][i * 512:(i + 1) * 512, :].rearrange("(s p) d -> p s d", p=128), in_=pv[:])],
                reads=["prev%d_%d_%d" % (b, s, hh) for s in range(4) for hh in range(2)], store=True)

        def tile(i):
            if i + 1 < NT:
                load_xT(i + 1)
            if i == 0 and NT > 1:
                load_prev(1)
            b = i % 2
            xT, H = xTr[b], Hr[b]

            def branch(jj, isval):
                col0 = jj * 128 + (1408 if isval else 0)
                J = hf * NP + jj + (22 if isval else 0)
                hidx = jj + (11 if isval else 0)
                m = nxt("M", 4)
                u = nxt("U", 4)
                U, A2 = Ur[u], A2r[u]

                def up(e):
                    r = None
                    for c in range(8):
                        r = e.matmul(PM[m][:], lhsT=wupb[:, c, col0:col0 + 128], rhs=xT[:, c, :],
                                     start=(c == 0), stop=(c == 7))
                    return r
                S.op("pe", up, reads=["xT%d" % b] + ["wupb%d" % c for c in range(8)], writes=["pM%d" % m])
                S.op("pool", lambda e: e.tensor_copy(out=U[:, 0:2], in_=hal[:, hidx, :]),
                     reads=["hal%d" % hidx], writes=["Ub%d_h" % u])
                S.op("act", lambda e: e.activation(out=U[:, 2:514], in_=PM[m][:], func=AF.Copy),
                     reads=["pM%d" % m], writes=["Ub%d_b" % u])
                S.op("act", lambda e: e.activation(out=A2[:], in_=PM[m][:], func=AF.Identity,
                                                   scale=ffw[:, J, 2:3], bias=ffb[:, J:J + 1]),
                     reads=["pM%d" % m, "ffw", "ffb%d" % hf], writes=["A2_%d" % u])
                S.op("pool", lambda e: e.tensor_copy(out=hal[:, hidx, :], in_=U[:, 512:514]),
                     reads=["Ub%d_b" % u], writes=["hal%d" % hidx])
                S.op("dve", lambda e: e.scalar_tensor_tensor(out=A2[:], in0=U[:, 1:513], scalar=ffw[:, J, 1:2], in1=A2[:],
                                                             op0=ALU.mult, op1=ALU.add),
                     reads=["Ub%d_h" % u, "Ub%d_b" % u, "A2_%d" % u, "ffw"], writes=["A2_%d" % u])
                S.op("dve", lambda e: e.scalar_tensor_tensor(out=A2[:], in0=U[:, 0:512], scalar=ffw[:, J, 0:1], in1=A2[:],
                                                             op0=ALU.mult, op1=ALU.add),
                     reads=["Ub%d_h" % u, "Ub%d_b" % u, "A2_%d" % u, "ffw"], writes=["A2_%d" % u])
                return u

            for jj in range(NP):
                ug = branch(jj, False)
                uv = branch(jj, True)
                t = nxt("t", 2)
                tb = tbr[t]
                zg, zv = A2r[ug], A2r[uv]
                S.op("act", lambda e, tb=tb, zg=zg: e.activation(out=tb[:], in_=zg[:], func=AF.Tanh, scale=0.5),
                     reads=["A2_%d" % ug], writes=["tb%d" % t])
                S.op("dve", lambda e, tb=tb, zg=zg: e.scalar_tensor_tensor(out=tb[:], in0=tb[:], scalar=1.0, in1=zg[:],
                                                                          op0=ALU.add, op1=ALU.mult),
                     reads=["tb%d" % t, "A2_%d" % ug], writes=["tb%d" % t])
                S.op("dve", lambda e, tb=tb, zv=zv, jj=jj: e.tensor_tensor(out=H[:, jj, :], in0=tb[:], in1=zv[:], op=ALU.mult),
                     reads=["tb%d" % t, "A2_%d" % uv], writes=["H%d_%d" % (b, jj)])
                if i > 0 and jj < 8:
                    down_group(i - 1, jj)
                    if jj == 7:
                        store(i - 1)
                        if i + 1 < NT:
                            load_prev(i + 1)
        load_xT(0)
        load_prev(0)
        for i in range(NT):
            tile(i)
        for gi in range(8):
            down_group(NT - 1, gi)
        store(NT - 1)
        S.dram_barrier()
        S.emit()
    nc.all_engine_barrier()


def pass2b(nc, S, T, dr, C):
    with contextlib.ExitStack() as st:
        sb, ps = _mk(nc, st)
        mk = sb("mk", [128, 24, 512], BF16)
        mm = sb("mm", [128, 24, 512], BF16)
        mst = [sb("mst%d" % i, [128, 1024], F32) for i in range(2)]
        for g in range(12):
            sid = "mst%d" % (g % 2)
            stg = mst[g % 2]
            S.dma(sid, lambda e, stg=stg, g=g: [e.dma_start(out=stg[:], in_=dr["mmask"][:, g * 1024:(g + 1) * 1024])],
                  writes=[sid])
            if g % 2 == 0:
                S.op("dve", lambda e, stg=stg, g=g: e.tensor_copy(out=mm[:, g * 2:(g + 1) * 2, :],
                                                                  in_=stg[:].rearrange("p (a b) -> p a b", a=2)),
                     reads=[sid], writes=["mm"])
            else:
                S.op("act", lambda e, stg=stg, g=g: e.activation(out=mm[:, g * 2:(g + 1) * 2, :],
                                                                 in_=stg[:].rearrange("p (a b) -> p a b", a=2), func=AF.Copy),
                     reads=[sid], writes=["mm"])
        for g in range(12):
            sid = "mst%d" % (g % 2)
            stg = mst[g % 2]
            S.dma(sid, lambda e, stg=stg, g=g: [e.dma_start(out=stg[:], in_=dr["amask"][:, g * 1024:(g + 1) * 1024])],
                  writes=[sid])
            if g % 2 == 0:
                S.op("dve", lambda e, stg=stg, g=g: e.tensor_copy(out=mk[:, g * 2:(g + 1) * 2, :],
                                                                  in_=stg[:].rearrange("p (a b) -> p a b", a=2)),
                     reads=[sid], writes=["mk"])
            else:
                S.op("act", lambda e, stg=stg, g=g: e.activation(out=mk[:, g * 2:(g + 1) * 2, :],
                                                                 in_=stg[:].rearrange("p (a b) -> p a b", a=2), func=AF.Copy),
                     reads=[sid], writes=["mk"])
        qtr = [sb("QTt%d" % i, [128, T], BF16) for i in range(1)]
        ktr = [sb("KTt%d" % i, [128, T], BF16) for i in range(1)]
        ACC = [sb("ACC%d" % i, [128, T], F32) for i in range(2)]
        OTst = sb("OTst", [128, T], BF16)
        RC = 2048 if T >= 2048 else T
        Rt = [sb("Rt%d" % i, [128, RC], F32) for i in range(2)]
        NVA = 8
        VAr = [sb("VAt%d" % i, [128, 2, 128], BF16) for i in range(NVA)]
        Pr = [sb("P%d" % i, [128, 2, 512], BF16) for i in range(4)]
        Er = [sb("E%d" % i, [128, 2, 512], BF16) for i in range(2)]
        PSs = [[ps("pSs%d_%d" % (hh, i), [128, 512], F32) for i in range(2)] for hh in range(2)]
        PO = [[ps("pO%d_%d" % (hh, i), [128, 512], F32) for i in range(2)] for hh in range(2)]
        cnt = {"S": 0, "E": 0, "V": 0, "G0": 0, "G1": 0, "R": 0, "X": 0, "K": 0}

        def nxt(k, n):
            v = cnt[k] % n
            cnt[k] += 1
            return v

        def load_qk(hp):
            b = 0
            S.dma("QTt%d" % b, lambda e: [e.dma_start(out=qtr[b][:], in_=dr["QT"][hp * 128:(hp + 1) * 128, :])],
                  writes=["QTt%d" % b])
            S.dma("KTt%d" % b, lambda e: [e.dma_start(out=ktr[b][:], in_=dr["KT"][hp * 128:(hp + 1) * 128, :])],
                  writes=["KTt%d" % b])

        def pair(hp):
            load_qk(hp)
            b = 0
            QTt, KTt = qtr[b], ktr[b]
            qid, kid = "QTt%d" % b, "KTt%d" % b
            items = []

            def accids(hh, lo, hi):
                return ["acc%d_%d" % (hh, bb) for bb in range(lo // 2048, (hi - 1) // 2048 + 1)]

            def normalise(hh, c0):
                acc = ACC[hh]
                rb = nxt("R", 2)
                Rtile = Rt[rb]
                S.op("act", lambda e: e.activation(out=Rtile[64:128, :], in_=acc[64:128, c0:c0 + RC], func=AF.Ln),
                     reads=accids(hh, c0, c0 + RC), writes=["Rl%d" % rb])
                S.op("act", lambda e: e.activation(out=Rtile[0:64, :], in_=Rtile[64:128, :], func=AF.Exp, scale=-1.0),
                     reads=["Rl%d" % rb], writes=["Rt%d" % rb])
                S.op("dve", lambda e: e.tensor_tensor(
                    out=OTst[hh * 64:(hh + 1) * 64, c0:c0 + RC], in0=acc[0:64, c0:c0 + RC], in1=Rtile[0:64, :], op=ALU.mult),
                    reads=accids(hh, c0, c0 + RC) + ["Rt%d" % rb], writes=["OTst_%d_%d" % (hh, c0)])

            def stageA(item):
                (pi, d, nb, gs, r, grp_bank, fresh, j0) = item
                js = [j for j in (j0, j0 + 1) if j < nb]
                nqs = [256 if j + 1 < nb else 128 for j in js]
                ncols = 256 * (len(js) - 1) + nqs[-1]
                vts = []
                for j in js:
                    t_lo = 128 * j * d + r
                    v = nxt("V", NVA)
                    S.dma("VAt%d" % v, lambda e, v=v, t_lo=t_lo: [e.dma_start(
                        out=VAr[v][:], in_=dr["VA"][t_lo:t_lo + 127 * d + 1:d, 2 * hp:2 * hp + 2, :])],
                        writes=["VAt%d" % v])
                    vts.append(v)
                sbk = nxt("S", 2)
                eb = nxt("E", 4)
                P = Pr[eb]

                pathB = (nxt("K", 2) == 1)

                def smm(e):
                    rr = None
                    for bi, j in enumerate(js):
                        t_lo = 128 * j * d + r
                        for hh in range(2):
                            pb = hh * 64
                            rr = e.matmul(PSs[hh][sbk][:, bi * 256:bi * 256 + nqs[bi]],
                                          lhsT=KTt[pb:pb + 64, t_lo:t_lo + 127 * d + 1:d],
                                          rhs=QTt[pb:pb + 64, t_lo:t_lo + (nqs[bi] - 1) * d + 1:d],
                                          start=True, stop=pathB)
                        if pathB:
                            continue
                        for hh in range(2):
                            m0 = (2 * hp + hh) * 3 + pi
                            rr = e.matmul(PSs[hh][sbk][:, bi * 256:bi * 256 + nqs[bi]],
                                          lhsT=C.ident[:], rhs=mk[:, m0, 0:nqs[bi]], start=False, stop=True)
                    return rr
                S.op("pe", smm, reads=[qid, kid, "mk", "ident"], writes=["pSs0_%d" % sbk, "pSs1_%d" % sbk])
                if pathB:
                    xb = nxt("X", 2)
                    E = Er[xb]
                for hh in range(2):
                    if not pathB:
                        S.op("act", lambda e, hh=hh: e.activation(
                            out=P[:, hh, 0:ncols], in_=PSs[hh][sbk][:, 0:ncols], func=AF.Exp),
                            reads=["pSs%d_%d" % (hh, sbk)], writes=["P%d_%d" % (eb, hh)])
                    else:
                        m0 = (2 * hp + hh) * 3 + pi
                        S.op("act", lambda e, hh=hh, E=E: e.activation(
                            out=E[:, hh, 0:ncols], in_=PSs[hh][sbk][:, 0:ncols], func=AF.Exp),
                            reads=["pSs%d_%d" % (hh, sbk)], writes=["E%d_%d" % (xb, hh)])
                        S.op("dve", lambda e, hh=hh, E=E, m0=m0: e.tensor_tensor(
                            out=P[:, hh, 0:ncols], in0=E[:, hh, 0:ncols], in1=mm[:, m0, 0:ncols], op=ALU.mult),
                            reads=["E%d_%d" % (xb, hh), "mm"], writes=["P%d_%d" % (eb, hh)])
                return (item, js, vts, eb)

            def stageB(ctx):
                (item, js, vts, eb) = ctx
                (pi, d, nb, gs, r, grp_bank, fresh, j0) = item
                P = Pr[eb]
                plan = []
                wr = []
                for hh in range(2):
                    for bi, j in enumerate(js):
                        parts = [(qb, half) for (qb, half) in ((j, 0), (j + 1, 1)) if qb < nb]
                        if len(parts) == 2 and parts[0][0] // gs == parts[1][0] // gs:
                            parts = [(j, 0, 2)]
                        else:
                            parts = [(qb, half, 1) for (qb, half) in parts]
                        for (qb, half, w2) in parts:
                            g = qb // gs
                            key = (hh, g)
                            if key not in grp_bank:
                                grp_bank[key] = nxt("G%d" % hh, 2)
                                fresh[key] = True
                            bank = grp_bank[key]
                            plan.append((hh, bi, half, bank, (qb % gs) * 128, fresh[key], w2))
                            fresh[key] = False
                            bid = "pO%d_%d" % (hh, bank)
                            if bid not in wr:
                                wr.append(bid)

                def pv(e):
                    rr = None
                    for (hh, bi, half, bank, col, fr, w2) in plan:
                        rr = e.matmul(PO[hh][bank][:, col:col + 128 * w2], lhsT=VAr[vts[bi]][:, hh, :],
                                      rhs=P[:, hh, bi * 256 + half * 128:bi * 256 + half * 128 + 128 * w2],
                                      start=fr, stop=False, skip_group_check=True)
                    return rr
                S.op("pe", pv, reads=["P%d_0" % eb, "P%d_1" % eb] + ["VAt%d" % v for v in vts], writes=wr)
                for j in js:
                    if j % gs != gs - 1:
                        continue
                    g = j // gs
                    c_lo = g * gs * 128 * d + r
                    ncol = gs * 128
                    c_hi = c_lo + (ncol - 1) * d + 1
                    for hh in range(2):
                        bank = grp_bank[(hh, g)]
                        bid = "pO%d_%d" % (hh, bank)
                        acc = ACC[hh]
                        aids = accids(hh, c_lo, c_hi)
                        if pi == 2:
                            S.op("act", lambda e, hh=hh, bank=bank, acc=acc, c_lo=c_lo, c_hi=c_hi: e.activation(
                                out=acc[:, c_lo:c_hi:d], in_=PO[hh][bank][:, 0:ncol], func=AF.Copy),
                                reads=[bid], writes=aids)
                        else:
                            S.op("dve", lambda e, hh=hh, bank=bank, acc=acc, c_lo=c_lo, c_hi=c_hi: e.tensor_tensor(
                                out=acc[:, c_lo:c_hi:d], in0=PO[hh][bank][:, 0:ncol],
                                in1=acc[:, c_lo:c_hi:d], op=ALU.add),
                                reads=[bid] + aids, writes=aids)
                    if pi == 0 and (c_hi % RC == 0):
                        for hh in range(2):
                            normalise(hh, c_hi - RC)

            for pi in (2, 1, 0):
                (w, d) = PATTERNS[pi]
                L = T // d
                nb = L // 128
                gs = min(4, nb)
                for r in range(d):
                    grp_bank = {}
                    fresh = {}

                    for j0 in range(0, nb, 2):
                        items.append((pi, d, nb, gs, r, grp_bank, fresh, j0))
            ctxs = {}
            LA = 3
            for k in range(len(items) + LA):
                if k < len(items):
                    ctxs[k] = stageA(items[k])
                if k - LA >= 0:
                    stageB(ctxs.pop(k - LA))
            S.dma("OTst", lambda e: [e.dma_start(out=dr["MIXT"][512 + hp * 128:512 + (hp + 1) * 128, :], in_=OTst[:])],
                  reads=["OTst_%d_%d" % (hh, c0) for hh in range(2) for c0 in range(0, T, RC)], store=True)
        for hp in range(4):
            pair(hp)
        S.dram_barrier()
        S.emit()
    nc.all_engine_barrier()


def consts(nc, S, st, dr):
    sb, ps = _mk(nc, st)
    C = Ctx()
    C.eps = sb("eps", [128, 1], F32)
    C.one = sb("one", [128, 1], F32)
    C.ident = sb("ident", [128, 128], BF16)
    C.ident32 = sb("ident32", [128, 128], F32)
    C.blk = sb("blk", [128, 128], BF16)
    C.ones = sb("ones", [128, 128], BF16)
    S.op("pool", lambda e: e.memset(C.eps[:], EPS), writes=["eps"])
    S.op("pool", lambda e: e.memset(C.one[:], 1.0), writes=["one"])
    S.op("pool", lambda e: e.memset(C.ident[:], 0.0), writes=["ident"])
    S.op("pool", lambda e: e.affine_select(out=C.ident[:], in_=C.ident[:], pattern=[[-1, 128]],
                                           compare_op=ALU.not_equal, fill=1.0, base=0, channel_multiplier=1),
         reads=["ident"], writes=["ident"])
    S.op("pool", lambda e: e.memset(C.ident32[:], 0.0), writes=["ident32"])
    S.op("pool", lambda e: e.affine_select(out=C.ident32[:], in_=C.ident32[:], pattern=[[-1, 128]],
                                           compare_op=ALU.not_equal, fill=1.0, base=0, channel_multiplier=1),
         reads=["ident32"], writes=["ident32"])
    S.op("pool", lambda e: e.memset(C.blk[:], 0.0), writes=["blk"])
    S.op("pool", lambda e: e.memset(C.blk[0:64, 0:64], 1.0), reads=["blk"], writes=["blk"])
    S.op("pool", lambda e: e.memset(C.blk[64:128, 64:128], 1.0), reads=["blk"], writes=["blk"])
    S.op("pool", lambda e: e.memset(C.ones[:], 1.0), writes=["ones"])
    return C


def build(T, stages=("p1",), debug=False):
    nc = bass.Bass("TRN2", target_bir_lowering=False)
    dr = {}

    def inp(name, shape):
        dr[name] = nc.dram_tensor(name, shape, F32, kind="ExternalInput").ap()
    inp("x", [T, 1024])
    inp("norm1_g", [1024])
    inp("w_in", [1024, 2560])
    inp("conv_w", [31, 512])
    inp("conv_b", [512])
    inp("cn_g", [512])
    inp("cn_b", [512])
    inp("q_norm_g", [64])
    inp("k_norm_g", [64])
    inp("w_out", [1024, 1024])
    inp("norm2_g", [1024])
    inp("w_up", [1024, 5632])
    inp("ffconv_w", [3, 5632])
    inp("ffconv_b", [5632])
    inp("w_down", [2816, 1024])
    inp("amask", [128, 24 * 512])
    inp("mmask", [128, 24 * 512])
    dr["out"] = nc.dram_tensor("out", [T, 1024], F32, kind="ExternalOutput").ap()
    kind = "ExternalOutput" if debug else "Internal"

    def scr(name, shape, dt):
        dr[name] = nc.dram_tensor(name, shape, dt, kind=kind).ap()
    scr("UT", [512, 32 + T], BF16)
    scr("QT", [512, T], BF16)
    scr("KT", [512, T], BF16)
    scr("VA", [T, 8, 128], BF16)
    scr("MIXT", [1024, T], BF16)
    scr("X1", [T, 1024], F32)
    scr("X1T", [1024, T], BF16)
    with contextlib.ExitStack() as st:
        S = Sched(nc, st)
        C = consts(nc, S, st, dr)
        if "p1" in stages:
            pass1(nc, S, T, dr, C)
        if "p2a" in stages:
            pass2a(nc, S, T, dr, C)
        if "p2b" in stages:
            pass2b(nc, S, T, dr, C)
        if "p2c" in stages:
            pass2c(nc, S, T, dr, C)
        if "p3" in stages:
            pass3(nc, S, T, dr, C, 0)
            pass3(nc, S, T, dr, C, 1)
    return nc


ALL_STAGES = ("p1", "p2a", "p2b", "p2c", "p3")


def kernel(**inputs):
    T = 8192
    nc = build(T, stages=ALL_STAGES)
    in_maps = [host_inputs(inputs, b, T) for b in range(8)]
    res = run_bass_kernel_spmd(nc, in_maps, core_ids=list(range(8)))
    out = np.stack([np.asarray(r["out"]).reshape(T, 1024) for r in res.results], 0)
    return out.astype(np.float32)


def attn_mask_table(additive=True):
    k = np.arange(128)[:, None]
    q = np.arange(256)[None, :]
    delta = q - k
    valid = (delta >= 0) & (delta <= 128)
    out = np.zeros((128, 24, 2, 256), np.float32)
    for h in range(8):
        slope = 2.0 ** (-(h + 1))
        for p, (w, d) in enumerate(PATTERNS):
            bias = -slope * d * np.maximum(delta, 0).astype(np.float64)
            m = np.where(valid, bias, -30000.0) if additive else np.where(valid, np.exp(bias), 0.0)
            out[:, h * 3 + p, 0] = m
            out[:, h * 3 + p, 1] = m
    return out.reshape(128, 24 * 512)


def host_inputs(inputs, b, T):
    m = {}
    for k, v in inputs.items():
        v = np.asarray(v)
        if k == "x":
            m[k] = np.ascontiguousarray(v[b, :T])
        else:
            m[k] = np.ascontiguousarray(v)
    m["amask"] = attn_mask_table(True)
    m["mmask"] = attn_mask_table(False)
    return m
```

```python
import contextlib
import numpy as np
import concourse.bass as bass
import concourse.mybir as mybir
from concourse.bass_utils import run_bass_kernel_spmd

F32 = mybir.dt.float32
BF16 = mybir.dt.bfloat16
ALU = mybir.AluOpType
AF = mybir.ActivationFunctionType

COMPUTE = ("pe", "act", "dve", "pool")
EPS = 1e-6
NH = 8
PATTERNS = ((128, 1), (512, 4), (2048, 16))


class Sched:
    def __init__(self, nc, stack):
        self.nc = nc
        self.stack = stack
        self.prog = {e: [] for e in COMPUTE + ("sp",)}
        self.cnt = {e: 0 for e in COMPUTE}
        self.sem = {e: stack.enter_context(nc.semaphore("s_" + e)) for e in COMPUTE}
        self.dma_sem = {}
        self.dma_cnt = {}
        self.last_w = {}
        self.readers = {}
        self.waited = {e: {} for e in self.prog}
        self.stores = []

    def _sem_of(self, k):
        return self.sem[k] if isinstance(k, str) else self.dma_sem[k[1]]

    def _deps(self, engine, reads, writes):
        toks = []
        for b in reads:
            if b in self.last_w:
                toks.append(self.last_w[b])
        for b in writes:
            if b in self.last_w:
                toks.append(self.last_w[b])
            toks.extend(self.readers.get(b, ()))
        best = {}
        for (k, v) in toks:
            if k == "pe" and engine == "pe":
                continue
            if best.get(k, 0) < v:
                best[k] = v
        out = []
        w = self.waited[engine]
        for k, v in best.items():
            if w.get(k, 0) >= v:
                continue
            w[k] = v
            out.append((k, v))
        return out

    def _record(self, tok, reads, writes):
        for b in writes:
            self.last_w[b] = tok
            self.readers[b] = []
        for b in reads:
            self.readers.setdefault(b, []).append(tok)

    def op(self, engine, fn, reads=(), writes=()):
        waits = self._deps(engine, reads, writes)
        self.cnt[engine] += 1
        tok = (engine, self.cnt[engine])
        self.prog[engine].append((waits, fn, engine, 1))
        self._record(tok, reads, writes)
        return tok

    def dma(self, key, fn, reads=(), writes=(), n=1, store=False):
        if key not in self.dma_sem:
            self.dma_sem[key] = self.stack.enter_context(
                self.nc.semaphore("d_%d" % len(self.dma_sem)))
            self.dma_cnt[key] = 0
        waits = self._deps("sp", reads, writes)
        self.dma_cnt[key] += n
        tok = (("dma", key), 16 * self.dma_cnt[key])
        self.prog["sp"].append((waits, fn, ("dma", key), n))
        self._record(tok, reads, writes)
        if store:
            self.stores.append(tok)
        return tok

    def dram_barrier(self):
        best = {}
        for (k, v) in self.stores:
            best[k] = max(best.get(k, 0), v)
        w = self.waited["sp"]
        waits = []
        for k, v in best.items():
            if w.get(k, 0) < v:
                w[k] = v
                waits.append((k, v))
        self.prog["sp"].append((waits, None, None, 0))
        self.stores = []

    def emit(self):
        nc = self.nc
        prog = self.prog
        self.prog = {e: [] for e in prog}
        with nc.Block() as block:
            def run(name):
                def body(e):
                    for (waits, fn, semkey, n) in prog[name]:
                        for (k, v) in waits:
                            e.wait_ge(self._sem_of(k), v)
                        if fn is None:
                            continue
                        r = fn(e)
                        if isinstance(semkey, str):
                            r.then_inc(self.sem[semkey], 1)
                        else:
                            if not isinstance(r, (list, tuple)):
                                r = [r]
                            assert len(r) == n, (len(r), n)
                            for ins in r:
                                ins.then_inc(self.dma_sem[semkey[1]], 16)
                return body
            block.tensor(run("pe"))
            block.scalar(run("act"))
            block.vector(run("dve"))
            block.gpsimd(run("pool"))
            block.sync(run("sp"))


class Ctx:
    pass


_UNIQ = [0]


def _mk(nc, st):
    _UNIQ[0] += 1
    pre = "k%d_" % _UNIQ[0]

    def sb(name, shape, dt):
        return st.enter_context(nc.sbuf_tensor(pre + name, shape, dt))

    def ps(name, shape, dt):
        return st.enter_context(nc.psum_tensor(pre + name, shape, dt))
    return sb, ps


def load_vec(S, nc, sb, name, src_ap, ncol, C, pbank, pbid):
    t = sb(name, [128, ncol], F32)
    rows = sb(name + "_r", [ncol, 128], F32)
    S.dma(name + "_r", lambda e: [e.dma_start(out=rows[:], in_=src_ap.rearrange("(c p) -> c p", p=128))],
          writes=[name + "_r"])
    S.op("pe", lambda e: e.transpose(pbank[:, 0:ncol], rows[:], C.ident32[0:ncol, 0:ncol]),
         reads=[name + "_r", "ident32"], writes=[pbid])
    S.op("dve", lambda e: e.tensor_copy(out=t[:], in_=pbank[:, 0:ncol]), reads=[pbid], writes=[name])
    return t


def pass1(nc, S, T, dr, C):
    NT = T // 512
    with contextlib.ExitStack() as st:
        sb, ps = _mk(nc, st)
        winb = sb("winb", [128, 8, 2560], BF16)
        PM = [ps("pM%d" % i, [128, 512], F32) for i in range(4)]
        g1t = load_vec(S, nc, sb, "g1t", dr["norm1_g"], 8, C, PM[0], "pM0")
        wst = [sb("wst%d" % i, [128, 2560], F32) for i in range(2)]
        for c in range(8):
            sid = "wst%d" % (c % 2)
            stg = wst[c % 2]
            S.dma(sid, lambda e, stg=stg, c=c: [e.dma_start(out=stg[:], in_=dr["w_in"][c * 128:(c + 1) * 128, :])],
                  writes=[sid])
            if c % 2 == 0:
                S.op("dve", lambda e, stg=stg, c=c: e.tensor_scalar(out=winb[:, c, :], in0=stg[:], scalar1=g1t[:, c:c + 1],
                                                                    scalar2=None, op0=ALU.mult),
                     reads=[sid, "g1t"], writes=["winb%d" % c])
            else:
                S.op("act", lambda e, stg=stg, c=c: e.activation(out=winb[:, c, :], in_=stg[:], func=AF.Identity,
                                                                 scale=g1t[:, c:c + 1]),
                     reads=[sid, "g1t"], writes=["winb%d" % c])
        gq = sb("gq", [128, 2], F32)
        S.dma("gq", lambda e: [
            e.dma_start(out=gq[0:64, 0:1], in_=dr["q_norm_g"].rearrange("(p o) -> p o", o=1)),
            e.dma_start(out=gq[64:128, 0:1], in_=dr["q_norm_g"].rearrange("(p o) -> p o", o=1)),
            e.dma_start(out=gq[0:64, 1:2], in_=dr["k_norm_g"].rearrange("(p o) -> p o", o=1)),
            e.dma_start(out=gq[64:128, 1:2], in_=dr["k_norm_g"].rearrange("(p o) -> p o", o=1)),
        ], writes=["gq0"], n=4)
        gqs = sb("gqs", [128, 2], F32)
        S.op("dve", lambda e: e.tensor_scalar(out=gqs[:, 0:1], in0=gq[:, 0:1], scalar1=0.125, scalar2=None, op0=ALU.mult),
             reads=["gq0"], writes=["gqs_a"])
        S.op("dve", lambda e: e.tensor_copy(out=gqs[:, 1:2], in_=gq[:, 1:2]), reads=["gq0"], writes=["gqs_b"])
        zt = sb("zt", [128, 4, 32], BF16)
        S.op("pool", lambda e: e.memset(zt[:], 0.0), writes=["zt"])
        S.dma("zt", lambda e: [e.dma_start(out=dr["UT"][:, 0:32].rearrange("(c p) t -> p c t", p=128), in_=zt[:])],
              reads=["zt"], store=True)

        xr = [sb("x%d" % i, [128, 4, 1024], F32) for i in range(3)]
        junk = sb("junk", [128, 1024], BF16)
        ssr = [sb("ss%d" % i, [128, 4], F32) for i in range(2)]
        lnr = [sb("ln%d" % i, [128, 4], F32) for i in range(2)]
        rsr = [sb("rs%d" % i, [128, 4], F32) for i in range(2)]
        xsr = [sb("xs%d" % i, [128, 1024], BF16) for i in range(2)]
        xTr = [sb("xT%d" % i, [128, 8, 512], BF16) for i in range(2)]
        uTr = [sb("uT%d" % i, [128, 4, 512], BF16) for i in range(2)]
        qkr = [sb("qk%d" % i, [128, 8, 512], BF16) for i in range(2)]
        var = [sb("va%d" % i, [128, 4, 8, 128], BF16) for i in range(2)]
        for i in range(2):
            S.op("pool", lambda e, i=i: e.memset(var[i][:, :, :, 64:128], 1.0), writes=["va_ones%d" % i])
        er = [sb("e%d" % i, [128, 512], F32) for i in range(2)]
        dr_ = [sb("d%d" % i, [128, 512], F32) for i in range(2)]
        sqr = [sb("sq%d" % i, [128, 512], BF16) for i in range(2)]
        lqr = [sb("lq%d" % i, [128, 512], F32) for i in range(2)]
        rqr = [sb("rq%d" % i, [128, 512], F32) for i in range(2)]
        PT = [ps("pT%d" % i, [128, 1024], BF16) for i in range(2)]
        PS_ = [ps("pS%d" % i, [128, 512], F32) for i in range(2)]
        q32r = [sb("q32_%d" % i, [128, 512], F32) for i in range(3)]
        cnt = {"T": 0, "M": 0, "S": 0, "e": 0, "q": 0, "q3": 0}

        def nxt(k, n):
            v = cnt[k] % n
            cnt[k] += 1
            return v

        WINB = ["winb%d" % c for c in range(8)]

        def load_x(i):
            xt = xr[i % 3]
            S.dma("x%d" % (i % 3), lambda e: [e.dma_start(
                out=xt[:], in_=dr["x"][i * 512:(i + 1) * 512, :].rearrange("(s p) d -> p s d", p=128))],
                writes=["x%d" % (i % 3)])

        def front(i):
            b = i % 2
            xt, ss, ln, rs, xT = xr[i % 3], ssr[b], lnr[b], rsr[b], xTr[b]
            xid = "x%d" % (i % 3)
            for s in range(4):
                S.op("act", lambda e, s=s: e.activation(out=junk[:], in_=xt[:, s, :], func=AF.Square,
                                                        accum_out=ss[:, s:s + 1]),
                     reads=[xid], writes=["junk", "ss%d_%d" % (b, s)])
            S.op("act", lambda e: e.activation(out=ln[:], in_=ss[:], func=AF.Ln, scale=1.0 / 1024, bias=C.eps[:, 0:1]),
                 reads=["ss%d_%d" % (b, s) for s in range(4)], writes=["ln%d" % b])
            S.op("act", lambda e: e.activation(out=rs[:], in_=ln[:], func=AF.Exp, scale=-0.5),
                 reads=["ln%d" % b], writes=["rs%d" % b])
            for s in range(4):
                xs = xsr[s % 2]
                xsid = "xs%d" % (s % 2)
                S.op("dve", lambda e, s=s, xs=xs: e.tensor_scalar(out=xs[:], in0=xt[:, s, :], scalar1=rs[:, s:s + 1],
                                                                 scalar2=None, op0=ALU.mult),
                     reads=[xid, "rs%d" % b], writes=[xsid])
                tb = nxt("T", 2)
                pt = PT[tb]

                def tr(e, xs=xs, pt=pt):
                    r = None
                    for c in range(8):
                        r = e.transpose(pt[:, c * 128:(c + 1) * 128], xs[:, c * 128:(c + 1) * 128], C.ident[:])
                    return r
                S.op("pe", tr, reads=[xsid, "ident"], writes=["pT%d" % tb])
                S.op("act", lambda e, s=s, pt=pt: e.activation(
                    out=xT[:, :, s * 128:(s + 1) * 128], in_=pt[:].rearrange("p (c t) -> p c t", c=8), func=AF.Copy),
                    reads=["pT%d" % tb], writes=["xT%d_%d" % (b, s)])
        def tile(i):
            if i + 2 < NT:
                load_x(i + 2)
            b = i % 2
            xT, uT, qk, va = xTr[b], uTr[b], qkr[b], var[b]
            xTids = ["xT%d_%d" % (b, s) for s in range(4)]

            def proj(fc, pm):
                def f(e):
                    r = None
                    for c in range(8):
                        r = e.matmul(pm[:], lhsT=winb[:, c, fc * 128:(fc + 1) * 128], rhs=xT[:, c, :],
                                     start=(c == 0), stop=(c == 7))
                    return r
                return f
            for c4 in range(4):
                mv = nxt("M", 4)
                mg = nxt("M", 4)
                S.op("pe", proj(4 + c4, PM[mg]), reads=xTids + WINB, writes=["pM%d" % mg])
                S.op("pe", proj(c4, PM[mv]), reads=xTids + WINB, writes=["pM%d" % mv])
                eb = nxt("e", 2)
                et, dt_ = er[eb], dr_[eb]
                S.op("act", lambda e, mg=mg, et=et: e.activation(out=et[:], in_=PM[mg][:], func=AF.Exp, scale=-1.0),
                     reads=["pM%d" % mg], writes=["e%d" % eb])
                S.op("act", lambda e, et=et, dt_=dt_: e.activation(out=dt_[:], in_=et[:], func=AF.Ln, bias=C.one[:, 0:1]),
                     reads=["e%d" % eb, "one"], writes=["d%d" % eb])
                S.op("act", lambda e, et=et, dt_=dt_: e.activation(out=et[:], in_=dt_[:], func=AF.Exp, scale=-1.0),
                     reads=["d%d" % eb], writes=["e%d" % eb])
                S.op("dve", lambda e, mv=mv, et=et, c4=c4: e.tensor_tensor(out=uT[:, c4, :], in0=PM[mv][:], in1=et[:],
                                                                         op=ALU.mult),
                     reads=["pM%d" % mv, "e%d" % eb], writes=["uT%d_%d" % (b, c4)])
            S.dma("uT%d" % b, lambda e: [e.dma_start(
                out=dr["UT"][:, 32 + i * 512:32 + (i + 1) * 512].rearrange("(c p) t -> p c t", p=128), in_=uT[:])],
                reads=["uT%d_%d" % (b, c4) for c4 in range(4)], store=True)
            if i + 1 < NT:
                front(i + 1)
            pend = []

            def qk_front(c8):
                m = nxt("M", 4)
                S.op("pe", proj(8 + c8, PM[m]), reads=xTids + WINB, writes=["pM%d" % m])
                qb = nxt("q", 2)
                sq = sqr[qb]
                S.op("act", lambda e, m=m, sq=sq: e.activation(out=sq[:], in_=PM[m][:], func=AF.Square),
                     reads=["pM%d" % m], writes=["sq%d" % qb])
                q3 = nxt("q3", 3)
                S.op("act", lambda e, m=m, q3=q3: e.activation(out=q32r[q3][:], in_=PM[m][:], func=AF.Copy),
                     reads=["pM%d" % m], writes=["q32_%d" % q3])
                pend.append((c8, q3, qb))

            def qk_back():
                (c8, m, qb) = pend.pop(0)
                sq, lq, rq = sqr[qb], lqr[qb], rqr[qb]
                sbk = nxt("S", 2)
                S.op("pe", lambda e, sq=sq, sbk=sbk: e.matmul(PS_[sbk][:], lhsT=C.blk[:], rhs=sq[:], start=True, stop=True),
                     reads=["sq%d" % qb, "blk"], writes=["pS%d" % sbk])
                S.op("act", lambda e, lq=lq, sbk=sbk: e.activation(out=lq[:], in_=PS_[sbk][:], func=AF.Ln, scale=1.0 / 64,
                                                                  bias=C.eps[:, 0:1]),
                     reads=["pS%d" % sbk], writes=["lq%d" % qb])
                S.op("act", lambda e, lq=lq, rq=rq: e.activation(out=rq[:], in_=lq[:], func=AF.Exp, scale=-0.5),
                     reads=["lq%d" % qb], writes=["rq%d" % qb])
                gcol = 0 if c8 < 4 else 1
                S.op("dve", lambda e, m=m, rq=rq, c8=c8, gcol=gcol: e.scalar_tensor_tensor(
                    out=qk[:, c8, :], in0=q32r[m][:], scalar=gqs[:, gcol:gcol + 1], in1=rq[:], op0=ALU.mult, op1=ALU.mult),
                    reads=["q32_%d" % m, "rq%d" % qb, "gqs_a", "gqs_b"], writes=["qk%d_%d" % (b, c8)])

            for c8 in range(8):
                qk_front(c8)
                if c8 > 0:
                    qk_back()
            qk_back()
            S.dma("qk%d" % b, lambda e: [
                e.dma_start(out=dr["QT"][:, i * 512:(i + 1) * 512].rearrange("(c p) t -> p c t", p=128), in_=qk[:, 0:4, :]),
                e.dma_start(out=dr["KT"][:, i * 512:(i + 1) * 512].rearrange("(c p) t -> p c t", p=128), in_=qk[:, 4:8, :]),
            ], reads=["qk%d_%d" % (b, c8) for c8 in range(8)], n=2, store=True)
            for s in range(4):
                m = nxt("M", 4)

                def vproj(e, s=s, m=m):
                    r = None
                    for c in range(8):
                        r = e.matmul(PM[m][:], lhsT=xT[:, c, s * 128:(s + 1) * 128], rhs=winb[:, c, 2048:2560],
                                     start=(c == 0), stop=(c == 7))
                    return r
                S.op("pe", vproj, reads=xTids + WINB, writes=["pM%d" % m])
                S.op("act", lambda e, s=s, m=m: e.activation(out=va[:, s, :, 0:64],
                                                             in_=PM[m][:].rearrange("p (h e) -> p h e", h=8), func=AF.Copy),
                     reads=["pM%d" % m, "va_ones%d" % b], writes=["va%d_%d" % (b, s)])
            S.dma("va%d" % b, lambda e: [e.dma_start(
                out=dr["VA"][i * 512:(i + 1) * 512, :, :].rearrange("(s p) h e -> p s h e", p=128), in_=va[:])],
                reads=["va%d_%d" % (b, s) for s in range(4)], store=True)
        load_x(0)
        if NT > 1:
            load_x(1)
        front(0)
        for i in range(NT):
            tile(i)
        S.dram_barrier()
        S.emit()
    nc.all_engine_barrier()


def pass2a(nc, S, T, dr, C):
    NT = T // 512
    with contextlib.ExitStack() as st:
        sb, ps = _mk(nc, st)
        PM = [ps("pM%d" % i, [128, 512], F32) for i in range(4)]
        cw = sb("cw", [128, 4, 31], F32)
        cwr = sb("cwr", [31, 512], F32)
        S.dma("cwr", lambda e: [e.dma_start(out=cwr[:], in_=dr["conv_w"][:, :])], writes=["cwr"])

        def cwT(e):
            r = None
            for c in range(4):
                r = e.transpose(PM[1][:, c * 31:(c + 1) * 31], cwr[:, c * 128:(c + 1) * 128], C.ident32[0:31, 0:31])
            return r
        S.op("pe", cwT, reads=["cwr", "ident32"], writes=["pM1"])
        S.op("dve", lambda e: e.tensor_copy(out=cw[:], in_=PM[1][:, 0:124].rearrange("p (c k) -> p c k", c=4)),
             reads=["pM1"], writes=["cw"])
        cb = load_vec(S, nc, sb, "cb", dr["conv_b"], 4, C, PM[0], "pM0")
        cg = load_vec(S, nc, sb, "cg", dr["cn_g"], 4, C, PM[2], "pM2")
        cnb = load_vec(S, nc, sb, "cnb", dr["cn_b"], 4, C, PM[3], "pM3")
        ncg = sb("ncg", [128, 4], F32)
        ncb = sb("ncb", [128, 4], F32)
        S.op("dve", lambda e: e.tensor_scalar(out=ncg[:], in0=cg[:], scalar1=-1.0, scalar2=None, op0=ALU.mult),
             reads=["cg"], writes=["ncg"])
        S.op("dve", lambda e: e.tensor_scalar(out=ncb[:], in0=cnb[:], scalar1=-1.0, scalar2=None, op0=ALU.mult),
             reads=["cnb"], writes=["ncb"])
        diag = sb("diag", [128, 124, 128], BF16)
        for c in range(4):
            for k in range(31):
                if k % 2 == 0:
                    S.op("dve", lambda e, c=c, k=k: e.tensor_scalar(out=diag[:, c * 31 + k, :], in0=C.ident[:],
                                                                    scalar1=cw[:, c, k:k + 1], scalar2=None, op0=ALU.mult),
                         reads=["cw", "ident"], writes=["diag_%d_%d" % (c, k)])
                else:
                    S.op("act", lambda e, c=c, k=k: e.activation(out=diag[:, c * 31 + k, :], in_=C.ident[:],
                                                                 func=AF.Identity, scale=cw[:, c, k:k + 1]),
                         reads=["cw", "ident"], writes=["diag_%d_%d" % (c, k)])
        diag_ids = ["diag_%d_%d" % (c, k) for c in range(4) for k in range(31)]
        ur = [sb("U%d" % i, [128, 4, 544], BF16) for i in range(2)]
        y32r = [sb("y32_%d" % i, [128, 4, 512], F32) for i in range(2)]
        ybr = [sb("yb_%d" % i, [128, 4, 512], BF16) for i in range(2)]
        ysr = [sb("ys_%d" % i, [128, 4, 512], BF16) for i in range(2)]
        aTr = [sb("aT_%d" % i, [128, 4, 512], BF16) for i in range(2)]
        mean = sb("mean", [128, 512], F32)
        msq = sb("msq", [128, 512], F32)
        varr = sb("var", [128, 512], F32)
        lv = sb("lv", [128, 512], F32)
        rstd = sb("rstd", [128, 512], F32)
        er = [sb("ce%d" % i, [128, 512], F32) for i in range(2)]
        dd = [sb("cd%d" % i, [128, 512], F32) for i in range(2)]
        PS1 = ps("pS1", [128, 512], F32)
        PS2 = ps("pS2", [128, 512], F32)
        cnt = {"M": 0, "e": 0}

        def nxt(k, n):
            v = cnt[k] % n
            cnt[k] += 1
            return v

        def load_u(i):
            U = ur[i % 2]
            S.dma("U%d" % (i % 2), lambda e: [e.dma_start(
                out=U[:, :, 0:542], in_=dr["UT"][:, 2 + i * 512:2 + i * 512 + 542].rearrange("(c p) t -> p c t", p=128))],
                writes=["U%d" % (i % 2)])

        def tile(i):
            if i + 1 < NT:
                load_u(i + 1)
            b = i % 2
            U, y32, yb, ys, aT = ur[b], y32r[b], ybr[b], ysr[b], aTr[b]
            for c in range(4):
                m = nxt("M", 4)

                def conv(e, c=c, m=m):
                    r = None
                    for k in range(31):
                        r = e.matmul(PM[m][:], lhsT=diag[:, c * 31 + k, :], rhs=U[:, c, k:k + 512],
                                     start=(k == 0), stop=(k == 30))
                    return r
                S.op("pe", conv, reads=["U%d" % b] + diag_ids[c * 31:(c + 1) * 31], writes=["pM%d" % m])
                S.op("act", lambda e, c=c, m=m: e.activation(out=y32[:, c, :], in_=PM[m][:], func=AF.Identity,
                                                             bias=cb[:, c:c + 1]),
                     reads=["pM%d" % m, "cb"], writes=["y32_%d_%d" % (b, c)])
                S.op("act", lambda e, c=c, m=m: e.activation(out=yb[:, c, :], in_=PM[m][:], func=AF.Identity,
                                                             bias=cb[:, c:c + 1]),
                     reads=["pM%d" % m, "cb"], writes=["yb_%d_%d" % (b, c)])
                S.op("act", lambda e, c=c, m=m: e.activation(out=ys[:, c, :], in_=PM[m][:], func=AF.Square,
                                                             bias=cb[:, c:c + 1]),
                     reads=["pM%d" % m, "cb"], writes=["ys_%d_%d" % (b, c)])

            def st1(e):
                r = None
                for c in range(4):
                    r = e.matmul(PS1[:], lhsT=C.ones[:], rhs=yb[:, c, :], start=(c == 0), stop=(c == 3))
                return r

            def st2(e):
                r = None
                for c in range(4):
                    r = e.matmul(PS2[:], lhsT=C.ones[:], rhs=ys[:, c, :], start=(c == 0), stop=(c == 3))
                return r
            S.op("pe", st1, reads=["yb_%d_%d" % (b, c) for c in range(4)] + ["ones"], writes=["pS1"])
            S.op("pe", st2, reads=["ys_%d_%d" % (b, c) for c in range(4)] + ["ones"], writes=["pS2"])
            S.op("dve", lambda e: e.tensor_scalar(out=mean[:], in0=PS1[:], scalar1=1.0 / 512, scalar2=None, op0=ALU.mult),
                 reads=["pS1"], writes=["mean"])
            S.op("dve", lambda e: e.tensor_tensor(out=msq[:], in0=mean[:], in1=mean[:], op=ALU.mult),
                 reads=["mean"], writes=["msq"])
            S.op("dve", lambda e: e.scalar_tensor_tensor(out=varr[:], in0=PS2[:], scalar=1.0 / 512, in1=msq[:],
                                                         op0=ALU.mult, op1=ALU.subtract),
                 reads=["pS2", "msq"], writes=["var"])
            S.op("act", lambda e: e.activation(out=lv[:], in_=varr[:], func=AF.Ln, bias=C.eps[:, 0:1]),
                 reads=["var", "eps"], writes=["lv"])
            S.op("act", lambda e: e.activation(out=rstd[:], in_=lv[:], func=AF.Exp, scale=-0.5),
                 reads=["lv"], writes=["rstd"])
            for c in range(4):
                yid = "y32_%d_%d" % (b, c)
                eb = nxt("e", 2)
                et, dt_ = er[eb], dd[eb]
                S.op("dve", lambda e, c=c: e.tensor_tensor(out=y32[:, c, :], in0=y32[:, c, :], in1=mean[:], op=ALU.subtract),
                     reads=[yid, "mean"], writes=[yid])
                S.op("dve", lambda e, c=c: e.tensor_tensor(out=y32[:, c, :], in0=y32[:, c, :], in1=rstd[:], op=ALU.mult),
                     reads=[yid, "rstd"], writes=[yid])
                S.op("act", lambda e, c=c, et=et: e.activation(out=et[:], in_=y32[:, c, :], func=AF.Exp,
                                                               scale=ncg[:, c:c + 1], bias=ncb[:, c:c + 1]),
                     reads=[yid, "ncg", "ncb"], writes=["ce%d" % eb])
                S.op("dve", lambda e, c=c: e.tensor_scalar(out=y32[:, c, :], in0=y32[:, c, :], scalar1=cg[:, c:c + 1],
                                                           scalar2=cnb[:, c:c + 1], op0=ALU.mult, op1=ALU.add),
                     reads=[yid, "cg", "cnb", "ce%d" % eb], writes=[yid])
                S.op("act", lambda e, et=et, dt_=dt_: e.activation(out=dt_[:], in_=et[:], func=AF.Ln, bias=C.one[:, 0:1]),
                     reads=["ce%d" % eb, "one"], writes=["cd%d" % eb])
                S.op("act", lambda e, et=et, dt_=dt_: e.activation(out=et[:], in_=dt_[:], func=AF.Exp, scale=-1.0),
                     reads=["cd%d" % eb], writes=["ce%d" % eb])
                S.op("dve", lambda e, c=c, et=et: e.tensor_tensor(out=aT[:, c, :], in0=y32[:, c, :], in1=et[:], op=ALU.mult),
                     reads=[yid, "ce%d" % eb], writes=["aT_%d_%d" % (b, c)])
            S.dma("aT%d" % b, lambda e: [e.dma_start(
                out=dr["MIXT"][0:512, i * 512:(i + 1) * 512].rearrange("(c p) t -> p c t", p=128), in_=aT[:])],
                reads=["aT_%d_%d" % (b, c) for c in range(4)], store=True)
        load_u(0)
        for i in range(NT):
            tile(i)
        S.dram_barrier()
        S.emit()
    nc.all_engine_barrier()


def pass2c(nc, S, T, dr, C):
    NT = T // 512
    with contextlib.ExitStack() as st:
        sb, ps = _mk(nc, st)
        woutb = sb("woutb", [128, 8, 1024], BF16)
        wst = [sb("wost%d" % i, [128, 1024], F32) for i in range(2)]
        for c in range(8):
            sid = "wost%d" % (c % 2)
            stg = wst[c % 2]
            S.dma(sid, lambda e, stg=stg, c=c: [e.dma_start(out=stg[:], in_=dr["w_out"][c * 128:(c + 1) * 128, :])],
                  writes=[sid])
            if c % 2 == 0:
                S.op("dve", lambda e, stg=stg, c=c: e.tensor_copy(out=woutb[:, c, :], in_=stg[:]),
                     reads=[sid], writes=["woutb%d" % c])
            else:
                S.op("act", lambda e, stg=stg, c=c: e.activation(out=woutb[:, c, :], in_=stg[:], func=AF.Copy),
                     reads=[sid], writes=["woutb%d" % c])
        mr = [sb("mix%d" % i, [128, 8, 1024], BF16) for i in range(2)]
        xr = [sb("x%d" % i, [128, 4, 1024], F32) for i in range(2)]
        x1r = [sb("x1_%d" % i, [128, 4, 1024], F32) for i in range(2)]
        x1Tr = [sb("x1T%d" % i, [128, 8, 1024], BF16) for i in range(2)]
        junk = sb("junk", [128, 1024], BF16)
        ssr = [sb("ss%d" % i, [128, 4], F32) for i in range(2)]
        lnr = [sb("ln%d" % i, [128, 4], F32) for i in range(2)]
        rsr = [sb("rs%d" % i, [128, 4], F32) for i in range(2)]
        xsr = [sb("xs%d" % i, [128, 1024], BF16) for i in range(2)]
        PT = [ps("pT%d" % i, [128, 1024], BF16) for i in range(2)]
        PM = [ps("pM%d" % i, [128, 512], F32) for i in range(4)]
        cnt = {"M": 0, "T": 0}

        def nxt(k, n):
            v = cnt[k] % n
            cnt[k] += 1
            return v

        def load(i):
            b = i % 2
            xt = xr[b]
            if i % 2 == 0:
                sI = i // 2
                mixs = mr[sI % 2]
                S.dma("mix%d" % (sI % 2), lambda e: [e.dma_start(
                    out=mixs[:], in_=dr["MIXT"][:, sI * 1024:(sI + 1) * 1024].rearrange("(c p) t -> p c t", p=128))],
                    writes=["mix%d" % (sI % 2)])
            S.dma("x%d" % b, lambda e: [e.dma_start(
                out=xt[:], in_=dr["x"][i * 512:(i + 1) * 512, :].rearrange("(s p) d -> p s d", p=128))],
                writes=["x%d" % b])

        def tile(i):
            if i + 1 < NT:
                load(i + 1)
            b = i % 2
            sI, hf2 = i // 2, i % 2
            mb = sI % 2
            mix, xt, x1, x1T, ss, ln, rs = mr[mb], xr[b], x1r[b], x1Tr[mb], ssr[b], lnr[b], rsr[b]
            o5 = hf2 * 512
            for s in range(4):
                for hh in range(2):
                    m = nxt("M", 4)

                    def mm(e, s=s, hh=hh, m=m):
                        r = None
                        for c in range(8):
                            r = e.matmul(PM[m][:], lhsT=mix[:, c, o5 + s * 128:o5 + (s + 1) * 128],
                                         rhs=woutb[:, c, hh * 512:(hh + 1) * 512], start=(c == 0), stop=(c == 7))
                        return r
                    S.op("pe", mm, reads=["mix%d" % mb] + ["woutb%d" % c for c in range(8)], writes=["pM%d" % m])
                    S.op("dve", lambda e, s=s, hh=hh, m=m: e.tensor_tensor(
                        out=x1[:, s, hh * 512:(hh + 1) * 512], in0=PM[m][:], in1=xt[:, s, hh * 512:(hh + 1) * 512], op=ALU.add),
                        reads=["pM%d" % m, "x%d" % b], writes=["x1_%d_%d_%d" % (b, s, hh)])
                S.op("act", lambda e, s=s: e.activation(out=junk[:], in_=x1[:, s, :], func=AF.Square,
                                                        accum_out=ss[:, s:s + 1]),
                     reads=["x1_%d_%d_0" % (b, s), "x1_%d_%d_1" % (b, s)], writes=["junk", "ss%d_%d" % (b, s)])
            x1ids = ["x1_%d_%d_%d" % (b, s, hh) for s in range(4) for hh in range(2)]
            S.dma("x1_%d" % b, lambda e: [e.dma_start(
                out=dr["X1"][i * 512:(i + 1) * 512, :].rearrange("(s p) d -> p s d", p=128), in_=x1[:])],
                reads=x1ids, store=True)
            S.op("act", lambda e: e.activation(out=ln[:], in_=ss[:], func=AF.Ln, scale=1.0 / 1024, bias=C.eps[:, 0:1]),
                 reads=["ss%d_%d" % (b, s) for s in range(4)] + ["eps"], writes=["ln%d" % b])
            S.op("act", lambda e: e.activation(out=rs[:], in_=ln[:], func=AF.Exp, scale=-0.5),
                 reads=["ln%d" % b], writes=["rs%d" % b])
            for s in range(4):
                xs = xsr[s % 2]
                xsid = "xs%d" % (s % 2)
                S.op("dve", lambda e, s=s, xs=xs: e.tensor_scalar(out=xs[:], in0=x1[:, s, :], scalar1=rs[:, s:s + 1],
                                                                 scalar2=None, op0=ALU.mult),
                     reads=["x1_%d_%d_0" % (b, s), "x1_%d_%d_1" % (b, s), "rs%d" % b], writes=[xsid])
                tb = nxt("T", 2)
                pt = PT[tb]

                def tr(e, xs=xs, pt=pt):
                    r = None
                    for c in range(8):
                        r = e.transpose(pt[:, c * 128:(c + 1) * 128], xs[:, c * 128:(c + 1) * 128], C.ident[:])
                    return r
                S.op("pe", tr, reads=[xsid, "ident"], writes=["pT%d" % tb])
                S.op("act", lambda e, s=s, pt=pt: e.activation(
                    out=x1T[:, :, o5 + s * 128:o5 + (s + 1) * 128], in_=pt[:].rearrange("p (c t) -> p c t", c=8), func=AF.Copy),
                    reads=["pT%d" % tb], writes=["x1T%d_%d_%d" % (mb, hf2, s)])
            if hf2 == 1:
                S.dma("x1T%d" % mb, lambda e: [e.dma_start(
                    out=dr["X1T"][:, sI * 1024:(sI + 1) * 1024].rearrange("(c p) t -> p c t", p=128), in_=x1T[:])],
                    reads=["x1T%d_%d_%d" % (mb, h2, s) for h2 in range(2) for s in range(4)], store=True)
        load(0)
        for i in range(NT):
            tile(i)
        S.dram_barrier()
        S.emit()
    nc.all_engine_barrier()


def pass3(nc, S, T, dr, C, hf):
    NT = T // 512
    NP = 11
    with contextlib.ExitStack() as st:
        sb, ps = _mk(nc, st)
        PM = [ps("pM%d" % i, [128, 512], F32) for i in range(4)]
        g2t = load_vec(S, nc, sb, "g2t%d" % hf, dr["norm2_g"], 8, C, PM[0], "pM0")
        ffb = load_vec(S, nc, sb, "ffb%d" % hf, dr["ffconv_b"], 44, C, PM[1], "pM1")
        ffw = sb("ffw", [128, 44, 3], F32)
        ffr = sb("ffr", [44, 3, 128], F32)
        S.dma("ffr%d" % hf, lambda e: [e.dma_start(out=ffr[:, k, :], in_=dr["ffconv_w"][k, :].rearrange("(j p) -> j p", p=128))
                                       for k in range(3)], writes=["ffr"], n=3)

        def ffT(e):
            r = None
            for k in range(3):
                r = e.transpose(PM[2][:, k * 44:(k + 1) * 44], ffr[:, k, :], C.ident32[0:44, 0:44])
            return r
        S.op("pe", ffT, reads=["ffr", "ident32"], writes=["pM2"])
        S.op("dve", lambda e: e.tensor_copy(out=ffw[:].rearrange("p j k -> p k j"),
                                            in_=PM[2][:, 0:132].rearrange("p (k j) -> p k j", k=3)),
             reads=["pM2"], writes=["ffw"])
        wupb = sb("wupb", [128, 8, 2816], BF16)
        wdb = sb("wdb", [128, NP, 1024], BF16)
        ust = [sb("ust%d" % i, [128, 2816], F32) for i in range(2)]
        dst = [sb("dst%d" % i, [128, 1024], F32) for i in range(2)]
        g2id = "g2t%d" % hf
        for c in range(8):
            sid = "ust%d" % (c % 2)
            stg = ust[c % 2]
            S.dma(sid, lambda e, stg=stg, c=c: [
                e.dma_start(out=stg[:, 0:1408], in_=dr["w_up"][c * 128:(c + 1) * 128, hf * 1408:(hf + 1) * 1408]),
                e.dma_start(out=stg[:, 1408:2816], in_=dr["w_up"][c * 128:(c + 1) * 128, 2816 + hf * 1408:2816 + (hf + 1) * 1408]),
            ], writes=[sid], n=2)
            if c % 2 == 0:
                S.op("dve", lambda e, stg=stg, c=c: e.tensor_scalar(out=wupb[:, c, :], in0=stg[:], scalar1=g2t[:, c:c + 1],
                                                                    scalar2=None, op0=ALU.mult),
                     reads=[sid, g2id], writes=["wupb%d" % c])
            else:
                S.op("act", lambda e, stg=stg, c=c: e.activation(out=wupb[:, c, :], in_=stg[:], func=AF.Identity,
                                                                 scale=g2t[:, c:c + 1]),
                     reads=[sid, g2id], writes=["wupb%d" % c])
        for j in range(NP):
            sid = "dst%d" % (j % 2)
            stg = dst[j % 2]
            r0 = (hf * NP + j) * 128
            S.dma(sid, lambda e, stg=stg, r0=r0: [e.dma_start(out=stg[:], in_=dr["w_down"][r0:r0 + 128, :])], writes=[sid])
            if j % 2 == 0:
                S.op("act", lambda e, stg=stg, j=j: e.activation(out=wdb[:, j, :], in_=stg[:], func=AF.Identity, scale=0.5),
                     reads=[sid], writes=["wdb%d" % j])
            else:
                S.op("dve", lambda e, stg=stg, j=j: e.tensor_scalar(out=wdb[:, j, :], in0=stg[:], scalar1=0.5, scalar2=None,
                                                                    op0=ALU.mult),
                     reads=[sid], writes=["wdb%d" % j])
        hal = sb("hal", [128, 22, 2], BF16)
        S.op("pool", lambda e: e.memset(hal[:], 0.0), writes=["hal%d" % k for k in range(22)])
        xTr = [sb("xT%d" % i, [128, 8, 512], BF16) for i in range(2)]
        pr = [sb("prev%d" % i, [128, 4, 1024], F32) for i in range(2)]
        Hr = [sb("H%d" % i, [128, NP, 512], BF16) for i in range(2)]
        Ur = [sb("Ub%d" % i, [128, 514], BF16) for i in range(4)]
        A2r = [sb("A2_%d" % i, [128, 512], F32) for i in range(4)]
        tbr = [sb("tb%d" % i, [128, 512], F32) for i in range(2)]
        PD = [ps("pD%d" % i, [128, 512], F32) for i in range(2)]
        cnt = {"M": 0, "D": 0, "U": 0, "t": 0}
        prev_src = dr["X1"] if hf == 0 else dr["out"]

        def nxt(k, n):
            v = cnt[k] % n
            cnt[k] += 1
            return v

        def load_xT(i):
            b = i % 2
            xT = xTr[b]
            S.dma("xT%d" % b, lambda e: [e.dma_start(
                out=xT[:], in_=dr["X1T"][:, i * 512:(i + 1) * 512].rearrange("(c p) t -> p c t", p=128))],
                writes=["xT%d" % b])

        def load_prev(i):
            b = i % 2
            pv = pr[b]
            S.dma("prev%d" % b, lambda e: [e.dma_start(
                out=pv[:], in_=prev_src[i * 512:(i + 1) * 512, :].rearrange("(s p) d -> p s d", p=128))],
                writes=["prev%d_%d_%d" % (b, s, hh) for s in range(4) for hh in range(2)])

        def down_group(i, gi):
            b = i % 2
            pv, H = pr[b], Hr[b]
            s, hh = gi // 2, gi % 2
            Hids = ["H%d_%d" % (b, jj) for jj in range(NP)]
            d = nxt("D", 2)

            def down(e):
                r = None
                for jj in range(NP):
                    r = e.matmul(PD[d][:], lhsT=H[:, jj, s * 128:(s + 1) * 128],
                                 rhs=wdb[:, jj, hh * 512:(hh + 1) * 512], start=(jj == 0), stop=(jj == NP - 1))
                return r
            S.op("pe", down, reads=Hids + ["wdb%d" % j for j in range(NP)], writes=["pD%d" % d])
            pid = "prev%d_%d_%d" % (b, s, hh)
            S.op("dve", lambda e: e.tensor_tensor(
                out=pv[:, s, hh * 512:(hh + 1) * 512], in0=PD[d][:], in1=pv[:, s, hh * 512:(hh + 1) * 512], op=ALU.add),
                reads=["pD%d" % d, pid], writes=[pid])

        def store(i):
            b = i % 2
            pv = pr[b]
            S.dma("prev%d" % b, lambda e: [e.dma_start(
                out=dr["out"][i * 512:(i + 1) * 512, :].rearrange("(s p) d -> p s d", p=128), in_=pv[:])],
                reads=["prev%d_%d_%d" % (b, s, hh) for s in range(4) for hh in range(2)], store=True)

        def tile(i):
            if i + 1 < NT:
                load_xT(i + 1)
            if i == 0 and NT > 1:
                load_prev(1)
            b = i % 2
            xT, H = xTr[b], Hr[b]

            def branch(jj, isval):
                col0 = jj * 128 + (1408 if isval else 0)
                J = hf * NP + jj + (22 if isval else 0)
                hidx = jj + (11 if isval else 0)
                m = nxt("M", 4)
                u = nxt("U", 4)
                U, A2 = Ur[u], A2r[u]

                def up(e):
                    r = None
                    for c in range(8):
                        r = e.matmul(PM[m][:], lhsT=wupb[:, c, col0:col0 + 128], rhs=xT[:, c, :],
                                     start=(c == 0), stop=(c == 7))
                    return r
                S.op("pe", up, reads=["xT%d" % b] + ["wupb%d" % c for c in range(8)], writes=["pM%d" % m])
                S.op("pool", lambda e: e.tensor_copy(out=U[:, 0:2], in_=hal[:, hidx, :]),
                     reads=["hal%d" % hidx], writes=["Ub%d_h" % u])
                S.op("act", lambda e: e.activation(out=U[:, 2:514], in_=PM[m][:], func=AF.Copy),
                     reads=["pM%d" % m], writes=["Ub%d_b" % u])
                S.op("act", lambda e: e.activation(out=A2[:], in_=PM[m][:], func=AF.Identity,
                                                   scale=ffw[:, J, 2:3], bias=ffb[:, J:J + 1]),
                     reads=["pM%d" % m, "ffw", "ffb%d" % hf], writes=["A2_%d" % u])
                S.op("pool", lambda e: e.tensor_copy(out=hal[:, hidx, :], in_=U[:, 512:514]),
                     reads=["Ub%d_b" % u], writes=["hal%d" % hidx])
                S.op("dve", lambda e: e.scalar_tensor_tensor(out=A2[:], in0=U[:, 1:513], scalar=ffw[:, J, 1:2], in1=A2[:],
                                                             op0=ALU.mult, op1=ALU.add),
                     reads=["Ub%d_h" % u, "Ub%d_b" % u, "A2_%d" % u, "ffw"], writes=["A2_%d" % u])
                S.op("dve", lambda e: e.scalar_tensor_tensor(out=A2[:], in0=U[:, 0:512], scalar=ffw[:, J, 0:1], in1=A2[:],
                                                             op0=ALU.mult, op1=ALU.add),
                     reads=["Ub%d_h" % u, "Ub%d_b" % u, "A2_%d" % u, "ffw"], writes=["A2_%d" % u])
                return u

            for jj in range(NP):
                ug = branch(jj, False)
                uv = branch(jj, True)
                t = nxt("t", 2)
                tb = tbr[t]
                zg, zv = A2r[ug], A2r[uv]
                S.op("act", lambda e, tb=tb, zg=zg: e.activation(out=tb[:], in_=zg[:], func=AF.Tanh, scale=0.5),
                     reads=["A2_%d" % ug], writes=["tb%d" % t])
                S.op("dve", lambda e, tb=tb, zg=zg: e.scalar_tensor_tensor(out=tb[:], in0=tb[:], scalar=1.0, in1=zg[:],
                                                                          op0=ALU.add, op1=ALU.mult),
                     reads=["tb%d" % t, "A2_%d" % ug], writes=["tb%d" % t])
                S.op("dve", lambda e, tb=tb, zv=zv, jj=jj: e.tensor_tensor(out=H[:, jj, :], in0=tb[:], in1=zv[:], op=ALU.mult),
                     reads=["tb%d" % t, "A2_%d" % uv], writes=["H%d_%d" % (b, jj)])
                if i > 0 and jj < 8:
                    down_group(i - 1, jj)
                    if jj == 7:
                        store(i - 1)
                        if i + 1 < NT:
                            load_prev(i + 1)
        load_xT(0)
        load_prev(0)
        for i in range(NT):
            tile(i)
        for gi in range(8):
            down_group(NT - 1, gi)
        store(NT - 1)
        S.dram_barrier()
        S.emit()
    nc.all_engine_barrier()


def pass2b(nc, S, T, dr, C):
    with contextlib.ExitStack() as st:
        sb, ps = _mk(nc, st)
        mk = None
        mm = sb("mm", [128, 24, 512], BF16)
        mst = [sb("mst%d" % i, [128, 1024], F32) for i in range(2)]
        for g in range(12):
            sid = "mst%d" % (g % 2)
            stg = mst[g % 2]
            S.dma(sid, lambda e, stg=stg, g=g: [e.dma_start(out=stg[:], in_=dr["mmask"][:, g * 1024:(g + 1) * 1024])],
                  writes=[sid])
            if g % 2 == 0:
                S.op("dve", lambda e, stg=stg, g=g: e.tensor_copy(out=mm[:, g * 2:(g + 1) * 2, :],
                                                                  in_=stg[:].rearrange("p (a b) -> p a b", a=2)),
                     reads=[sid], writes=["mm"])
            else:
                S.op("act", lambda e, stg=stg, g=g: e.activation(out=mm[:, g * 2:(g + 1) * 2, :],
                                                                 in_=stg[:].rearrange("p (a b) -> p a b", a=2), func=AF.Copy),
                     reads=[sid], writes=["mm"])
        qtr = [sb("QTt%d" % i, [128, T], BF16) for i in range(1)]
        ktr = [sb("KTt%d" % i, [128, T], BF16) for i in range(1)]
        ACC = [sb("ACC%d" % i, [128, T], F32) for i in range(2)]
        OTst = sb("OTst", [128, T], BF16)
        RC = 2048 if T >= 2048 else T
        Rt = [sb("Rt%d" % i, [128, RC], F32) for i in range(2)]
        NVA = 8
        VAr = [sb("VAt%d" % i, [128, 2, 128], BF16) for i in range(NVA)]
        Pr = [sb("P%d" % i, [128, 2, 512], BF16) for i in range(4)]
        Er = [sb("E%d" % i, [128, 2, 512], BF16) for i in range(4)]
        PSs = [[ps("pSs%d_%d" % (hh, i), [128, 512], F32) for i in range(2)] for hh in range(2)]
        PO = [[ps("pO%d_%d" % (hh, i), [128, 512], F32) for i in range(2)] for hh in range(2)]
        cnt = {"S": 0, "E": 0, "V": 0, "G0": 0, "G1": 0, "R": 0, "X": 0, "K": 0}

        def nxt(k, n):
            v = cnt[k] % n
            cnt[k] += 1
            return v

        def load_qk(hp):
            b = 0
            S.dma("QTt%d" % b, lambda e: [e.dma_start(out=qtr[b][:], in_=dr["QT"][hp * 128:(hp + 1) * 128, :])],
                  writes=["QTt%d" % b])
            S.dma("KTt%d" % b, lambda e: [e.dma_start(out=ktr[b][:], in_=dr["KT"][hp * 128:(hp + 1) * 128, :])],
                  writes=["KTt%d" % b])

        def pair(hp):
            load_qk(hp)
            b = 0
            QTt, KTt = qtr[b], ktr[b]
            qid, kid = "QTt%d" % b, "KTt%d" % b
            items = []

            def accids(hh, lo, hi):
                return ["acc%d_%d" % (hh, bb) for bb in range(lo // 2048, (hi - 1) // 2048 + 1)]

            def normalise(hh, c0):
                acc = ACC[hh]
                rb = nxt("R", 2)
                Rtile = Rt[rb]
                S.op("act", lambda e: e.activation(out=Rtile[64:128, :], in_=acc[64:128, c0:c0 + RC], func=AF.Ln),
                     reads=accids(hh, c0, c0 + RC), writes=["Rl%d" % rb])
                S.op("act", lambda e: e.activation(out=Rtile[0:64, :], in_=Rtile[64:128, :], func=AF.Exp, scale=-1.0),
                     reads=["Rl%d" % rb], writes=["Rt%d" % rb])
                S.op("dve", lambda e: e.tensor_tensor(
                    out=OTst[hh * 64:(hh + 1) * 64, c0:c0 + RC], in0=acc[0:64, c0:c0 + RC], in1=Rtile[0:64, :], op=ALU.mult),
                    reads=accids(hh, c0, c0 + RC) + ["Rt%d" % rb], writes=["OTst_%d_%d" % (hh, c0)])

            def stageA(item):
                (pi, d, nb, gs, r, grp_bank, fresh, j0) = item
                js = [j for j in (j0, j0 + 1) if j < nb]
                nqs = [256 if j + 1 < nb else 128 for j in js]
                ncols = 256 * (len(js) - 1) + nqs[-1]
                vts = []
                for j in js:
                    t_lo = 128 * j * d + r
                    v = nxt("V", NVA)
                    S.dma("VAt%d" % v, lambda e, v=v, t_lo=t_lo: [e.dma_start(
                        out=VAr[v][:], in_=dr["VA"][t_lo:t_lo + 127 * d + 1:d, 2 * hp:2 * hp + 2, :])],
                        writes=["VAt%d" % v])
                    vts.append(v)
                sbk = nxt("S", 2)
                eb = nxt("E", 4)
                P = Pr[eb]

                pathB = True

                def smm(e):
                    rr = None
                    for bi, j in enumerate(js):
                        t_lo = 128 * j * d + r
                        for hh in range(2):
                            pb = hh * 64
                            rr = e.matmul(PSs[hh][sbk][:, bi * 256:bi * 256 + nqs[bi]],
                                          lhsT=KTt[pb:pb + 64, t_lo:t_lo + 127 * d + 1:d],
                                          rhs=QTt[pb:pb + 64, t_lo:t_lo + (nqs[bi] - 1) * d + 1:d],
                                          start=True, stop=pathB)
                        if pathB:
                            continue
                        for hh in range(2):
                            m0 = (2 * hp + hh) * 3 + pi
                            rr = e.matmul(PSs[hh][sbk][:, bi * 256:bi * 256 + nqs[bi]],
                                          lhsT=C.ident[:], rhs=mk[:, m0, 0:nqs[bi]], start=False, stop=True)
                    return rr
                S.op("pe", smm, reads=[qid, kid], writes=["pSs0_%d" % sbk, "pSs1_%d" % sbk])
                if pathB:
                    xb = nxt("X", 4)
                    E = Er[xb]
                for hh in range(2):
                    if not pathB:
                        S.op("act", lambda e, hh=hh: e.activation(
                            out=P[:, hh, 0:ncols], in_=PSs[hh][sbk][:, 0:ncols], func=AF.Exp),
                            reads=["pSs%d_%d" % (hh, sbk)], writes=["P%d_%d" % (eb, hh)])
                    else:
                        m0 = (2 * hp + hh) * 3 + pi
                        S.op("act", lambda e, hh=hh, E=E: e.activation(
                            out=E[:, hh, 0:ncols], in_=PSs[hh][sbk][:, 0:ncols], func=AF.Exp),
                            reads=["pSs%d_%d" % (hh, sbk)], writes=["E%d_%d" % (xb, hh)])
                        S.op("dve", lambda e, hh=hh, E=E, m0=m0: e.tensor_tensor(
                            out=P[:, hh, 0:ncols], in0=E[:, hh, 0:ncols], in1=mm[:, m0, 0:ncols], op=ALU.mult),
                            reads=["E%d_%d" % (xb, hh), "mm"], writes=["P%d_%d" % (eb, hh)])
                return (item, js, vts, eb)

            def stageB(ctx):
                (item, js, vts, eb) = ctx
                (pi, d, nb, gs, r, grp_bank, fresh, j0) = item
                P = Pr[eb]
                plan = []
                wr = []
                for hh in range(2):
                    for bi, j in enumerate(js):
                        parts = [(qb, half) for (qb, half) in ((j, 0), (j + 1, 1)) if qb < nb]
                        if len(parts) == 2 and parts[0][0] // gs == parts[1][0] // gs:
                            parts = [(j, 0, 2)]
                        else:
                            parts = [(qb, half, 1) for (qb, half) in parts]
                        for (qb, half, w2) in parts:
                            g = qb // gs
                            key = (hh, g)
                            if key not in grp_bank:
                                grp_bank[key] = nxt("G%d" % hh, 2)
                                fresh[key] = True
                            bank = grp_bank[key]
                            plan.append((hh, bi, half, bank, (qb % gs) * 128, fresh[key], w2))
                            fresh[key] = False
                            bid = "pO%d_%d" % (hh, bank)
                            if bid not in wr:
                                wr.append(bid)

                def pv(e):
                    rr = None
                    for (hh, bi, half, bank, col, fr, w2) in plan:
                        rr = e.matmul(PO[hh][bank][:, col:col + 128 * w2], lhsT=VAr[vts[bi]][:, hh, :],
                                      rhs=P[:, hh, bi * 256 + half * 128:bi * 256 + half * 128 + 128 * w2],
                                      start=fr, stop=False, skip_group_check=True)
                    return rr
                S.op("pe", pv, reads=["P%d_0" % eb, "P%d_1" % eb] + ["VAt%d" % v for v in vts], writes=wr)
                for j in js:
                    if j % gs != gs - 1:
                        continue
                    g = j // gs
                    c_lo = g * gs * 128 * d + r
                    ncol = gs * 128
                    c_hi = c_lo + (ncol - 1) * d + 1
                    for hh in range(2):
                        bank = grp_bank[(hh, g)]
                        bid = "pO%d_%d" % (hh, bank)
                        acc = ACC[hh]
                        aids = accids(hh, c_lo, c_hi)
                        if pi == 2:
                            S.op("dve", lambda e, hh=hh, bank=bank, acc=acc, c_lo=c_lo, c_hi=c_hi: e.tensor_copy(
                                out=acc[:, c_lo:c_hi:d], in_=PO[hh][bank][:, 0:ncol]),
                                reads=[bid], writes=aids)
                        else:
                            S.op("dve", lambda e, hh=hh, bank=bank, acc=acc, c_lo=c_lo, c_hi=c_hi: e.tensor_tensor(
                                out=acc[:, c_lo:c_hi:d], in0=PO[hh][bank][:, 0:ncol],
                                in1=acc[:, c_lo:c_hi:d], op=ALU.add),
                                reads=[bid] + aids, writes=aids)
                    if pi == 0 and (c_hi % RC == 0):
                        for hh in range(2):
                            normalise(hh, c_hi - RC)

            for pi in (2, 1, 0):
                (w, d) = PATTERNS[pi]
                L = T // d
                nb = L // 128
                gs = min(4, nb)
                for r in range(d):
                    grp_bank = {}
                    fresh = {}

                    for j0 in range(0, nb, 2):
                        items.append((pi, d, nb, gs, r, grp_bank, fresh, j0))
            ctxs = {}
            LA = 3
            for k in range(len(items) + LA):
                if k < len(items):
                    ctxs[k] = stageA(items[k])
                if k - LA >= 0:
                    stageB(ctxs.pop(k - LA))
            S.dma("OTst", lambda e: [e.dma_start(out=dr["MIXT"][512 + hp * 128:512 + (hp + 1) * 128, :], in_=OTst[:])],
                  reads=["OTst_%d_%d" % (hh, c0) for hh in range(2) for c0 in range(0, T, RC)], store=True)
        for hp in range(4):
            pair(hp)
        S.dram_barrier()
        S.emit()
    nc.all_engine_barrier()


def consts(nc, S, st, dr):
    sb, ps = _mk(nc, st)
    C = Ctx()
    C.eps = sb("eps", [128, 1], F32)
    C.one = sb("one", [128, 1], F32)
    C.ident = sb("ident", [128, 128], BF16)
    C.ident32 = sb("ident32", [128, 128], F32)
    C.blk = sb("blk", [128, 128], BF16)
    C.ones = sb("ones", [128, 128], BF16)
    S.op("pool", lambda e: e.memset(C.eps[:], EPS), writes=["eps"])
    S.op("pool", lambda e: e.memset(C.one[:], 1.0), writes=["one"])
    S.op("pool", lambda e: e.memset(C.ident[:], 0.0), writes=["ident"])
    S.op("pool", lambda e: e.affine_select(out=C.ident[:], in_=C.ident[:], pattern=[[-1, 128]],
                                           compare_op=ALU.not_equal, fill=1.0, base=0, channel_multiplier=1),
         reads=["ident"], writes=["ident"])
    S.op("pool", lambda e: e.memset(C.ident32[:], 0.0), writes=["ident32"])
    S.op("pool", lambda e: e.affine_select(out=C.ident32[:], in_=C.ident32[:], pattern=[[-1, 128]],
                                           compare_op=ALU.not_equal, fill=1.0, base=0, channel_multiplier=1),
         reads=["ident32"], writes=["ident32"])
    S.op("pool", lambda e: e.memset(C.blk[:], 0.0), writes=["blk"])
    S.op("pool", lambda e: e.memset(C.blk[0:64, 0:64], 1.0), reads=["blk"], writes=["blk"])
    S.op("pool", lambda e: e.memset(C.blk[64:128, 64:128], 1.0), reads=["blk"], writes=["blk"])
    S.op("pool", lambda e: e.memset(C.ones[:], 1.0), writes=["ones"])
    return C


def build(T, stages=("p1",), debug=False):
    nc = bass.Bass("TRN2", target_bir_lowering=False)
    dr = {}

    def inp(name, shape):
        dr[name] = nc.dram_tensor(name, shape, F32, kind="ExternalInput").ap()
    inp("x", [T, 1024])
    inp("norm1_g", [1024])
    inp("w_in", [1024, 2560])
    inp("conv_w", [31, 512])
    inp("conv_b", [512])
    inp("cn_g", [512])
    inp("cn_b", [512])
    inp("q_norm_g", [64])
    inp("k_norm_g", [64])
    inp("w_out", [1024, 1024])
    inp("norm2_g", [1024])
    inp("w_up", [1024, 5632])
    inp("ffconv_w", [3, 5632])
    inp("ffconv_b", [5632])
    inp("w_down", [2816, 1024])
    inp("amask", [128, 24 * 512])
    inp("mmask", [128, 24 * 512])
    dr["out"] = nc.dram_tensor("out", [T, 1024], F32, kind="ExternalOutput").ap()
    kind = "ExternalOutput" if debug else "Internal"

    def scr(name, shape, dt):
        dr[name] = nc.dram_tensor(name, shape, dt, kind=kind).ap()
    scr("UT", [512, 32 + T], BF16)
    scr("QT", [512, T], BF16)
    scr("KT", [512, T], BF16)
    scr("VA", [T, 8, 128], BF16)
    scr("MIXT", [1024, T], BF16)
    scr("X1", [T, 1024], F32)
    scr("X1T", [1024, T], BF16)
    with contextlib.ExitStack() as st:
        S = Sched(nc, st)
        C = consts(nc, S, st, dr)
        if "p1" in stages:
            pass1(nc, S, T, dr, C)
        if "p2a" in stages:
            pass2a(nc, S, T, dr, C)
        if "p2b" in stages:
            pass2b(nc, S, T, dr, C)
        if "p2c" in stages:
            pass2c(nc, S, T, dr, C)
        if "p3" in stages:
            pass3(nc, S, T, dr, C, 0)
            pass3(nc, S, T, dr, C, 1)
    return nc


ALL_STAGES = ("p1", "p2a", "p2b", "p2c", "p3")


def kernel(**inputs):
    T = 8192
    nc = build(T, stages=ALL_STAGES)
    in_maps = [host_inputs(inputs, b, T) for b in range(8)]
    res = run_bass_kernel_spmd(nc, in_maps, core_ids=list(range(8)))
    out = np.stack([np.asarray(r["out"]).reshape(T, 1024) for r in res.results], 0)
    return out.astype(np.float32)


def attn_mask_table(additive=True):
    k = np.arange(128)[:, None]
    q = np.arange(256)[None, :]
    delta = q - k
    valid = (delta >= 0) & (delta <= 128)
    out = np.zeros((128, 24, 2, 256), np.float32)
    for h in range(8):
        slope = 2.0 ** (-(h + 1))
        for p, (w, d) in enumerate(PATTERNS):
            bias = -slope * d * np.maximum(delta, 0).astype(np.float64)
            m = np.where(valid, bias, -30000.0) if additive else np.where(valid, np.exp(bias), 0.0)
            out[:, h * 3 + p, 0] = m
            out[:, h * 3 + p, 1] = m
    return out.reshape(128, 24 * 512)


def host_inputs(inputs, b, T):
    m = {}
    for k, v in inputs.items():
        v = np.asarray(v)
        if k == "x":
            m[k] = np.ascontiguousarray(v[b, :T])
        else:
            m[k] = np.ascontiguousarray(v)
    m["amask"] = attn_mask_table(True)
    m["mmask"] = attn_mask_table(False)
    return m
```
